# Optimizing a Trainium2 kernel written in Bass

```python
import jax, jax.numpy as jnp
from jax import lax
import numpy as np

D_MODEL = 1024
BATCH = 8
SEQ = 4096
DEPTH = 4

EPS = 1e-6
D_FF = 2816
HEAD_DIM = 64
CONV_DIM = D_MODEL // 2
CONV_WIDTH = 3
NSA_HEADS = (D_MODEL // 2) // HEAD_DIM
NSA_KV_HEADS = NSA_HEADS // 4
NSA_GROUP = NSA_HEADS // NSA_KV_HEADS
NSA_Q_DIM = NSA_HEADS * HEAD_DIM
NSA_KV_DIM = NSA_KV_HEADS * HEAD_DIM
CMP_BLOCK = 32
CMP_STRIDE = 16
CMP_HIDDEN = 128
SEL_BLOCK = 64
SEL_TOPK = 16
WINDOW = 512
NSA_Q_BLOCK = 64
FORCE_BONUS = 1e4
SB_HEADS = D_MODEL // HEAD_DIM
SB_DIM = SB_HEADS * HEAD_DIM
SB_Q_BLOCK = 128
AB_SPLITS = [CONV_DIM] * 3 + [NSA_Q_DIM] + [NSA_KV_DIM] * 6 + [3 * NSA_HEADS]
AB_IN_DIM = sum(AB_SPLITS)
AB_OUT_DIM = CONV_DIM + NSA_Q_DIM
NEG = -1e30

kernel_name = 'hybrid_conv_nsa_stickbreak_macaron'


def rms_norm(x, g):
    xf = x.astype(jnp.float32)
    y = xf * lax.rsqrt(jnp.mean(xf * xf, axis=-1, keepdims=True) + EPS)
    return (y * g.astype(jnp.float32)).astype(x.dtype)


def swiglu(x, w_in, w_out):
    gate, up = jnp.split(x @ w_in, 2, axis=-1)
    return (jax.nn.silu(gate) * up) @ w_out


def masked_softmax(s, mask):
    p = jax.nn.softmax(jnp.where(mask, s, NEG), axis=-1)
    return jnp.where(mask, p, 0.0)


def short_conv(b_gate, c_gate, h, conv_w):
    u = c_gate * h
    y = lax.conv_general_dilated(
        u, conv_w[:, None, :].astype(u.dtype), window_strides=(1,),
        padding=[(CONV_WIDTH - 1, 0)], dimension_numbers=('NWC', 'WIO', 'NWC'),
        feature_group_count=u.shape[-1])
    return b_gate * y


def compress_blocks(kv, pe, w1, w2):
    B, T, G, d = kv.shape
    nc = (T - CMP_BLOCK) // CMP_STRIDE + 1
    idx = jnp.arange(nc)[:, None] * CMP_STRIDE + jnp.arange(CMP_BLOCK)[None, :]
    blocks = kv[:, idx] + pe[:, None, :]
    flat = blocks.transpose(0, 1, 3, 2, 4).reshape(B, nc, G, CMP_BLOCK * d)
    return jax.nn.gelu(flat @ w1) @ w2


def cmp_to_sel_overlap(nc, ns):
    c0 = jnp.arange(nc) * CMP_STRIDE
    s0 = jnp.arange(ns) * SEL_BLOCK
    lo = jnp.maximum(c0[:, None], s0[None, :])
    hi = jnp.minimum(c0[:, None] + CMP_BLOCK, s0[None, :] + SEL_BLOCK)
    return jnp.maximum(hi - lo, 0).astype(jnp.float32) / CMP_BLOCK


def nsa_attention(q, kc, vc, ks, vs, kw, vw, gates):
    B, T, G, R, d = q.shape
    scale = d ** -0.5
    nc = kc.shape[1]
    ns = T // SEL_BLOCK
    topk = min(SEL_TOPK, ns)
    cmp_end = jnp.arange(nc) * CMP_STRIDE + CMP_BLOCK - 1
    overlap = cmp_to_sel_overlap(nc, ns)
    ks_blk = ks.reshape(B, ns, SEL_BLOCK, G, d).transpose(0, 3, 1, 2, 4)
    vs_blk = vs.reshape(B, ns, SEL_BLOCK, G, d).transpose(0, 3, 1, 2, 4)
    pad = ((0, 0), (WINDOW, 0), (0, 0), (0, 0))
    kw_pad = jnp.pad(kw, pad)
    vw_pad = jnp.pad(vw, pad)
    b_idx = jnp.arange(B)[:, None, None, None]
    g_idx = jnp.arange(G)[None, :, None, None]
    blk = jnp.arange(ns)
    n_sel = topk * SEL_BLOCK

    def block(n):
        start = n * NSA_Q_BLOCK
        qb = lax.dynamic_slice_in_dim(q, start, NSA_Q_BLOCK, axis=1)
        gb = lax.dynamic_slice_in_dim(gates, start, NSA_Q_BLOCK, axis=1)
        pos = start + jnp.arange(NSA_Q_BLOCK)
        s = jnp.einsum('bqgrd,bcgd->bgrqc', qb, kc, preferred_element_type=jnp.float32) * scale
        p_cmp = masked_softmax(s, cmp_end[None, :] <= pos[:, None])
        o_cmp = jnp.einsum('bgrqc,bcgd->bqgrd', p_cmp.astype(vc.dtype), vc)
        imp = jnp.einsum('bgrqc,cs->bgqs', p_cmp, overlap)
        cur = pos[:, None] // SEL_BLOCK
        forced = (blk[None, :] == 0) | (blk[None, :] == cur) | (blk[None, :] == cur - 1)
        valid = blk[None, :] * SEL_BLOCK <= pos[:, None]
        score = jnp.where(valid, imp + jnp.where(forced, FORCE_BONUS, 0.0), NEG)
        _, sel = lax.top_k(score, topk)
        k_sel = ks_blk[b_idx, g_idx, sel].reshape(B, G, NSA_Q_BLOCK, n_sel, d)
        v_sel = vs_blk[b_idx, g_idx, sel].reshape(B, G, NSA_Q_BLOCK, n_sel, d)
        key_pos = (sel[..., None] * SEL_BLOCK + jnp.arange(SEL_BLOCK)).reshape(B, G, NSA_Q_BLOCK, n_sel)
        s = jnp.einsum('bqgrd,bgqnd->bgrqn', qb, k_sel, preferred_element_type=jnp.float32) * scale
        p = masked_softmax(s, (key_pos <= pos[None, None, :, None])[:, :, None])
        o_sel = jnp.einsum('bgrqn,bgqnd->bqgrd', p.astype(v_sel.dtype), v_sel)
        kwb = lax.dynamic_slice_in_dim(kw_pad, start, WINDOW + NSA_Q_BLOCK, axis=1)
        vwb = lax.dynamic_slice_in_dim(vw_pad, start, WINDOW + NSA_Q_BLOCK, axis=1)
        kp = start - WINDOW + jnp.arange(WINDOW + NSA_Q_BLOCK)
        wmask = (kp[None, :] <= pos[:, None]) & (kp[None, :] > pos[:, None] - WINDOW) & (kp[None, :] >= 0)
        s = jnp.einsum('bqgrd,bkgd->bgrqk', qb, kwb, preferred_element_type=jnp.float32) * scale
        p = masked_softmax(s, wmask)
        o_win = jnp.einsum('bgrqk,bkgd->bqgrd', p.astype(vwb.dtype), vwb)
        return gb[..., 0:1] * o_cmp + gb[..., 1:2] * o_sel + gb[..., 2:3] * o_win

    out = lax.map(block, jnp.arange(T // NSA_Q_BLOCK))
    return jnp.moveaxis(out, 0, 1).reshape(B, T, G * R * d)


def conv_nsa_mixer(h, w_in, conv_w, pe_k, w1_k, w2_k, pe_v, w1_v, w2_v, w_out):
    B, T, _ = h.shape
    offs = np.cumsum(AB_SPLITS)[:-1].tolist()
    (b_gate, c_gate, hc, q, k_cmp, v_cmp, k_sel, v_sel, k_win, v_win,
     g) = jnp.split(h @ w_in, offs, axis=-1)
    y_conv = short_conv(b_gate, c_gate, hc, conv_w)
    kvs = lambda t: t.reshape(B, T, NSA_KV_HEADS, HEAD_DIM)
    kc = compress_blocks(kvs(k_cmp), pe_k, w1_k, w2_k)
    vc = compress_blocks(kvs(v_cmp), pe_v, w1_v, w2_v)
    qh = q.reshape(B, T, NSA_KV_HEADS, NSA_GROUP, HEAD_DIM)
    gates = jax.nn.sigmoid(g).reshape(B, T, NSA_KV_HEADS, NSA_GROUP, 3)
    y_nsa = nsa_attention(qh, kc, vc, kvs(k_sel), kvs(v_sel), kvs(k_win), kvs(v_win), gates)
    return jnp.concatenate([y_conv, y_nsa], axis=-1) @ w_out


def stick_breaking_attention(q, k, v):
    B, T, H, d = q.shape
    scale = d ** -0.5
    kpos = jnp.arange(T)

    def block(n):
        start = n * SB_Q_BLOCK
        qb = lax.dynamic_slice_in_dim(q, start, SB_Q_BLOCK, axis=1)
        qpos = start + jnp.arange(SB_Q_BLOCK)
        z = jnp.einsum('bqhd,bkhd->bhqk', qb, k, preferred_element_type=jnp.float32) * scale
        mask = kpos[None, :] < qpos[:, None]
        log_stay = jnp.where(mask, jax.nn.log_sigmoid(-z), 0.0)
        between = lax.cumsum(log_stay, axis=3, reverse=True) - log_stay
        a = jnp.where(mask, jnp.exp(jax.nn.log_sigmoid(z) + between), 0.0)
        return jnp.einsum('bhqk,bkhd->bqhd', a.astype(v.dtype), v)

    out = lax.map(block, jnp.arange(T // SB_Q_BLOCK))
    return jnp.moveaxis(out, 0, 1).reshape(B, T, H * d)


def stick_breaking_mixer(h, w_qkv, w_out):
    B, T, _ = h.shape
    q, k, v = jnp.split(h @ w_qkv, 3, axis=-1)
    hd = lambda t: t.reshape(B, T, SB_HEADS, HEAD_DIM)
    return stick_breaking_attention(hd(q), hd(k), hd(v)) @ w_out


def setup_inputs(seed: int = 0) -> dict:
    key = jax.random.key(seed)
    k = jax.random.split(key, 20)
    n_even = (DEPTH + 1) // 2
    n_odd = DEPTH // 2
    nrm = lambda i, shape, s: jax.random.normal(k[i], shape, jnp.float32) * s
    gain = lambda i, shape: 1.0 + 0.02 * jax.random.normal(k[i], shape, jnp.float32)
    L = CMP_BLOCK * HEAD_DIM
    return {
        'x': nrm(0, (BATCH, SEQ, D_MODEL), 1.0),
        'norm_ffn1': gain(1, (DEPTH, D_MODEL)),
        'w_ffn1_in': nrm(2, (DEPTH, D_MODEL, 2 * D_FF), D_MODEL ** -0.5),
        'w_ffn1_out': nrm(3, (DEPTH, D_FF, D_MODEL), D_FF ** -0.5),
        'norm_mix': gain(4, (DEPTH, D_MODEL)),
        'w_in_ab': nrm(5, (n_even, D_MODEL, AB_IN_DIM), D_MODEL ** -0.5),
        'conv_w': nrm(6, (n_even, CONV_WIDTH, CONV_DIM), CONV_WIDTH ** -0.5),
        'cmp_pe_k': nrm(7, (n_even, CMP_BLOCK, HEAD_DIM), 0.1),
        'cmp_w1_k': nrm(8, (n_even, L, CMP_HIDDEN), L ** -0.5),
        'cmp_w2_k': nrm(9, (n_even, CMP_HIDDEN, HEAD_DIM), CMP_HIDDEN ** -0.5),
        'cmp_pe_v': nrm(10, (n_even, CMP_BLOCK, HEAD_DIM), 0.1),
        'cmp_w1_v': nrm(11, (n_even, L, CMP_HIDDEN), L ** -0.5),
        'cmp_w2_v': nrm(12, (n_even, CMP_HIDDEN, HEAD_DIM), CMP_HIDDEN ** -0.5),
        'w_out_ab': nrm(13, (n_even, AB_OUT_DIM, D_MODEL), AB_OUT_DIM ** -0.5),
        'w_qkv_sb': nrm(14, (n_odd, D_MODEL, 3 * SB_DIM), D_MODEL ** -0.5),
        'w_out_sb': nrm(15, (n_odd, SB_DIM, D_MODEL), SB_DIM ** -0.5),
        'norm_ffn2': gain(16, (DEPTH, D_MODEL)),
        'w_ffn2_in': nrm(17, (DEPTH, D_MODEL, 2 * D_FF), D_MODEL ** -0.5),
        'w_ffn2_out': nrm(18, (DEPTH, D_FF, D_MODEL), D_FF ** -0.5),
        'norm_final': gain(19, (D_MODEL,)),
    }


def reference(x, norm_ffn1, w_ffn1_in, w_ffn1_out, norm_mix, w_in_ab, conv_w,
              cmp_pe_k, cmp_w1_k, cmp_w2_k, cmp_pe_v, cmp_w1_v, cmp_w2_v, w_out_ab,
              w_qkv_sb, w_out_sb, norm_ffn2, w_ffn2_in, w_ffn2_out, norm_final):
    for layer in range(DEPTH):
        x = x + 0.5 * swiglu(rms_norm(x, norm_ffn1[layer]), w_ffn1_in[layer], w_ffn1_out[layer])
        h = rms_norm(x, norm_mix[layer])
        i = layer // 2
        if layer % 2 == 0:
            x = x + conv_nsa_mixer(h, w_in_ab[i], conv_w[i], cmp_pe_k[i], cmp_w1_k[i],
                                   cmp_w2_k[i], cmp_pe_v[i], cmp_w1_v[i], cmp_w2_v[i],
                                   w_out_ab[i])
        else:
            x = x + stick_breaking_mixer(h, w_qkv_sb[i], w_out_sb[i])
        x = x + 0.5 * swiglu(rms_norm(x, norm_ffn2[layer]), w_ffn2_in[layer], w_ffn2_out[layer])
    return rms_norm(x, norm_final)
```

```python
import numpy as np
from contextlib import ExitStack
import concourse.bass as bass
import concourse.mybir as mybir
from concourse.bass_utils import run_bass_kernel_spmd

F32 = mybir.dt.float32
BF16 = mybir.dt.bfloat16
AF = mybir.ActivationFunctionType
ALU = mybir.AluOpType

D = 1024
KC = 8
DFF = 2816
NJ = 22
EPS = 1e-6
NEG = -1e30
BIG = 16384.0
AB_IN = 2840
CH = 512
import os
NSTOP = int(os.environ.get('NSTOP', '99'))
NATT = int(os.environ.get('NATT', '99'))


class Buf:
    __slots__ = ("name", "w", "r", "sem", "sem_sw")

    def __init__(self, name):
        self.name = name
        self.w = None
        self.r = {}
        self.sem = None
        self.sem_sw = None


class DSem:
    __slots__ = ("h", "cnt", "name")

    def __init__(self, h, name):
        self.h = h
        self.cnt = 0
        self.name = name


class Sched:
    def __init__(self, nc, st, n_dma_sems=88):
        self.nc = nc
        self.eng = {"pe": nc.tensor, "act": nc.scalar, "dve": nc.vector, "pool": nc.gpsimd, "sp": nc.sync}
        self.sem = {}
        self.cnt = {}
        for e in ("pe", "act", "dve", "pool"):
            self.sem[e] = st.enter_context(nc.semaphore("s_" + e))
            self.cnt[e] = 0
        self.known = {e: {} for e in self.eng}
        alld = [DSem(st.enter_context(nc.semaphore("d%d" % i)), "d%d" % i) for i in range(n_dma_sems)]
        self.free_dsems = alld[:n_dma_sems - 24]
        self.free_dsems_sw = alld[n_dma_sems - 24:]
        self.used_dsems = []
        self.used_dsems_sw = []
        self.bufs_with_sem = []

    def _waits(self, e, evs):
        need = {}
        for (key, h, val) in evs:
            if self.known[e].get(key, 0) < val and need.get(key, (None, 0))[1] < val:
                need[key] = (h, val)
        for key, (h, val) in need.items():
            self.eng[e].wait_ge(h, val)
            self.known[e][key] = val

    def _collect(self, e, reads, writes, acc):
        ev = []
        for b in reads:
            if b.w is not None:
                ev.append(b.w)
        for b in writes:
            if b.w is not None and not (acc and b.w[0] == e):
                ev.append(b.w)
            for k, v in b.r.items():
                ev.append(v)
        return ev

    def op(self, e, fn, reads=(), writes=(), acc=False):
        self._waits(e, self._collect(e, reads, writes, acc))
        inst = fn()
        self.cnt[e] += 1
        inst.then_inc(self.sem[e], 1)
        me = (e, self.sem[e], self.cnt[e])
        for b in reads:
            b.r[e] = me
        for b in writes:
            b.w = me
            b.r = {}
        return inst

    def dma(self, q, pairs, reads, writes, owner):
        self._waits(q, self._collect(q, reads, writes, False))
        if q == "pool":
            if owner.sem_sw is None:
                owner.sem_sw = self.free_dsems_sw.pop()
                self.used_dsems_sw.append(owner.sem_sw)
                self.bufs_with_sem.append(owner)
            ds = owner.sem_sw
        else:
            if owner.sem is None:
                owner.sem = self.free_dsems.pop()
                self.used_dsems.append(owner.sem)
                self.bufs_with_sem.append(owner)
            ds = owner.sem
        for (o, i) in pairs:
            self.eng[q].dma_start(out=o, in_=i).then_inc(ds.h, 16)
            ds.cnt += 16
        me = (ds.name, ds.h, ds.cnt)
        for b in reads:
            b.r[ds.name] = me
        for b in writes:
            b.w = me
            b.r = {}

    def barrier(self):
        evs = [(e, self.sem[e], self.cnt[e]) for e in self.sem if self.cnt[e] > 0]
        evs += [(d.name, d.h, d.cnt) for d in self.used_dsems + self.used_dsems_sw if d.cnt > 0]
        for e in self.eng:
            self._waits(e, evs)
        for b in self.bufs_with_sem:
            b.sem = None
            b.sem_sw = None
        self.bufs_with_sem = []
        self.free_dsems.extend(self.used_dsems)
        self.free_dsems_sw.extend(self.used_dsems_sw)
        self.used_dsems = []
        self.used_dsems_sw = []


class Prog:
    def __init__(self, T, depth=4, plan=None):
        self.T = T
        self.NCH = T // CH
        self.depth = depth
        self.plan = plan
        nc = self.nc = bass.Bass("TRN2", target_bir_lowering=False)
        self.st = ExitStack()
        self.inp = {}
        self.declare_io()

    def din(self, name, shape, dt=F32):
        t = self.nc.dram_tensor(name, list(shape), dt, kind="ExternalInput").ap()
        self.inp[name] = t
        return t

    def dscr(self, name, shape, dt):
        return self.nc.dram_tensor(name, list(shape), dt).ap()

    def declare_io(self):
        T = self.T
        nd = self.depth
        ne = (nd + 1) // 2
        no = nd // 2
        self.xT = self.din("xT", [D, T])
        self.gains = self.din("gains", [128, 13 * KC])
        self.w_ffn_in = [self.din("w_ffn1_in", [nd, D, 2 * DFF]), self.din("w_ffn2_in", [nd, D, 2 * DFF])]
        self.w_ffn_out = [self.din("w_ffn1_out", [nd, DFF, D]), self.din("w_ffn2_out", [nd, DFF, D])]
        self.w_in_ab = self.din("w_in_ab", [ne, D, AB_IN])
        self.w_out_ab = self.din("w_out_ab", [ne, D, D])
        self.w_qkv_sb = self.din("w_qkv_sb", [max(no, 1), D, 3 * D])
        self.w_out_sb = self.din("w_out_sb", [max(no, 1), D, D])
        self.convw = self.din("convw", [128, ne * 12])
        self.cmp_w1 = [self.din("cmp_w1_k", [ne, 2048, 128]), self.din("cmp_w1_v", [ne, 2048, 128])]
        self.cmp_w2 = [self.din("cmp_w2_k", [ne, 128, 64]), self.din("cmp_w2_v", [ne, 128, 64])]
        self.cmp_peT = [self.din("cmp_peT_k", [ne, 64, 32]), self.din("cmp_peT_v", [ne, 64, 32])]
        self.c_selbias = self.din("c_selbias", [128, T // 128, 64])
        self.c_etab = self.din("c_etab", [64, T])
        self.c_overlap = self.din("c_overlap", [256, 64])
        self.c_tri = self.din("c_tri", [128, 256])
        self.c_ident = self.din("c_ident", [128, 128])
        self.outT = self.nc.dram_tensor("outT", [D, T], F32, kind="ExternalOutput").ap()
        self.xr = self.dscr("xr", [D, T], F32)
        self.mixT = self.dscr("mixT", [D, T], BF16)
        self.sb_q = self.dscr("sb_q", [D, T], BF16)
        self.sb_k = self.dscr("sb_k", [D, T], BF16)
        self.sb_v = self.dscr("sb_v", [T, D], BF16)
        self.n_q = self.dscr("n_q", [8, 64, T], BF16)
        self.n_ksel = self.dscr("n_ksel", [2, 64, T], BF16)
        self.n_kwin = self.dscr("n_kwin", [2, 64, T], BF16)
        NB = T // 128
        self.NCP = T // 16
        self.NCB = T // 16 - 1
        self.CW = min(128, self.NCP)
        self.NBLK = (self.NCP + 127) // 128
        self.n_vsel = self.dscr("n_vsel", [2, 128, NB, 66], BF16)
        self.n_vwin = self.dscr("n_vwin", [2, 128, NB, 66], BF16)
        self.n_gates = self.dscr("n_gates", [128, NB, 24], F32)
        self.n_kc = self.dscr("n_kc", [2, 64, self.NCP], BF16)
        self.n_vc = self.dscr("n_vc", [2, self.CW, self.NBLK, 130], BF16)

    def sb(self, st, name, shape, dt):
        self._uid = getattr(self, "_uid", 0) + 1
        return st.enter_context(self.nc.sbuf_tensor("%s_%d" % (name, self._uid), list(shape), dt))

    def build(self):
        nc = self.nc
        st = self.st
        S = self.S = Sched(nc, st)
        self.psw = [st.enter_context(nc.psum_tensor("psw%d" % i, [128, 1024], F32)) for i in range(3)]
        ps6 = st.enter_context(nc.psum_tensor("ps6", [128, 512], F32))
        self.ps = [self.psw[i // 2][:, (i % 2) * 512:(i % 2 + 1) * 512] for i in range(6)] + [ps6[:]]
        self.psb = [Buf("ps%d" % i) for i in range(7)]
        self.pst = st.enter_context(nc.psum_tensor("pst", [128, 1024], BF16))
        self.pstb = Buf("pst")
        self.gains_sb = self.sb(st, "gains_sb", [128, 13 * KC], F32)
        self.ones_bf = self.sb(st, "ones_bf", [128, 128], BF16)
        self.tri_bf = self.sb(st, "tri_bf", [128, 256], BF16)
        self.ident_bf = self.sb(st, "ident_bf", [128, 128], BF16)
        self.gconst = Buf("gconst")
        S.dma("sp", [(self.gains_sb[:], self.gains[:, :])], [], [self.gconst], self.gconst)
        S.dma("pool", [(self.tri_bf[:], self.c_tri[:, :]), (self.ident_bf[:], self.c_ident[:, :])], [], [self.gconst], self.gconst)
        S.op("dve", lambda: nc.vector.memset(self.ones_bf[:], 1.0), [], [self.gconst])
        S.barrier()

        plan = self.plan
        if plan is None:
            plan = []
            for l in range(self.depth):
                plan.append(("ffn", l, 0))
                plan.append(("mix", l))
                plan.append(("ffn", l, 1) if l < self.depth - 1 else ("ffn", l, 1, True))
        src = self.xT
        for ph in plan:
            if ph[0] == "ffn":
                l, which = ph[1], ph[2]
                gi = (l if which == 0 else 8 + l)
                self.phase_ffn(src, self.w_ffn_in[which][l], self.w_ffn_out[which][l], gi, final=(len(ph) > 3 and ph[3]))
                src = self.xr
            elif ph[0] == "mix":
                l = ph[1]
                if l % 2 == 1:
                    self.phase_sb_qkv(src, l)
                    self.phase_sb_att(l)
                    self.phase_mix_out(src, self.w_out_sb[l // 2])
                else:
                    self.phase_nsa_proj(src, l)
                    self.phase_nsa_att(l)
                    self.phase_mix_out(src, self.w_out_ab[l // 2])
                src = self.xr
            elif ph[0] == "sb_qkv":
                self.phase_sb_qkv(src, ph[1])
            elif ph[0] == "sb_att":
                self.phase_sb_att(ph[1])
            elif ph[0] == "mix_out":
                self.phase_mix_out(src, self.w_out_sb[0])
                src = self.xr
            elif ph[0] == "nsa_proj":
                self.phase_nsa_proj(src, ph[1])
            elif ph[0] == "nsa_att":
                self.phase_nsa_att(ph[1])
            elif ph[0] == "final":
                self.phase_final(src)
        S.barrier()
        st.close()
        return nc

    def xview(self, ap, c, w=CH):
        return ap.rearrange("(k p) t -> p k t", p=128)[:, :, c * w:(c + 1) * w]

    def load_weight(self, dst_tile, dst_buf, src_ap, nk, split=1):
        F = src_ap.shape[1]
        v = src_ap.rearrange("(k p) f -> p k f", p=128)
        pairs = []
        fs = F // split
        for k in range(nk):
            for s in range(split):
                f0 = s * fs
                f1 = F if s == split - 1 else (s + 1) * fs
                pairs.append((dst_tile[:, k, f0:f1], v[:, k, f0:f1]))
        self.S.dma("pool", pairs, [], [dst_buf], dst_buf)

    def load_weight_groups(self, dst_tile, src_ap, nk, groups):
        v = src_ap.rearrange("(k p) f -> p k f", p=128)
        bufs = []
        for gi_, ranges in enumerate(groups):
            b = Buf("wg%d" % gi_)
            pairs = [(dst_tile[:, k, c0:c1], v[:, k, c0:c1]) for k in range(nk) for (c0, c1) in ranges]
            self.S.dma("pool", pairs, [], [b], b)
            bufs.append(b)
        return bufs

    def emit_norm(self, xt, xb, hT, hb, gi, sq, sqb, rstd, rstdb, ps_ss, ps_ssb, w=CH):
        nc, S = self.nc, self.S
        nsq = len(sqb)
        for k in range(KC):
            S.op("pool", lambda k=k: nc.gpsimd.tensor_tensor(out=sq[:, k % nsq, :w], in0=xt[:, k, :w], in1=xt[:, k, :w], op=ALU.mult),
                 [xb[k]], [sqb[k % nsq]])
            S.op("pe", lambda k=k: nc.tensor.matmul(ps_ss[:, :w], self.ones_bf[:], sq[:, k % nsq, :w], start=(k == 0), stop=(k == KC - 1)),
                 [sqb[k % nsq], self.gconst], [ps_ssb], acc=True)
        S.op("act", lambda: nc.scalar.activation(out=rstd[:, :w], in_=ps_ss[:, :w], func=AF.Ln, scale=1.0 / D, bias=EPS), [ps_ssb], [rstdb])
        S.op("act", lambda: nc.scalar.activation(out=rstd[:, :w], in_=rstd[:, :w], func=AF.Exp, scale=-0.5), [rstdb], [rstdb])
        for k in range(KC):
            S.op("dve", lambda k=k: nc.vector.scalar_tensor_tensor(out=hT[:, k, :w], in0=xt[:, k, :w], scalar=self.gains_sb[:, gi * KC + k:gi * KC + k + 1],
                                                                     in1=rstd[:, :w], op0=ALU.mult, op1=ALU.mult),
                 [xb[k], rstdb, self.gconst], [hb[k]])

    def phase_ffn(self, src, w_in, w_out, gi, final=False):
        nc, S = self.nc, self.S
        with ExitStack() as st:
            win = self.sb(st, "win", [128, KC, 2 * DFF], BF16)
            wout = self.sb(st, "wout", [128, NJ, D], BF16)
            winb, woutb = Buf("win"), Buf("wout")
            xs = [self.sb(st, "x%d" % i, [128, KC, CH], F32) for i in range(2)]
            xsb = [[Buf("x%d_%d" % (i, k)) for k in range(KC)] for i in range(2)]
            hT = self.sb(st, "hT", [128, KC, CH], BF16)
            hb = [Buf("h%d" % k) for k in range(KC)]
            sq = self.sb(st, "sq", [128, 2, CH], BF16)
            sqb = [Buf("sq%d" % k) for k in range(2)]
            act = self.sb(st, "act", [128, NJ, CH], BF16)
            actb = [Buf("act%d" % j) for j in range(NJ)]
            sg = [self.sb(st, "sg%d" % i, [128, CH], F32) for i in range(2)]
            sgb = [Buf("sg%d" % i) for i in range(2)]
            rstd = self.sb(st, "rstd", [128, CH], F32)
            rstdb = Buf("rstd")
            jr = [(0, 6), (6, 12), (12, 17), (17, 22)]
            wing = self.load_weight_groups(win, w_in, KC, [[(a * 128, b * 128), (DFF + a * 128, DFF + b * 128)] for (a, b) in jr])
            winj = [wing[[i for i, (a, b) in enumerate(jr) if a <= j < b][0]] for j in range(NJ)]
            woutg = self.load_weight_groups(wout, w_out, NJ, [[(0, 512)], [(512, 1024)]])
            ps, psb = self.ps, self.psb

            def load_x(c):
                s = c % 2
                S.dma("sp", [(xs[s][:], self.xview(src, c))], [], xsb[s], xsb[s][0])

            load_x(0)
            self.emit_norm(xs[0], xsb[0], hT, hb, gi, sq, sqb, rstd, rstdb, ps[6], psb[6])
            for c in range(self.NCH):
                s = c % 2
                xt, xb = xs[s], xsb[s]
                if c + 1 < self.NCH:
                    load_x(c + 1)
                for j in range(NJ):
                    pg, pu = j % 2, 2 + j % 2
                    for k in range(KC):
                        S.op("pe", lambda k=k, j=j, pg=pg: nc.tensor.matmul(ps[pg][:], win[:, k, j * 128:(j + 1) * 128], hT[:, k, :], start=(k == 0), stop=(k == KC - 1)),
                             [winj[j], hb[k]], [psb[pg]], acc=True)
                    for k in range(KC):
                        S.op("pe", lambda k=k, j=j, pu=pu: nc.tensor.matmul(ps[pu][:], win[:, k, DFF + j * 128:DFF + (j + 1) * 128], hT[:, k, :], start=(k == 0), stop=(k == KC - 1)),
                             [winj[j], hb[k]], [psb[pu]], acc=True)
                    S.op("act", lambda j=j, pg=pg: nc.scalar.activation(out=sg[j % 2][:], in_=ps[pg][:], func=AF.Silu),
                         [psb[pg]], [sgb[j % 2]])
                    S.op("dve", lambda j=j, pu=pu: nc.vector.tensor_tensor(out=act[:, j, :], in0=sg[j % 2][:], in1=ps[pu][:], op=ALU.mult),
                         [sgb[j % 2], psb[pu]], [actb[j]])
                if c + 1 < self.NCH:
                    self.emit_norm(xs[(c + 1) % 2], xsb[(c + 1) % 2], hT, hb, gi, sq, sqb, rstd, rstdb, ps[6], psb[6])
                for m in range(KC):
                    po = 4 + m % 2
                    for j in range(NJ):
                        S.op("pe", lambda m=m, j=j, po=po: nc.tensor.matmul(ps[po][:], wout[:, j, m * 128:(m + 1) * 128], act[:, j, :], start=(j == 0), stop=(j == NJ - 1)),
                             [woutg[m // 4], actb[j]], [psb[po]], acc=True)
                    S.op("dve", lambda m=m, po=po: nc.vector.scalar_tensor_tensor(out=xt[:, m, :], in0=ps[po][:], scalar=0.5, in1=xt[:, m, :], op0=ALU.mult, op1=ALU.add),
                         [psb[po], xb[m]], [xb[m]])
                if final:
                    g12 = 12
                    for k in range(KC):
                        S.op("pool", lambda k=k: nc.gpsimd.tensor_tensor(out=sq[:, k % 2, :], in0=xt[:, k, :], in1=xt[:, k, :], op=ALU.mult), [xb[k]], [sqb[k % 2]])
                        S.op("pe", lambda k=k: nc.tensor.matmul(ps[6][:], self.ones_bf[:], sq[:, k % 2, :], start=(k == 0), stop=(k == KC - 1)),
                             [sqb[k % 2], self.gconst], [psb[6]], acc=True)
                    S.op("act", lambda: nc.scalar.activation(out=rstd[:], in_=ps[6][:], func=AF.Ln, scale=1.0 / D, bias=EPS), [psb[6]], [rstdb])
                    S.op("act", lambda: nc.scalar.activation(out=rstd[:], in_=rstd[:], func=AF.Exp, scale=-0.5), [rstdb], [rstdb])
                    for k in range(KC):
                        S.op("dve", lambda k=k: nc.vector.scalar_tensor_tensor(out=xt[:, k, :], in0=xt[:, k, :], scalar=self.gains_sb[:, g12 * KC + k:g12 * KC + k + 1],
                                                                                 in1=rstd[:], op0=ALU.mult, op1=ALU.mult), [xb[k], rstdb, self.gconst], [xb[k]])
                    S.dma("sp", [(self.xview(self.outT, c), xt[:])], xb, [], xb[0])
                else:
                    S.dma("sp", [(self.xview(self.xr, c), xt[:])], xb, [], xb[0])
            S.barrier()

    def phase_final(self, src):
        nc, S = self.nc, self.S
        gi = 12
        with ExitStack() as st:
            xs = [self.sb(st, "x%d" % i, [128, KC, CH], F32) for i in range(2)]
            xsb = [[Buf("x%d_%d" % (i, k)) for k in range(KC)] for i in range(2)]
            sq = self.sb(st, "sq", [128, 2, CH], BF16)
            sqb = [Buf("sq%d" % k) for k in range(2)]
            rstd = self.sb(st, "rstd", [128, CH], F32)
            rstdb = Buf("rstd")
            ps, psb = self.ps, self.psb
            for c in range(self.NCH):
                s = c % 2
                xt, xb = xs[s], xsb[s]
                S.dma("sp", [(xt[:], self.xview(src, c))], [], xb, xb[0])
                for k in range(KC):
                    S.op("pool", lambda k=k: nc.gpsimd.tensor_tensor(out=sq[:, k % 2, :], in0=xt[:, k, :], in1=xt[:, k, :], op=ALU.mult), [xb[k]], [sqb[k % 2]])
                    S.op("pe", lambda k=k: nc.tensor.matmul(ps[6][:], self.ones_bf[:], sq[:, k % 2, :], start=(k == 0), stop=(k == KC - 1)),
                         [sqb[k % 2], self.gconst], [psb[6]], acc=True)
                S.op("act", lambda: nc.scalar.activation(out=rstd[:], in_=ps[6][:], func=AF.Ln, scale=1.0 / D, bias=EPS), [psb[6]], [rstdb])
                S.op("act", lambda: nc.scalar.activation(out=rstd[:], in_=rstd[:], func=AF.Exp, scale=-0.5), [rstdb], [rstdb])
                for k in range(KC):
                    S.op("dve", lambda k=k: nc.vector.scalar_tensor_tensor(out=xt[:, k, :], in0=xt[:, k, :], scalar=self.gains_sb[:, gi * KC + k:gi * KC + k + 1],
                                                                             in1=rstd[:], op0=ALU.mult, op1=ALU.mult), [xb[k], rstdb, self.gconst], [xb[k]])
                S.dma("sp", [(self.xview(self.outT, c), xt[:])], xb, [], xb[0])
            S.barrier()

    def phase_mix_out(self, src, w_out):
        nc, S = self.nc, self.S
        with ExitStack() as st:
            wo = self.sb(st, "wo", [128, KC, D], BF16)
            wob = Buf("wo")
            xs = [self.sb(st, "x%d" % i, [128, KC, CH], F32) for i in range(2)]
            xsb = [[Buf("x%d_%d" % (i, k)) for k in range(KC)] for i in range(2)]
            ms = [self.sb(st, "m%d" % i, [128, KC, CH], BF16) for i in range(2)]
            msb = [Buf("m%d" % i) for i in range(2)]
            wog = self.load_weight_groups(wo, w_out, KC, [[(0, 256)], [(256, 1024)]])
            ps, psb = self.ps, self.psb
            def ld(c):
                s_ = c % 2
                S.dma("sp", [(xs[s_][:], self.xview(src, c))], [], xsb[s_], xsb[s_][0])
                S.dma("sp", [(ms[s_][:], self.xview(self.mixT, c))], [], [msb[s_]], msb[s_])

            ld(0)
            for c in range(self.NCH):
                s = c % 2
                xt, xb = xs[s], xsb[s]
                if c + 1 < self.NCH:
                    ld(c + 1)
                for m in range(KC):
                    po = m % 4
                    for k in range(KC):
                        S.op("pe", lambda m=m, k=k, po=po: nc.tensor.matmul(ps[po][:], wo[:, k, m * 128:(m + 1) * 128], ms[s][:, k, :], start=(k == 0), stop=(k == KC - 1)),
                             [wog[0 if m < 2 else 1], msb[s]], [psb[po]], acc=True)
                    S.op("dve", lambda m=m, po=po: nc.vector.tensor_tensor(out=xt[:, m, :], in0=ps[po][:], in1=xt[:, m, :], op=ALU.add),
                         [psb[po], xb[m]], [xb[m]])
                S.dma("sp", [(self.xview(self.xr, c), xt[:])], xb, [], xb[0])
            S.barrier()

    def phase_sb_qkv(self, src, l):
        nc, S = self.nc, self.S
        gi = 4 + l
        T = self.T
        with ExitStack() as st:
            wq = self.sb(st, "wq", [128, KC, 3 * D], BF16)
            wqb = Buf("wq")
            wqg = self.load_weight_groups(wq, self.w_qkv_sb[l // 2], KC, [[(0, 512)], [(512, D)], [(D, 2 * D)], [(2 * D, 3 * D)]])
            xs = [self.sb(st, "x%d" % i, [128, KC, CH], F32) for i in range(2)]
            xsb = [[Buf("x%d_%d" % (i, k)) for k in range(KC)] for i in range(2)]
            hT = self.sb(st, "hT", [128, KC, CH], BF16)
            hb = [Buf("h%d" % k) for k in range(KC)]
            sq = self.sb(st, "sq", [128, 2, CH], BF16)
            sqb = [Buf("sq%d" % k) for k in range(2)]
            rstd = self.sb(st, "rstd", [128, CH], F32)
            rstdb = Buf("rstd")
            qo = [self.sb(st, "qo%d" % i, [128, KC, CH], BF16) for i in range(2)]
            qob = [Buf("qo%d" % i) for i in range(2)]
            vo = [self.sb(st, "vo%d" % i, [128, 4, D], BF16) for i in range(2)]
            vob = [Buf("vo%d" % i) for i in range(2)]
            ps, psb = self.ps, self.psb
            it = 0

            def ld(c):
                S.dma("sp", [(xs[c % 2][:], self.xview(src, c))], [], xsb[c % 2], xsb[c % 2][0])

            ld(0)
            for c in range(self.NCH):
                s = c % 2
                xt, xb = xs[s], xsb[s]
                if c + 1 < self.NCH:
                    ld(c + 1)
                self.emit_norm(xt, xb, hT, hb, gi, sq, sqb, rstd, rstdb, ps[6], psb[6])
                for qk in range(2):
                    stg, stgb = qo[qk], qob[qk]
                    for m in range(KC):
                        po = it % 4
                        it += 1
                        f0 = qk * D + m * 128
                        for k in range(KC):
                            S.op("pe", lambda k=k, f0=f0, po=po: nc.tensor.matmul(ps[po][:], wq[:, k, f0:f0 + 128], hT[:, k, :], start=(k == 0), stop=(k == KC - 1)),
                                 [wqg[0 if f0 < 512 else (1 if f0 < D else 2)], hb[k]], [psb[po]], acc=True)
                        S.op("act", lambda m=m, po=po, stg=stg, qk=qk: nc.scalar.activation(out=stg[:, m, :], in_=ps[po][:], func=AF.Copy, scale=(0.125 if qk == 0 else 1.0)),
                             [psb[po]], [stgb])
                    dst = self.sb_q if qk == 0 else self.sb_k
                    S.dma("sp", [(self.xview(dst, c), stg[:])], [stgb], [], stgb)
                vs, vsb = vo[c % 2], vob[c % 2]
                for sub in range(4):
                    for half in range(2):
                        po = it % 4
                        it += 1
                        f0 = 2 * D + half * 512
                        for k in range(KC):
                            S.op("pe", lambda k=k, f0=f0, po=po, sub=sub: nc.tensor.matmul(ps[po][:], hT[:, k, sub * 128:(sub + 1) * 128], wq[:, k, f0:f0 + 512], start=(k == 0), stop=(k == KC - 1)),
                                 [wqg[3], hb[k]], [psb[po]], acc=True)
                        S.op("dve", lambda po=po, sub=sub, half=half, vs=vs: nc.vector.tensor_copy(out=vs[:, sub, half * 512:(half + 1) * 512], in_=ps[po][:]),
                             [psb[po]], [vsb])
                S.dma("sp", [(self.sb_v[c * CH:(c + 1) * CH, :].rearrange("(s p) f -> p s f", p=128), vs[:])], [vsb], [], vsb)
            S.barrier()

    def phase_sb_att(self, l):
        nc, S = self.nc, self.S
        T = self.T
        NB = T // 128
        with ExitStack() as st:
            kT = [self.sb(st, "kT%d" % i, [128, T], BF16) for i in range(2)]
            qT = [self.sb(st, "qT%d" % i, [128, T], BF16) for i in range(2)]
            vv = [self.sb(st, "vv%d" % i, [128, NB, 128], BF16) for i in range(2)]
            inb = [Buf("in%d" % i) for i in range(2)]
            oT = [self.sb(st, "oT%d" % i, [128, T], BF16) for i in range(2)]
            oTb = [Buf("oT%d" % i) for i in range(2)]
            e32 = [self.sb(st, "e32_%d" % i, [128, 2 * CH], F32) for i in range(2)]
            e32b = [Buf("e32_%d" % i) for i in range(2)]
            Lb = [self.sb(st, "Lb%d" % i, [128, 2 * CH], BF16) for i in range(4)]
            Lbb = [Buf("Lb%d" % i) for i in range(4)]
            Sb = [self.sb(st, "Sb%d" % i, [128, 2 * CH], BF16) for i in range(3)]
            Sbb = [Buf("Sb%d" % i) for i in range(3)]
            Ab = [self.sb(st, "Ab%d" % i, [128, 2 * CH], BF16) for i in range(3)]
            Abb = [Buf("Ab%d" % i) for i in range(3)]
            ps, psb = self.ps, self.psb
            tri = self.tri_bf
            it = 0

            def load_pair(p):
                s = p % 2
                rows = slice(p * 128, (p + 1) * 128)
                S.dma("sp", [(kT[s][:], self.sb_k[rows, :]), (qT[s][:], self.sb_q[rows, :]),
                             (vv[s][:], self.sb_v[:, rows].rearrange("(j p) f -> p j f", p=128))], [], [inb[s]], inb[s])

            tiles = []
            for p in range(8):
                for c in range(self.NCH):
                    jd = 4 * c + 3
                    for j in range(jd, -1, -1):
                        tiles.append(dict(p=p, s=p % 2, c=c, j=j, first=(j == jd), last=(j == 0), diag=(j >= 4 * c),
                                          pair_last=(c == self.NCH - 1 and j == 0)))
            psw = self.psw
            pswb = [Buf("psw%d" % i) for i in range(3)]
            def v3(tile_ap, c0):
                return tile_ap.rearrange("p (h c) -> p h c", h=2)[:, :, c0:CH]

            def hs(hh, c0):
                return slice(hh * CH + c0, (hh + 1) * CH)

            def stage_a(t, tl):
                s, c, j = tl["s"], tl["c"], tl["j"]
                Q0 = c * CH
                c0 = 128 * (j - 4 * c) if tl["diag"] else 0
                w = CH - c0
                wi, ei, li = t % 3, t % 2, t % 4
                ks = slice(j * 128, (j + 1) * 128)
                qs = slice(Q0 + c0, Q0 + CH)
                for hh in range(2):
                    pr = slice(64 * hh, 64 * hh + 64)
                    S.op("pe", lambda: nc.tensor.matmul(psw[wi][:, hs(hh, c0)], kT[s][pr, ks], qT[s][pr, qs], start=True, stop=True),
                         [inb[s]], [pswb[wi]], acc=(hh == 1))
                S.op("act", lambda: nc.scalar.activation(out=v3(e32[ei][:], c0), in_=v3(psw[wi][:], c0), func=AF.Exp), [pswb[wi]], [e32b[ei]])
                S.op("act", lambda: nc.scalar.activation(out=v3(Lb[li][:], c0), in_=v3(e32[ei][:], c0), func=AF.Ln, bias=1.0), [e32b[ei]], [Lbb[li]])
                if tl["diag"]:
                    S.op("pool", lambda: nc.gpsimd.affine_select(out=v3(Lb[li][:], c0), in_=v3(Lb[li][:], c0), pattern=[[0, 2], [1, w]], compare_op=ALU.is_gt,
                                                                 fill=0.0, base=0, channel_multiplier=-1), [Lbb[li]], [Lbb[li]])
                if not tl["last"]:
                    so, sn = t % 3, (t + 1) % 3
                    if tl["first"]:
                        S.op("dve", lambda: nc.vector.tensor_copy(out=v3(Sb[sn][:], c0), in_=v3(Lb[li][:], c0)), [Lbb[li]], [Sbb[sn]])
                    elif tl["diag"]:
                        S.op("dve", lambda: nc.vector.tensor_copy(out=v3(Sb[sn][:], c0)[:, :, 0:128], in_=v3(Lb[li][:], c0)[:, :, 0:128]), [Lbb[li]], [Sbb[sn]])
                        S.op("dve", lambda: nc.vector.tensor_tensor(out=v3(Sb[sn][:], c0 + 128), in0=v3(Sb[so][:], c0 + 128), in1=v3(Lb[li][:], c0 + 128), op=ALU.add),
                             [Lbb[li], Sbb[so], Sbb[sn]], [Sbb[sn]])
                    else:
                        S.op("dve", lambda: nc.vector.tensor_tensor(out=Sb[sn][:], in0=Sb[so][:], in1=Lb[li][:], op=ALU.add), [Lbb[li], Sbb[so]], [Sbb[sn]])

            def stage_b(t, tl):
                c, j = tl["c"], tl["j"]
                c0 = 128 * (j - 4 * c) if tl["diag"] else 0
                w = CH - c0
                wi, li, ai, so = t % 3, t % 4, t % 3, t % 3
                first = tl["first"]
                for hh in range(2):
                    S.op("pe", lambda: nc.tensor.matmul(psw[wi][:, hs(hh, c0)], tri[:, 0:128], Lb[li][:, hs(hh, c0)], start=False, stop=first, skip_group_check=True),
                         [Lbb[li], self.gconst], [pswb[wi]], acc=True)
                    if not first:
                        c1 = c0 + 128 if tl["diag"] else 0
                        S.op("pe", lambda: nc.tensor.matmul(psw[wi][:, hs(hh, c1)], tri[:, 128:256], Sb[so][:, hs(hh, c1)], start=False, stop=True, skip_group_check=True),
                             [Sbb[so], self.gconst], [pswb[wi]], acc=True)
                S.op("act", lambda: nc.scalar.activation(out=v3(Ab[ai][:], c0), in_=v3(psw[wi][:], c0), func=AF.Exp), [pswb[wi]], [Abb[ai]])
                if tl["diag"]:
                    S.op("pool", lambda: nc.gpsimd.affine_select(out=v3(Ab[ai][:], c0), in_=v3(Ab[ai][:], c0), pattern=[[0, 2], [1, w]], compare_op=ALU.is_gt,
                                                                 fill=0.0, base=0, channel_multiplier=-1), [Abb[ai]], [Abb[ai]])

            def stage_c(t, tl):
                s, c, j, p = tl["s"], tl["c"], tl["j"], tl["p"]
                Q0 = c * CH
                c0 = 128 * (j - 4 * c) if tl["diag"] else 0
                cq = slice(c0, CH)
                ai = t % 3
                for hh in range(2):
                    pr = slice(64 * hh, 64 * hh + 64)
                    S.op("pe", lambda: nc.tensor.matmul(ps[6][pr, cq], vv[s][:, j, pr], Ab[ai][:, hs(hh, c0)], start=tl["first"], stop=tl["last"], skip_group_check=True),
                         [inb[s], Abb[ai]], [psb[6]], acc=True)
                if tl["last"]:
                    S.op("dve", lambda: nc.vector.tensor_copy(out=oT[s][:, Q0:Q0 + CH], in_=ps[6][:, :]), [psb[6]], [oTb[s]])
                if tl["pair_last"]:
                    S.dma("sp", [(self.mixT[p * 128:(p + 1) * 128, :], oT[s][:])], [oTb[s]], [], oTb[s])
                    if p + 2 < 8:
                        load_pair(p + 2)

            load_pair(0)
            load_pair(1)
            n = len(tiles)
            for t in range(n + 2):
                if t < n:
                    stage_a(t, tiles[t])
                if 0 <= t - 1 < n:
                    stage_b(t - 1, tiles[t - 1])
                if 0 <= t - 2 < n:
                    stage_c(t - 2, tiles[t - 2])
            S.barrier()

    def phase_nsa_proj(self, src, l):
        nc, S = self.nc, self.S
        T = self.T
        e = l // 2
        gi = 4 + l
        NCB, NCP, CW, NBLK = self.NCB, self.NCP, self.CW, self.NBLK
        ps, psb = self.ps, self.psb
        with ExitStack() as st:
            wab = self.sb(st, "wab", [128, KC, AB_IN], BF16)
            wabb = Buf("wab")
            wabg = self.load_weight_groups(wab, self.w_in_ab[e], KC, [[(512, 1536)], [(0, 512)], [(1536, 2304)], [(2304, AB_IN)]])

            def wab_buf(f0):
                return wabg[0] if 512 <= f0 < 1536 else (wabg[1] if f0 < 512 else (wabg[2] if f0 < 2304 else wabg[3]))
            cw = self.sb(st, "cw", [128, 12], F32)
            cwb = Buf("cw")
            S.dma("sp", [(cw[:], self.convw[:, e * 12:(e + 1) * 12])], [], [cwb], cwb)
            xs = [self.sb(st, "x%d" % i, [128, KC, CH], F32) for i in range(2)]
            xsb = [[Buf("x%d_%d" % (i, k)) for k in range(KC)] for i in range(2)]
            hT = self.sb(st, "hT", [128, KC, CH], BF16)
            hb = [Buf("h%d" % k) for k in range(KC)]
            sq = self.sb(st, "sq", [128, 2, CH], BF16)
            sqb = [Buf("sq%d" % k) for k in range(2)]
            rstd = self.sb(st, "rstd", [128, CH], F32)
            rstdb = Buf("rstd")
            cT = [self.sb(st, "cT%d" % i, [128, T], BF16) for i in range(2)]
            cTb = [Buf("cT%d" % i) for i in range(2)]
            u = self.sb(st, "u", [128, 4, CH + 2], F32)
            ub = [Buf("u%d" % i) for i in range(4)]
            cS = [self.sb(st, "cS%d" % i, [128, CH], F32) for i in range(2)]
            cSb = [Buf("cS%d" % i) for i in range(2)]
            t1 = [self.sb(st, "t1_%d" % i, [128, CH], F32) for i in range(2)]
            t1b = [Buf("t1_%d" % i) for i in range(2)]
            ycv = [self.sb(st, "ycv%d" % i, [128, 4, CH], BF16) for i in range(2)]
            ycvb = [Buf("ycv%d" % i) for i in range(2)]
            qst = [self.sb(st, "qst%d" % i, [64, 8, CH], BF16) for i in range(2)]
            qstb = [Buf("qst%d" % i) for i in range(2)]
            kst = [self.sb(st, "kst%d" % i, [64, 4, CH], BF16) for i in range(2)]
            kstb = [Buf("kst%d" % i) for i in range(2)]
            vsl = [self.sb(st, "vsl%d" % i, [128, 2, 4, 66], BF16) for i in range(2)]
            vwn = [self.sb(st, "vwn%d" % i, [128, 2, 4, 66], BF16) for i in range(2)]
            gst = [self.sb(st, "gst%d" % i, [128, 4, 24], F32) for i in range(2)]
            tkb = [Buf("tk%d" % i) for i in range(2)]
            for i in range(2):
                S.op("pool", lambda i=i: nc.gpsimd.memset(vsl[i][:], 1.0), [], [tkb[i]])
                S.op("pool", lambda i=i: nc.gpsimd.memset(vwn[i][:], 1.0), [], [tkb[i]])
            S.op("pool", lambda: nc.gpsimd.memset(u[:], 0.0), [], ub)
            it = 0

            def proj(f0, M, rows=None):
                nonlocal it
                po = it % 6
                it += 1
                for k in range(KC):
                    S.op("pe", lambda k=k, po=po: nc.tensor.matmul(ps[po][0:M, :], wab[:, k, f0:f0 + M], hT[:, k, :], start=(k == 0), stop=(k == KC - 1)),
                         [wab_buf(f0), hb[k]], [psb[po]], acc=True)
                return po

            def ld(c):
                S.dma("sp", [(xs[c % 2][:], self.xview(src, c))], [], xsb[c % 2], xsb[c % 2][0])

            ld(0)
            for c in range(self.NCH):
                s = c % 2
                cs = slice(c * CH, (c + 1) * CH)
                xt, xb = xs[s], xsb[s]
                if c + 1 < self.NCH:
                    ld(c + 1)
                self.emit_norm(xt, xb, hT, hb, gi, sq, sqb, rstd, rstdb, ps[6], psb[6])
                for i in range(4 if NSTOP >= 2 else 0):
                    pc = proj(512 + 128 * i, 128)
                    S.op("act", lambda pc=pc, i=i: nc.scalar.activation(out=cS[i % 2][:], in_=ps[pc][:], func=AF.Copy), [psb[pc]], [cSb[i % 2]])
                    ph = proj(1024 + 128 * i, 128)
                    S.op("dve", lambda ph=ph, i=i: nc.vector.tensor_tensor(out=u[:, i, 2:CH + 2], in0=cS[i % 2][:], in1=ps[ph][:], op=ALU.mult),
                         [cSb[i % 2], psb[ph]], [ub[i]])
                    pb_ = proj(128 * i, 128)
                    tt, ttb = t1[i % 2], t1b[i % 2]
                    S.op("dve", lambda i=i, tt=tt: nc.vector.tensor_scalar(out=tt[:], in0=u[:, i, 0:CH], scalar1=cw[:, i * 3:i * 3 + 1], scalar2=None, op0=ALU.mult),
                         [ub[i], cwb], [ttb])
                    S.op("dve", lambda i=i, tt=tt: nc.vector.scalar_tensor_tensor(out=tt[:], in0=u[:, i, 1:CH + 1], scalar=cw[:, i * 3 + 1:i * 3 + 2], in1=tt[:], op0=ALU.mult, op1=ALU.add),
                         [ub[i], cwb, ttb], [ttb])
                    S.op("dve", lambda i=i, tt=tt: nc.vector.scalar_tensor_tensor(out=tt[:], in0=u[:, i, 2:CH + 2], scalar=cw[:, i * 3 + 2:i * 3 + 3], in1=tt[:], op0=ALU.mult, op1=ALU.add),
                         [ub[i], cwb, ttb], [ttb])
                    S.op("dve", lambda i=i, tt=tt, pb_=pb_: nc.vector.tensor_tensor(out=ycv[s][:, i, :], in0=tt[:], in1=ps[pb_][:], op=ALU.mult),
                         [ttb, psb[pb_]], [ycvb[s]])
                    S.op("dve", lambda i=i: nc.vector.tensor_copy(out=u[:, i, 0:2], in_=u[:, i, CH:CH + 2]), [ub[i]], [ub[i]])
                if NSTOP >= 2:
                    S.dma("sp", [(self.mixT[0:512, cs].rearrange("(i p) t -> p i t", p=128), ycv[s][:])], [ycvb[s]], [], ycvb[s])
                if NSTOP < 3:
                    continue
                for hq in range(8):
                    pq = proj(1536 + 64 * hq, 64)
                    S.op("act", lambda pq=pq, hq=hq: nc.scalar.activation(out=qst[s][:, hq, :], in_=ps[pq][0:64, :], func=AF.Copy, scale=0.125), [psb[pq]], [qstb[s]])
                S.dma("sp", [(self.n_q.rearrange("h d t -> d h t")[:, :, cs], qst[s][:])], [qstb[s]], [], qstb[s])
                if NSTOP < 4:
                    continue
                for kv in range(2):
                    pk = proj(2048 + 128 * kv, 128)
                    S.op("dve", lambda pk=pk, kv=kv: nc.vector.tensor_copy(out=cT[kv][:, cs], in_=ps[pk][:]), [psb[pk]], [cTb[kv]])
                for idx, f0 in enumerate((2304, 2368, 2560, 2624)):
                    pk = proj(f0, 64)
                    S.op("act", lambda pk=pk, idx=idx: nc.scalar.activation(out=kst[s][:, idx, :], in_=ps[pk][0:64, :], func=AF.Copy), [psb[pk]], [kstb[s]])
                S.dma("sp", [(self.n_ksel.rearrange("g d t -> d g t")[:, :, cs], kst[s][:, 0:2, :]),
                             (self.n_kwin.rearrange("g d t -> d g t")[:, :, cs], kst[s][:, 2:4, :])], [kstb[s]], [], kstb[s])
                if NSTOP < 5:
                    continue
                for sub in range(4):
                    po = it % 6
                    it += 1
                    for k in range(KC):
                        S.op("pe", lambda k=k, po=po, sub=sub: nc.tensor.matmul(ps[po][:, 0:408], hT[:, k, sub * 128:(sub + 1) * 128], wab[:, k, 2432:2840], start=(k == 0), stop=(k == KC - 1)),
                             [wabg[3], hb[k]], [psb[po]], acc=True)
                    S.op("dve", lambda po=po, sub=sub: nc.vector.tensor_copy(out=vsl[s][:, :, sub, 0:64], in_=ps[po][:, 0:128].rearrange("p (g d) -> p g d", g=2)), [psb[po]], [tkb[s]])
                    S.op("dve", lambda po=po, sub=sub: nc.vector.tensor_copy(out=vwn[s][:, :, sub, 0:64], in_=ps[po][:, 256:384].rearrange("p (g d) -> p g d", g=2)), [psb[po]], [tkb[s]])
                    S.op("act", lambda po=po, sub=sub: nc.scalar.activation(out=gst[s][:, sub, :], in_=ps[po][:, 384:408], func=AF.Sigmoid), [psb[po]], [tkb[s]])
                S.dma("sp", [(self.n_vsel[:, :, 4 * c:4 * c + 4, :].rearrange("g p s e -> p g s e"), vsl[s][:]),
                             (self.n_vwin[:, :, 4 * c:4 * c + 4, :].rearrange("g p s e -> p g s e"), vwn[s][:]),
                             (self.n_gates[:, 4 * c:4 * c + 4, :], gst[s][:])], [tkb[s]], [], tkb[s])

            if NSTOP < 6:
                S.barrier()
                return
            w1 = [self.sb(st, "w1_%d" % i, [128, 32, 128], BF16) for i in range(2)]
            w2 = [self.sb(st, "w2_%d" % i, [128, 64], BF16) for i in range(2)]
            peT = [self.sb(st, "peT%d" % i, [64, 32], BF16) for i in range(2)]
            cwtb = Buf("cmpw")
            pairs = []
            for kv in range(2):
                v = self.cmp_w1[kv][e].rearrange("(l d) h -> d l h", d=64)
                pairs += [(w1[kv][0:64], v), (w1[kv][64:128], v), (w2[kv][:], self.cmp_w2[kv][e]), (peT[kv][:], self.cmp_peT[kv][e])]
            S.dma("pool", pairs, [], [cwtb], cwtb)
            cvec = self.sb(st, "cvec", [128, 2], F32)
            cvecb = Buf("cvec")
            xh = self.sb(st, "xh", [128, NCP], F32)
            x2 = self.sb(st, "x2", [128, NCP], F32)
            sgm = self.sb(st, "sgm", [128, NCP], F32)
            hid = self.sb(st, "hid", [128, NCP], BF16)
            xhb, x2b, sgmb, hidb = Buf("xh"), Buf("x2"), Buf("sgm"), Buf("hid")
            kcs = self.sb(st, "kcs", [64, NCP], BF16)
            kcsb = Buf("kcs")
            vcs = self.sb(st, "vcs", [CW, NBLK, 130], BF16)
            vcsb = Buf("vcs")
            S.op("pool", lambda: nc.gpsimd.memset(hid[:], 0.0), [], [hidb])
            for kv in range(2):
                pv = it % 6
                it += 1
                for l_ in range(32):
                    S.op("pe", lambda l_=l_, pv=pv, kv=kv: nc.tensor.matmul(ps[pv][:, 0:1], w1[kv][0:64, l_, :], peT[kv][0:64, l_:l_ + 1], start=(l_ == 0), stop=(l_ == 31)),
                         [cwtb], [psb[pv]], acc=True)
                S.op("dve", lambda pv=pv, kv=kv: nc.vector.tensor_copy(out=cvec[:, kv:kv + 1], in_=ps[pv][:, 0:1]), [psb[pv]], [cvecb])
            for kv in range(2 if NSTOP >= 7 else 0):
                for g in range(2):
                    gr = slice(64 * g, 64 * g + 64)
                    ph = it % 6
                    it += 1
                    srcv = cT[kv][gr, :].rearrange("p (c s) -> p c s", s=16)
                    for l_ in range(32):
                        S.op("pe", lambda l_=l_, ph=ph, kv=kv, gr=gr, srcv=srcv: nc.tensor.matmul(ps[ph][:, 0:NCB], w1[kv][gr, l_, :], srcv[:, l_ // 16:l_ // 16 + NCB, l_ % 16],
                                                                                                   start=(l_ == 0), stop=(l_ == 31)),
                             [cwtb, cTb[kv]], [psb[ph]], acc=True)
                    S.op("act", lambda ph=ph, kv=kv: nc.scalar.activation(out=xh[:, 0:NCB], in_=ps[ph][:, 0:NCB], func=AF.Identity, bias=cvec[:, kv:kv + 1]), [psb[ph], cvecb], [xhb])
                    S.op("dve", lambda: nc.vector.tensor_tensor(out=x2[:, 0:NCB], in0=xh[:, 0:NCB], in1=xh[:, 0:NCB], op=ALU.mult), [xhb], [x2b])
                    S.op("dve", lambda: nc.vector.tensor_scalar(out=x2[:, 0:NCB], in0=x2[:, 0:NCB], scalar1=0.044715, scalar2=1.0, op0=ALU.mult, op1=ALU.add), [x2b], [x2b])
                    S.op("dve", lambda: nc.vector.tensor_tensor(out=x2[:, 0:NCB], in0=x2[:, 0:NCB], in1=xh[:, 0:NCB], op=ALU.mult), [x2b, xhb], [x2b])
                    S.op("act", lambda: nc.scalar.activation(out=sgm[:, 0:NCB], in_=x2[:, 0:NCB], func=AF.Sigmoid, scale=1.5957691216057308), [x2b], [sgmb])
                    S.op("dve", lambda: nc.vector.tensor_tensor(out=hid[:, 0:NCB], in0=xh[:, 0:NCB], in1=sgm[:, 0:NCB], op=ALU.mult), [xhb, sgmb], [hidb])
                    if NSTOP < 8:
                        continue
                    if kv == 1 and NSTOP < 9:
                        continue
                    if kv == 0:
                        pk = it % 6
                        it += 1
                        S.op("pe", lambda pk=pk: nc.tensor.matmul(ps[pk][0:64, 0:NCP], w2[0][:], hid[:], start=True, stop=True), [cwtb, hidb], [psb[pk]])
                        S.op("dve", lambda pk=pk: nc.vector.tensor_copy(out=kcs[:], in_=ps[pk][0:64, 0:NCP]), [psb[pk]], [kcsb])
                        S.dma("sp", [(self.n_kc[g], kcs[:])], [kcsb], [], kcsb)
                    else:
                        S.dma("pool", [(vcs[:, b, 0:64], self.c_overlap[b * 128:b * 128 + CW, :]) for b in range(NBLK)], [], [vcsb], vcsb)
                        S.op("pool", lambda: nc.gpsimd.memset(vcs[:, :, 128:130], 1.0), [], [vcsb])
                        for b in range(NBLK):
                            pk = it % 6
                            it += 1
                            S.op("pe", lambda pk=pk, b=b: nc.tensor.matmul(ps[pk][0:CW, 0:64], hid[:, b * 128:b * 128 + CW], w2[1][:], start=True, stop=True), [cwtb, hidb], [psb[pk]])
                            S.op("dve", lambda pk=pk, b=b: nc.vector.tensor_copy(out=vcs[:, b, 64:128], in_=ps[pk][0:CW, 0:64]), [psb[pk]], [vcsb])
                        S.dma("sp", [(self.n_vc[g], vcs[:])], [vcsb], [], vcsb)
            S.barrier()

    def phase_nsa_att(self, l):
        nc, S = self.nc, self.S
        T = self.T
        NB = T // 128
        NCB, NCP, CW, NBLK = self.NCB, self.NCP, self.CW, self.NBLK
        ps, psb = self.ps, self.psb
        pst = self.pst
        with ExitStack() as st:
            KE = self.sb(st, "KE", [128, T], BF16)
            keb_k, keb_e = Buf("ke_k"), Buf("ke_e")
            kwn = self.sb(st, "kwn", [64, T], BF16)
            kcs = self.sb(st, "kcs", [64, NCP], BF16)
            vcs = self.sb(st, "vcs", [CW, NBLK, 130], BF16)
            vsl = self.sb(st, "vsl", [128, NB, 66], BF16)
            vwn = self.sb(st, "vwn", [128, NB, 66], BF16)
            QM = self.sb(st, "QM", [128, 4, T], BF16)
            gin = Buf("gin")
            qm_m = [Buf("qm_m%d" % c) for c in range(self.NCH)]
            gts = [self.sb(st, "gts%d" % i, [128, 4, 24], F32) for i in range(2)]
            slb = [self.sb(st, "slb%d" % i, [128, 4, 64], F32) for i in range(2)]
            gtb = [Buf("gt%d" % i) for i in range(2)]
            P = [self.sb(st, "P%d" % i, [128, CH], BF16) for i in range(4)]
            Pb = [Buf("P%d" % i) for i in range(4)]
            yacc = self.sb(st, "yacc", [128, 4, 256], F32)
            yaccb = [Buf("yacc%d" % i) for i in range(4)]
            ybf = self.sb(st, "ybf", [128, 4, 256], BF16)
            ybfb = Buf("ybf")
            imp = self.sb(st, "imp", [128, 4, 64], F32)
            impb = [Buf("imp%d" % i) for i in range(4)]
            sm = self.sb(st, "sm", [128, 8], F32)
            smb = Buf("sm")
            sm4 = self.sb(st, "sm4", [128, 12], F32)
            sm4b = Buf("sm4")
            sc = self.sb(st, "sc", [128, 64], F32)
            wk = self.sb(st, "wk", [128, 64], F32)
            wk2 = self.sb(st, "wk2", [128, 64], F32)
            m8 = self.sb(st, "m8", [128, 16], F32)
            tkb = Buf("topk")
            mk = self.sb(st, "mk", [128, 128], BF16)
            mkb = Buf("mk")
            yT = [self.sb(st, "yT%d" % i, [128, 2, CH], BF16) for i in range(2)]
            yTb = [Buf("yT%d" % i) for i in range(2)]
            pstb = [self.pstb] * 8
            S.dma("pool", [(KE[64:128, :], self.c_etab[:, :])], [], [keb_e], keb_e)
            S.op("pool", lambda: nc.gpsimd.memset(mk[:], 0.0), [], [mkb])
            it = 0
            pit = 0
            tit = 0

            def evac(acc_ap, z_ap, gate_ap, out_ap, first, accb, outb, gb):
                S.op("dve", lambda: nc.vector.tensor_scalar_max(out=sm[:, 0:1], in0=z_ap, scalar1=1e-30), [accb], [smb])
                S.op("dve", lambda: nc.vector.reciprocal(out=sm[:, 1:2], in_=sm[:, 0:1]), [smb], [smb])
                S.op("dve", lambda: nc.vector.tensor_tensor(out=sm[:, 2:3], in0=sm[:, 1:2], in1=gate_ap, op=ALU.mult), [smb, gb], [smb])
                if first:
                    S.op("dve", lambda: nc.vector.tensor_scalar(out=out_ap, in0=acc_ap, scalar1=sm[:, 2:3], scalar2=None, op0=ALU.mult), [accb, smb], [outb])
                else:
                    S.op("dve", lambda: nc.vector.scalar_tensor_tensor(out=out_ap, in0=acc_ap, scalar=sm[:, 2:3], in1=out_ap, op0=ALU.mult, op1=ALU.add), [accb, smb, outb], [outb])

            items = []

            def add_tile(s1, s2, pre=None, post=None, flush=False):
                items.append(dict(s1=s1, s2=s2, pre=pre or [], post=post or [], flush=flush))

            def load_g(g):
                S.dma("sp", [(KE[0:64, :], self.n_ksel[g]), (kwn[:], self.n_kwin[g]), (kcs[:], self.n_kc[g]), (vcs[:], self.n_vc[g]),
                             (vsl[:], self.n_vsel[g]), (vwn[:], self.n_vwin[g]),
                             (QM[0:64, :, :], self.n_q[4 * g:4 * g + 4].rearrange("h d t -> d h t"))], [], [gin, keb_k], gin)

            def load_gates(g, c):
                gs = (g * self.NCH + c) % 2
                S.dma("sp", [(gts[gs][:], self.n_gates[:, 4 * c:4 * c + 4, :]), (slb[gs][:], self.c_selbias[:, 4 * c:4 * c + 4, :])], [], [gtb[gs]], gtb[gs])

            def cmp_evac(g, c, r, banks):
                gs = (g * self.NCH + c) % 2
                hd = 4 * g + r
                for sub in range(4):
                    bk = banks[sub // 2]
                    c0 = (sub % 2) * 129
                    evac(ps[bk][:, c0 + 64:c0 + 128], ps[bk][:, c0 + 128:c0 + 129], gts[gs][:, sub, 3 * hd:3 * hd + 1], yacc[:, sub, r * 64:(r + 1) * 64],
                         True, psb[bk], yaccb[sub], gtb[gs])
                    if r == 0:
                        S.op("dve", lambda: nc.vector.tensor_scalar(out=imp[:, sub, :], in0=ps[bk][:, c0:c0 + 64], scalar1=sm[:, 1:2], scalar2=None, op0=ALU.mult),
                             [psb[bk], smb], [impb[sub]])
                    else:
                        S.op("dve", lambda: nc.vector.scalar_tensor_tensor(out=imp[:, sub, :], in0=ps[bk][:, c0:c0 + 64], scalar=sm[:, 1:2], in1=imp[:, sub, :],
                                                                             op0=ALU.mult, op1=ALU.add), [psb[bk], smb, impb[sub]], [impb[sub]])

            def topk(g, c):
                Q0 = c * CH
                gs = (g * self.NCH + c) % 2
                for sub in range(4):
                    S.op("dve", lambda: nc.vector.tensor_tensor(out=sc[:], in0=imp[:, sub, :], in1=slb[gs][:, sub, :], op=ALU.add), [impb[sub], gtb[gs]], [tkb])
                    S.op("dve", lambda: nc.vector.max(out=m8[:, 0:8], in_=sc[:]), [tkb], [tkb])
                    S.op("dve", lambda: nc.vector.match_replace(out=wk[:], in_to_replace=m8[:, 0:8], in_values=sc[:], imm_value=-3e38), [tkb], [tkb])
                    S.op("dve", lambda: nc.vector.max(out=m8[:, 8:16], in_=wk[:]), [tkb], [tkb])
                    S.op("dve", lambda: nc.vector.match_replace(out=wk2[:], in_to_replace=m8[:, 8:16], in_values=wk[:], imm_value=-3e38), [tkb], [tkb])
                    S.op("dve", lambda: nc.vector.tensor_tensor(out=wk[:], in0=sc[:], in1=wk2[:], op=ALU.subtract), [tkb], [tkb])
                    S.op("dve", lambda: nc.vector.tensor_scalar_min(out=mk[:, 64:128], in0=wk[:], scalar1=1.0), [tkb], [mkb])
                    S.op("pe", lambda: nc.tensor.transpose(out=pst[:, 0:128], in_=mk[:], identity=self.ident_bf[:]), [mkb, self.gconst], [self.pstb])
                    for r in range(4):
                        q0 = Q0 + sub * 128
                        if r % 2 == 0:
                            S.op("dve", lambda: nc.vector.tensor_scalar_add(out=QM[64:128, r, q0:q0 + 128], in0=pst[64:128, 0:128], scalar1=-1.0),
                                 [self.pstb], [qm_m[c]])
                        else:
                            S.op("act", lambda: nc.scalar.activation(out=QM[64:128, r, q0:q0 + 128], in_=pst[64:128, 0:128], func=AF.Identity, bias=-1.0),
                                 [self.pstb], [qm_m[c]])

            def branch_evac(g, c, r, br, bk):
                gs = (g * self.NCH + c) % 2
                hd = 4 * g + r
                gidx = 3 * hd + (2 if br == "win" else 1)
                zv = ps[bk][:, 0:260].rearrange("p (s e) -> p s e", e=65)[:, :, 64]
                S.op("dve", lambda: nc.vector.tensor_scalar_max(out=sm4[:, 0:4], in0=zv, scalar1=1e-30), [psb[bk]], [sm4b])
                S.op("dve", lambda: nc.vector.reciprocal(out=sm4[:, 4:8], in_=sm4[:, 0:4]), [sm4b], [sm4b])
                S.op("dve", lambda: nc.vector.tensor_tensor(out=sm4[:, 8:12], in0=sm4[:, 4:8], in1=gts[gs][:, :, gidx], op=ALU.mult), [sm4b, gtb[gs]], [sm4b])
                for sub in range(4):
                    S.op("dve", lambda: nc.vector.scalar_tensor_tensor(out=yacc[:, sub, r * 64:(r + 1) * 64], in0=ps[bk][:, sub * 65:sub * 65 + 64], scalar=sm4[:, 8 + sub:9 + sub],
                                                                         in1=yacc[:, sub, r * 64:(r + 1) * 64], op0=ALU.mult, op1=ALU.add),
                         [psb[bk], sm4b, yaccb[sub]], [yaccb[sub]])

            def epilogue(g, c):
                cs = slice(c * CH, (c + 1) * CH)
                S.op("dve", lambda: nc.vector.tensor_copy(out=ybf[:], in_=yacc[:]), yaccb, [ybfb])
                ys = (g * self.NCH + c) % 2
                for sub in range(4):
                    for half in range(2):
                        S.op("pe", lambda: nc.tensor.transpose(out=pst[:, 128:256], in_=ybf[:, sub, half * 128:(half + 1) * 128], identity=self.ident_bf[:]),
                             [ybfb, self.gconst], [self.pstb])
                        S.op("act", lambda: nc.scalar.activation(out=yT[ys][:, half, sub * 128:(sub + 1) * 128], in_=pst[:, 128:256], func=AF.Copy),
                             [self.pstb], [yTb[ys]])
                r0 = 512 + 256 * g
                S.dma("sp", [(self.mixT[r0:r0 + 256, cs].rearrange("(h p) t -> p h t", p=128), yT[ys][:])], [yTb[ys]], [], yTb[ys])

            tcount = [0]

            def mk_tile(kind, g, c, r, j, bk, subs_fl, mask, pre=None, post=None, flush=False):
                t = tcount[0]
                tcount[0] += 1
                pz, pi = t % 2, t % 4
                Q0 = c * CH
                cs = slice(Q0, Q0 + CH)
                rows = CW if kind == "cmp" else 128

                def s1():
                    if kind == "cmp":
                        S.op("pe", lambda: nc.tensor.matmul(ps[pz][0:CW, :], kcs[0:64, j * 128:j * 128 + CW], QM[0:64, r, cs], start=True, stop=True), [gin], [psb[pz]])
                    elif kind == "win":
                        S.op("pe", lambda: nc.tensor.matmul(ps[pz][:], kwn[0:64, j * 128:(j + 1) * 128], QM[0:64, r, cs], start=True, stop=True), [gin], [psb[pz]])
                    else:
                        S.op("pe", lambda: nc.tensor.matmul(ps[pz][:], KE[:, j * 128:(j + 1) * 128], QM[:, r, cs], start=True, stop=True),
                             [gin, keb_k, keb_e, qm_m[c]], [psb[pz]])
                    S.op("act", lambda: nc.scalar.activation(out=P[pi][0:rows, :], in_=ps[pz][0:rows, :], func=AF.Exp), [psb[pz]], [Pb[pi]])
                    if mask is not None:
                        S.op("pool", lambda: nc.gpsimd.affine_select(out=P[pi][0:rows, :], in_=P[pi][0:rows, :], compare_op=ALU.is_ge, fill=0.0, **mask), [Pb[pi]], [Pb[pi]])

                def s2():
                    for (sub, st_, sp_) in subs_fl:
                        if kind == "cmp":
                            b_ = bk[sub // 2]
                            c0 = (sub % 2) * 129
                            S.op("pe", lambda: nc.tensor.matmul(ps[b_][:, c0:c0 + 129], P[pi][0:CW, sub * 128:(sub + 1) * 128], vcs[0:CW, j, 0:129],
                                                                start=st_, stop=sp_, skip_group_check=True), [Pb[pi], gin], [psb[b_]], acc=True)
                        else:
                            vt = vwn if kind == "win" else vsl
                            S.op("pe", lambda: nc.tensor.matmul(ps[bk][:, sub * 65:sub * 65 + 65], P[pi][:, sub * 128:(sub + 1) * 128], vt[:, j, 0:65],
                                                                start=st_, stop=sp_, skip_group_check=True), [Pb[pi], gin], [psb[bk]], acc=True)

                add_tile(s1, s2, pre, post, flush)

            from functools import partial
            for g in range(2):
                for c in range(self.NCH):
                    Q0 = c * CH
                    first_item_pre = [partial(load_gates, g, c)]
                    flush = False
                    if c == 0:
                        first_item_pre = [partial(load_g, g)] + first_item_pre
                        flush = True
                    vblk = [b for b in range(NBLK) if 16 * (128 * b) + 31 <= Q0 + CH - 1]
                    for r in range(4):
                        banks = (2, 3) if r % 2 == 0 else (4, 5)
                        for bi, b in enumerate(vblk):
                            mask = None
                            if not (16 * (128 * b + CW - 1) + 31 <= Q0):
                                mask = dict(pattern=[[1, CH]], base=Q0 - 2048 * b - 31, channel_multiplier=-16)
                            subs_fl = [(sub, (bi == 0 and sub % 2 == 0), (bi == len(vblk) - 1)) for sub in range(4)]
                            post = []
                            if bi == len(vblk) - 1:
                                post.append(partial(cmp_evac, g, c, r, banks))
                                if r == 3:
                                    post.append(partial(topk, g, c))
                            mk_tile("cmp", g, c, r, b, banks, subs_fl, mask, pre=first_item_pre, post=post, flush=flush)
                            first_item_pre = None
                            flush = False
                    for br in ("win", "sel"):
                        for r in range(4):
                            if br == "win":
                                blocks = list(range(max(0, 4 * c - 4), 4 * c + 4))
                                bk = 6 if r % 2 == 0 else 5
                            else:
                                blocks = list(range(0, 4 * c + 4))
                                bk = 2 + (r % 3)
                            for j in blocks:
                                rel = 128 * j - Q0
                                if rel >= 0:
                                    subs = list(range(rel // 128, 4))
                                    mask = dict(pattern=[[1, CH]], base=Q0 - 128 * j, channel_multiplier=-1)
                                elif br == "win":
                                    subs = list(range(0, (rel + 512) // 128 + 1))
                                    mask = dict(pattern=[[-1, CH]], base=128 * j - Q0 + 511, channel_multiplier=1)
                                else:
                                    subs = [0, 1, 2, 3]
                                    mask = None
                                subs_fl = [(sub, (j == blocks[0] and sub == subs[0]), (j == 4 * c + sub)) for sub in subs]
                                post = []
                                if j == blocks[-1]:
                                    post.append(partial(branch_evac, g, c, r, br, bk))
                                    if br == "sel" and r == 3:
                                        post.append(partial(epilogue, g, c))
                                mk_tile(br, g, c, r, j, bk, subs_fl, mask, post=post)
            SKEW = 2
            pending = []

            def retire():
                itm = pending.pop(0)
                itm["s2"]()
                for f in itm["post"]:
                    f()

            for itm in items:
                if itm["flush"]:
                    while pending:
                        retire()
                for f in itm["pre"]:
                    f()
                itm["s1"]()
                pending.append(itm)
                while len(pending) > SKEW:
                    retire()
            while pending:
                retire()
            S.barrier()


def host_constants(T):
    pos = np.arange(T)
    s = np.arange(64)
    cur = pos[:, None] // 64
    forced = (s[None, :] == 0) | (s[None, :] == cur) | (s[None, :] == cur - 1)
    valid = s[None, :] * 64 <= pos[:, None]
    selbias = np.where(valid, np.where(forced, 1e4, 0.0), NEG).astype(np.float32)
    selbias = np.ascontiguousarray(selbias.reshape(T // 128, 128, 64).transpose(1, 0, 2))
    etab = np.where((np.arange(T)[None, :] // 64) == s[:, None], BIG, 0.0).astype(np.float32)
    c0 = np.arange(256) * 16
    s0 = np.arange(64) * 64
    lo = np.maximum(c0[:, None], s0[None, :])
    hi = np.minimum(c0[:, None] + 32, s0[None, :] + 64)
    overlap = (np.maximum(hi - lo, 0).astype(np.float32) / 32)
    overlap[255:] = 0
    k = np.arange(128)
    tri = np.concatenate([np.where(k[:, None] >= k[None, :], -1.0, 0.0), -np.ones((128, 128))], axis=1).astype(np.float32)
    ident = np.eye(128, dtype=np.float32)
    return dict(c_selbias=selbias, c_etab=etab, c_overlap=overlap.astype(np.float32), c_tri=tri, c_ident=ident)


def pack_gains(norm_ffn1, norm_mix, norm_ffn2, norm_final, depth=4):
    g = np.ones((13, D), np.float32)
    for l in range(depth):
        g[l] = norm_ffn1[l]
        g[4 + l] = norm_mix[l]
        g[8 + l] = norm_ffn2[l]
    g[12] = norm_final
    return np.ascontiguousarray(g.reshape(13, KC, 128).transpose(2, 0, 1).reshape(128, 13 * KC))


def make_shared_inputs(inputs, T, depth=4):
    f = lambda a: np.ascontiguousarray(np.asarray(a, dtype=np.float32))
    ne = (depth + 1) // 2
    sh = {}
    sh["gains"] = pack_gains(f(inputs["norm_ffn1"]), f(inputs["norm_mix"]), f(inputs["norm_ffn2"]), f(inputs["norm_final"]), depth)
    for k in ("w_ffn1_in", "w_ffn2_in", "w_ffn1_out", "w_ffn2_out", "w_in_ab", "w_out_ab", "w_qkv_sb", "w_out_sb",
              "cmp_w1_k", "cmp_w1_v", "cmp_w2_k", "cmp_w2_v"):
        sh[k] = f(inputs[k])
    cw = f(inputs["conv_w"])
    sh["convw"] = np.ascontiguousarray(cw.reshape(ne, 3, 4, 128).transpose(3, 0, 2, 1).reshape(128, ne * 12))
    sh["cmp_peT_k"] = np.ascontiguousarray(f(inputs["cmp_pe_k"]).transpose(0, 2, 1))
    sh["cmp_peT_v"] = np.ascontiguousarray(f(inputs["cmp_pe_v"]).transpose(0, 2, 1))
    sh.update(host_constants(T))
    return sh


_PROG_CACHE = {}


def kernel(**inputs):
    x = np.asarray(inputs["x"], dtype=np.float32)
    B, T, _ = x.shape
    key = (T,)
    if key not in _PROG_CACHE:
        _PROG_CACHE[key] = Prog(T).build()
    nc = _PROG_CACHE[key]
    sh = make_shared_inputs(inputs, T)
    in_maps = []
    for b in range(B):
        m = dict(sh)
        m["xT"] = np.ascontiguousarray(x[b].T)
        in_maps.append(m)
    res = run_bass_kernel_spmd(nc, in_maps, core_ids=list(range(B)))
    out = np.stack([np.ascontiguousarray(r["outT"].T) for r in res.results], axis=0)
    return out.astype(np.float32)
```

```python
import numpy as np
from contextlib import ExitStack
import concourse.bass as bass
import concourse.mybir as mybir
from concourse.bass_utils import run_bass_kernel_spmd

F32 = mybir.dt.float32
BF16 = mybir.dt.bfloat16
AF = mybir.ActivationFunctionType
ALU = mybir.AluOpType

D = 1024
KC = 8
DFF = 2816
NJ = 22
EPS = 1e-6
NEG = -1e30
BIG = 16384.0
AB_IN = 2840
CH = 512
import os
NSTOP = int(os.environ.get('NSTOP', '99'))
NATT = int(os.environ.get('NATT', '99'))


class Buf:
    __slots__ = ("name", "w", "r", "sem", "sem_sw")

    def __init__(self, name):
        self.name = name
        self.w = None
        self.r = {}
        self.sem = None
        self.sem_sw = None


class DSem:
    __slots__ = ("h", "cnt", "name")

    def __init__(self, h, name):
        self.h = h
        self.cnt = 0
        self.name = name


class Sched:
    def __init__(self, nc, st, n_dma_sems=88):
        self.nc = nc
        self.eng = {"pe": nc.tensor, "act": nc.scalar, "dve": nc.vector, "pool": nc.gpsimd, "sp": nc.sync}
        self.sem = {}
        self.cnt = {}
        for e in ("pe", "act", "dve", "pool"):
            self.sem[e] = st.enter_context(nc.semaphore("s_" + e))
            self.cnt[e] = 0
        self.known = {e: {} for e in self.eng}
        alld = [DSem(st.enter_context(nc.semaphore("d%d" % i)), "d%d" % i) for i in range(n_dma_sems)]
        self.free_dsems = alld[:n_dma_sems - 24]
        self.free_dsems_sw = alld[n_dma_sems - 24:]
        self.used_dsems = []
        self.used_dsems_sw = []
        self.bufs_with_sem = []

    def _waits(self, e, evs):
        need = {}
        for (key, h, val) in evs:
            if self.known[e].get(key, 0) < val and need.get(key, (None, 0))[1] < val:
                need[key] = (h, val)
        for key, (h, val) in need.items():
            self.eng[e].wait_ge(h, val)
            self.known[e][key] = val

    def _collect(self, e, reads, writes, acc):
        ev = []
        for b in reads:
            if b.w is not None:
                ev.append(b.w)
        for b in writes:
            if b.w is not None and not (acc and b.w[0] == e):
                ev.append(b.w)
            for k, v in b.r.items():
                ev.append(v)
        return ev

    def op(self, e, fn, reads=(), writes=(), acc=False):
        self._waits(e, self._collect(e, reads, writes, acc))
        inst = fn()
        self.cnt[e] += 1
        inst.then_inc(self.sem[e], 1)
        me = (e, self.sem[e], self.cnt[e])
        for b in reads:
            b.r[e] = me
        for b in writes:
            b.w = me
            b.r = {}
        return inst

    def dma(self, q, pairs, reads, writes, owner):
        self._waits(q, self._collect(q, reads, writes, False))
        if q == "pool":
            if owner.sem_sw is None:
                owner.sem_sw = self.free_dsems_sw.pop()
                self.used_dsems_sw.append(owner.sem_sw)
                self.bufs_with_sem.append(owner)
            ds = owner.sem_sw
        else:
            if owner.sem is None:
                owner.sem = self.free_dsems.pop()
                self.used_dsems.append(owner.sem)
                self.bufs_with_sem.append(owner)
            ds = owner.sem
        for (o, i) in pairs:
            self.eng[q].dma_start(out=o, in_=i).then_inc(ds.h, 16)
            ds.cnt += 16
        me = (ds.name, ds.h, ds.cnt)
        for b in reads:
            b.r[ds.name] = me
        for b in writes:
            b.w = me
            b.r = {}

    def barrier(self):
        evs = [(e, self.sem[e], self.cnt[e]) for e in self.sem if self.cnt[e] > 0]
        evs += [(d.name, d.h, d.cnt) for d in self.used_dsems + self.used_dsems_sw if d.cnt > 0]
        for e in self.eng:
            self._waits(e, evs)
        for b in self.bufs_with_sem:
            b.sem = None
            b.sem_sw = None
        self.bufs_with_sem = []
        self.free_dsems.extend(self.used_dsems)
        self.free_dsems_sw.extend(self.used_dsems_sw)
        self.used_dsems = []
        self.used_dsems_sw = []


class Prog:
    def __init__(self, T, depth=4, plan=None):
        self.T = T
        self.NCH = T // CH
        self.depth = depth
        self.plan = plan
        nc = self.nc = bass.Bass("TRN2", target_bir_lowering=False)
        self.st = ExitStack()
        self.inp = {}
        self.declare_io()

    def din(self, name, shape, dt=F32):
        t = self.nc.dram_tensor(name, list(shape), dt, kind="ExternalInput").ap()
        self.inp[name] = t
        return t

    def dscr(self, name, shape, dt):
        return self.nc.dram_tensor(name, list(shape), dt).ap()

    def declare_io(self):
        T = self.T
        nd = self.depth
        ne = (nd + 1) // 2
        no = nd // 2
        self.xT = self.din("xT", [D, T])
        self.gains = self.din("gains", [128, 13 * KC])
        self.w_ffn_in = [self.din("w_ffn1_in", [nd, D, 2 * DFF]), self.din("w_ffn2_in", [nd, D, 2 * DFF])]
        self.w_ffn_out = [self.din("w_ffn1_out", [nd, DFF, D]), self.din("w_ffn2_out", [nd, DFF, D])]
        self.w_in_ab = self.din("w_in_ab", [ne, D, AB_IN])
        self.w_out_ab = self.din("w_out_ab", [ne, D, D])
        self.w_qkv_sb = self.din("w_qkv_sb", [max(no, 1), D, 3 * D])
        self.w_out_sb = self.din("w_out_sb", [max(no, 1), D, D])
        self.convw = self.din("convw", [128, ne * 12])
        self.cmp_w1 = [self.din("cmp_w1_k", [ne, 2048, 128]), self.din("cmp_w1_v", [ne, 2048, 128])]
        self.cmp_w2 = [self.din("cmp_w2_k", [ne, 128, 64]), self.din("cmp_w2_v", [ne, 128, 64])]
        self.cmp_peT = [self.din("cmp_peT_k", [ne, 64, 32]), self.din("cmp_peT_v", [ne, 64, 32])]
        self.c_selbias = self.din("c_selbias", [128, T // 128, 64])
        self.c_etab = self.din("c_etab", [64, T])
        self.c_overlap = self.din("c_overlap", [256, 64])
        self.c_tri = self.din("c_tri", [128, 256])
        self.c_ident = self.din("c_ident", [128, 128])
        self.outT = self.nc.dram_tensor("outT", [D, T], F32, kind="ExternalOutput").ap()
        self.xr = self.dscr("xr", [D, T], F32)
        self.mixT = self.dscr("mixT", [D, T], BF16)
        self.sb_q = self.dscr("sb_q", [D, T], BF16)
        self.sb_k = self.dscr("sb_k", [D, T], BF16)
        self.sb_v = self.dscr("sb_v", [T, D], BF16)
        self.n_q = self.dscr("n_q", [8, 64, T], BF16)
        self.n_ksel = self.dscr("n_ksel", [2, 64, T], BF16)
        self.n_kwin = self.dscr("n_kwin", [2, 64, T], BF16)
        NB = T // 128
        self.NCP = T // 16
        self.NCB = T // 16 - 1
        self.CW = min(128, self.NCP)
        self.NBLK = (self.NCP + 127) // 128
        self.n_vsel = self.dscr("n_vsel", [2, 128, NB, 66], BF16)
        self.n_vwin = self.dscr("n_vwin", [2, 128, NB, 66], BF16)
        self.n_gates = self.dscr("n_gates", [128, NB, 24], F32)
        self.n_kc = self.dscr("n_kc", [2, 64, self.NCP], BF16)
        self.n_vc = self.dscr("n_vc", [2, self.CW, self.NBLK, 130], BF16)

    def sb(self, st, name, shape, dt):
        self._uid = getattr(self, "_uid", 0) + 1
        return st.enter_context(self.nc.sbuf_tensor("%s_%d" % (name, self._uid), list(shape), dt))

    def build(self):
        nc = self.nc
        st = self.st
        S = self.S = Sched(nc, st)
        self.psw = [st.enter_context(nc.psum_tensor("psw%d" % i, [128, 1024], F32)) for i in range(3)]
        ps6 = st.enter_context(nc.psum_tensor("ps6", [128, 512], F32))
        self.ps = [self.psw[i // 2][:, (i % 2) * 512:(i % 2 + 1) * 512] for i in range(6)] + [ps6[:]]
        self.psb = [Buf("ps%d" % i) for i in range(7)]
        self.pst = st.enter_context(nc.psum_tensor("pst", [128, 1024], BF16))
        self.pstb = Buf("pst")
        self.gains_sb = self.sb(st, "gains_sb", [128, 13 * KC], F32)
        self.ones_bf = self.sb(st, "ones_bf", [128, 128], BF16)
        self.tri_bf = self.sb(st, "tri_bf", [128, 256], BF16)
        self.ident_bf = self.sb(st, "ident_bf", [128, 128], BF16)
        self.gconst = Buf("gconst")
        S.dma("sp", [(self.gains_sb[:], self.gains[:, :])], [], [self.gconst], self.gconst)
        S.dma("pool", [(self.tri_bf[:], self.c_tri[:, :]), (self.ident_bf[:], self.c_ident[:, :])], [], [self.gconst], self.gconst)
        S.op("dve", lambda: nc.vector.memset(self.ones_bf[:], 1.0), [], [self.gconst])
        S.barrier()

        plan = self.plan
        if plan is None:
            plan = []
            for l in range(self.depth):
                plan.append(("ffn", l, 0))
                plan.append(("mix", l))
                plan.append(("ffn", l, 1) if l < self.depth - 1 else ("ffn", l, 1, True))
        src = self.xT
        for ph in plan:
            if ph[0] == "ffn":
                l, which = ph[1], ph[2]
                gi = (l if which == 0 else 8 + l)
                self.phase_ffn(src, self.w_ffn_in[which][l], self.w_ffn_out[which][l], gi, final=(len(ph) > 3 and ph[3]))
                src = self.xr
            elif ph[0] == "mix":
                l = ph[1]
                if l % 2 == 1:
                    self.phase_sb_qkv(src, l)
                    self.phase_sb_att(l)
                    self.phase_mix_out(src, self.w_out_sb[l // 2])
                else:
                    self.phase_nsa_proj(src, l)
                    self.phase_nsa_att(l)
                    self.phase_mix_out(src, self.w_out_ab[l // 2])
                src = self.xr
            elif ph[0] == "sb_qkv":
                self.phase_sb_qkv(src, ph[1])
            elif ph[0] == "sb_att":
                self.phase_sb_att(ph[1])
            elif ph[0] == "mix_out":
                self.phase_mix_out(src, self.w_out_sb[0])
                src = self.xr
            elif ph[0] == "nsa_proj":
                self.phase_nsa_proj(src, ph[1])
            elif ph[0] == "nsa_att":
                self.phase_nsa_att(ph[1])
            elif ph[0] == "final":
                self.phase_final(src)
        S.barrier()
        st.close()
        return nc

    def xview(self, ap, c, w=CH):
        return ap.rearrange("(k p) t -> p k t", p=128)[:, :, c * w:(c + 1) * w]

    def load_weight(self, dst_tile, dst_buf, src_ap, nk, split=1):
        F = src_ap.shape[1]
        v = src_ap.rearrange("(k p) f -> p k f", p=128)
        pairs = []
        fs = F // split
        for k in range(nk):
            for s in range(split):
                f0 = s * fs
                f1 = F if s == split - 1 else (s + 1) * fs
                pairs.append((dst_tile[:, k, f0:f1], v[:, k, f0:f1]))
        self.S.dma("pool", pairs, [], [dst_buf], dst_buf)

    def load_weight_groups(self, dst_tile, src_ap, nk, groups):
        v = src_ap.rearrange("(k p) f -> p k f", p=128)
        bufs = []
        for gi_, ranges in enumerate(groups):
            b = Buf("wg%d" % gi_)
            pairs = [(dst_tile[:, k, c0:c1], v[:, k, c0:c1]) for k in range(nk) for (c0, c1) in ranges]
            self.S.dma("pool", pairs, [], [b], b)
            bufs.append(b)
        return bufs

    def emit_norm(self, xt, xb, hT, hb, gi, sq, sqb, rstd, rstdb, ps_ss, ps_ssb, w=CH):
        nc, S = self.nc, self.S
        nsq = len(sqb)
        for k in range(KC):
            S.op("pool", lambda k=k: nc.gpsimd.tensor_tensor(out=sq[:, k % nsq, :w], in0=xt[:, k, :w], in1=xt[:, k, :w], op=ALU.mult),
                 [xb[k]], [sqb[k % nsq]])
            S.op("pe", lambda k=k: nc.tensor.matmul(ps_ss[:, :w], self.ones_bf[:], sq[:, k % nsq, :w], start=(k == 0), stop=(k == KC - 1)),
                 [sqb[k % nsq], self.gconst], [ps_ssb], acc=True)
        S.op("act", lambda: nc.scalar.activation(out=rstd[:, :w], in_=ps_ss[:, :w], func=AF.Ln, scale=1.0 / D, bias=EPS), [ps_ssb], [rstdb])
        S.op("act", lambda: nc.scalar.activation(out=rstd[:, :w], in_=rstd[:, :w], func=AF.Exp, scale=-0.5), [rstdb], [rstdb])
        for k in range(KC):
            S.op("dve", lambda k=k: nc.vector.scalar_tensor_tensor(out=hT[:, k, :w], in0=xt[:, k, :w], scalar=self.gains_sb[:, gi * KC + k:gi * KC + k + 1],
                                                                     in1=rstd[:, :w], op0=ALU.mult, op1=ALU.mult),
                 [xb[k], rstdb, self.gconst], [hb[k]])

    def phase_ffn(self, src, w_in, w_out, gi, final=False):
        nc, S = self.nc, self.S
        with ExitStack() as st:
            win = self.sb(st, "win", [128, KC, 2 * DFF], BF16)
            wout = self.sb(st, "wout", [128, NJ, D], BF16)
            winb, woutb = Buf("win"), Buf("wout")
            xs = [self.sb(st, "x%d" % i, [128, KC, CH], F32) for i in range(2)]
            xsb = [[Buf("x%d_%d" % (i, k)) for k in range(KC)] for i in range(2)]
            hT = self.sb(st, "hT", [128, KC, CH], BF16)
            hb = [Buf("h%d" % k) for k in range(KC)]
            sq = self.sb(st, "sq", [128, 2, CH], BF16)
            sqb = [Buf("sq%d" % k) for k in range(2)]
            act = self.sb(st, "act", [128, NJ, CH], BF16)
            actb = [Buf("act%d" % j) for j in range(NJ)]
            sg = [self.sb(st, "sg%d" % i, [128, CH], F32) for i in range(2)]
            sgb = [Buf("sg%d" % i) for i in range(2)]
            rstd = self.sb(st, "rstd", [128, CH], F32)
            rstdb = Buf("rstd")
            jr = [(0, 6), (6, 12), (12, 17), (17, 22)]
            wing = self.load_weight_groups(win, w_in, KC, [[(a * 128, b * 128), (DFF + a * 128, DFF + b * 128)] for (a, b) in jr])
            winj = [wing[[i for i, (a, b) in enumerate(jr) if a <= j < b][0]] for j in range(NJ)]
            woutg = self.load_weight_groups(wout, w_out, NJ, [[(0, 512)], [(512, 1024)]])
            ps, psb = self.ps, self.psb

            def load_x(c):
                s = c % 2
                S.dma("sp", [(xs[s][:], self.xview(src, c))], [], xsb[s], xsb[s][0])

            load_x(0)
            self.emit_norm(xs[0], xsb[0], hT, hb, gi, sq, sqb, rstd, rstdb, ps[6], psb[6])
            for c in range(self.NCH):
                s = c % 2
                xt, xb = xs[s], xsb[s]
                if c + 1 < self.NCH:
                    load_x(c + 1)
                for j in range(NJ):
                    pg, pu = j % 2, 2 + j % 2
                    for k in range(KC):
                        S.op("pe", lambda k=k, j=j, pg=pg: nc.tensor.matmul(ps[pg][:], win[:, k, j * 128:(j + 1) * 128], hT[:, k, :], start=(k == 0), stop=(k == KC - 1)),
                             [winj[j], hb[k]], [psb[pg]], acc=True)
                    for k in range(KC):
                        S.op("pe", lambda k=k, j=j, pu=pu: nc.tensor.matmul(ps[pu][:], win[:, k, DFF + j * 128:DFF + (j + 1) * 128], hT[:, k, :], start=(k == 0), stop=(k == KC - 1)),
                             [winj[j], hb[k]], [psb[pu]], acc=True)
                    S.op("act", lambda j=j, pg=pg: nc.scalar.activation(out=sg[j % 2][:], in_=ps[pg][:], func=AF.Silu),
                         [psb[pg]], [sgb[j % 2]])
                    S.op("dve", lambda j=j, pu=pu: nc.vector.tensor_tensor(out=act[:, j, :], in0=sg[j % 2][:], in1=ps[pu][:], op=ALU.mult),
                         [sgb[j % 2], psb[pu]], [actb[j]])
                if c + 1 < self.NCH:
                    self.emit_norm(xs[(c + 1) % 2], xsb[(c + 1) % 2], hT, hb, gi, sq, sqb, rstd, rstdb, ps[6], psb[6])
                for m in range(KC):
                    po = 4 + m % 2
                    for j in range(NJ):
                        S.op("pe", lambda m=m, j=j, po=po: nc.tensor.matmul(ps[po][:], wout[:, j, m * 128:(m + 1) * 128], act[:, j, :], start=(j == 0), stop=(j == NJ - 1)),
                             [woutg[m // 4], actb[j]], [psb[po]], acc=True)
                    S.op("dve", lambda m=m, po=po: nc.vector.scalar_tensor_tensor(out=xt[:, m, :], in0=ps[po][:], scalar=0.5, in1=xt[:, m, :], op0=ALU.mult, op1=ALU.add),
                         [psb[po], xb[m]], [xb[m]])
                if final:
                    g12 = 12
                    for k in range(KC):
                        S.op("pool", lambda k=k: nc.gpsimd.tensor_tensor(out=sq[:, k % 2, :], in0=xt[:, k, :], in1=xt[:, k, :], op=ALU.mult), [xb[k]], [sqb[k % 2]])
                        S.op("pe", lambda k=k: nc.tensor.matmul(ps[6][:], self.ones_bf[:], sq[:, k % 2, :], start=(k == 0), stop=(k == KC - 1)),
                             [sqb[k % 2], self.gconst], [psb[6]], acc=True)
                    S.op("act", lambda: nc.scalar.activation(out=rstd[:], in_=ps[6][:], func=AF.Ln, scale=1.0 / D, bias=EPS), [psb[6]], [rstdb])
                    S.op("act", lambda: nc.scalar.activation(out=rstd[:], in_=rstd[:], func=AF.Exp, scale=-0.5), [rstdb], [rstdb])
                    for k in range(KC):
                        S.op("dve", lambda k=k: nc.vector.scalar_tensor_tensor(out=xt[:, k, :], in0=xt[:, k, :], scalar=self.gains_sb[:, g12 * KC + k:g12 * KC + k + 1],
                                                                                 in1=rstd[:], op0=ALU.mult, op1=ALU.mult), [xb[k], rstdb, self.gconst], [xb[k]])
                    S.dma("sp", [(self.xview(self.outT, c), xt[:])], xb, [], xb[0])
                else:
                    S.dma("sp", [(self.xview(self.xr, c), xt[:])], xb, [], xb[0])
            S.barrier()

    def phase_final(self, src):
        nc, S = self.nc, self.S
        gi = 12
        with ExitStack() as st:
            xs = [self.sb(st, "x%d" % i, [128, KC, CH], F32) for i in range(2)]
            xsb = [[Buf("x%d_%d" % (i, k)) for k in range(KC)] for i in range(2)]
            sq = self.sb(st, "sq", [128, 2, CH], BF16)
            sqb = [Buf("sq%d" % k) for k in range(2)]
            rstd = self.sb(st, "rstd", [128, CH], F32)
            rstdb = Buf("rstd")
            ps, psb = self.ps, self.psb
            for c in range(self.NCH):
                s = c % 2
                xt, xb = xs[s], xsb[s]
                S.dma("sp", [(xt[:], self.xview(src, c))], [], xb, xb[0])
                for k in range(KC):
                    S.op("pool", lambda k=k: nc.gpsimd.tensor_tensor(out=sq[:, k % 2, :], in0=xt[:, k, :], in1=xt[:, k, :], op=ALU.mult), [xb[k]], [sqb[k % 2]])
                    S.op("pe", lambda k=k: nc.tensor.matmul(ps[6][:], self.ones_bf[:], sq[:, k % 2, :], start=(k == 0), stop=(k == KC - 1)),
                         [sqb[k % 2], self.gconst], [psb[6]], acc=True)
                S.op("act", lambda: nc.scalar.activation(out=rstd[:], in_=ps[6][:], func=AF.Ln, scale=1.0 / D, bias=EPS), [psb[6]], [rstdb])
                S.op("act", lambda: nc.scalar.activation(out=rstd[:], in_=rstd[:], func=AF.Exp, scale=-0.5), [rstdb], [rstdb])
                for k in range(KC):
                    S.op("dve", lambda k=k: nc.vector.scalar_tensor_tensor(out=xt[:, k, :], in0=xt[:, k, :], scalar=self.gains_sb[:, gi * KC + k:gi * KC + k + 1],
                                                                             in1=rstd[:], op0=ALU.mult, op1=ALU.mult), [xb[k], rstdb, self.gconst], [xb[k]])
                S.dma("sp", [(self.xview(self.outT, c), xt[:])], xb, [], xb[0])
            S.barrier()

    def phase_mix_out(self, src, w_out):
        nc, S = self.nc, self.S
        with ExitStack() as st:
            wo = self.sb(st, "wo", [128, KC, D], BF16)
            wob = Buf("wo")
            xs = [self.sb(st, "x%d" % i, [128, KC, CH], F32) for i in range(2)]
            xsb = [[Buf("x%d_%d" % (i, k)) for k in range(KC)] for i in range(2)]
            ms = [self.sb(st, "m%d" % i, [128, KC, CH], BF16) for i in range(2)]
            msb = [Buf("m%d" % i) for i in range(2)]
            wog = self.load_weight_groups(wo, w_out, KC, [[(0, 256)], [(256, 1024)]])
            ps, psb = self.ps, self.psb
            def ld(c):
                s_ = c % 2
                S.dma("sp", [(xs[s_][:], self.xview(src, c))], [], xsb[s_], xsb[s_][0])
                S.dma("sp", [(ms[s_][:], self.xview(self.mixT, c))], [], [msb[s_]], msb[s_])

            ld(0)
            for c in range(self.NCH):
                s = c % 2
                xt, xb = xs[s], xsb[s]
                if c + 1 < self.NCH:
                    ld(c + 1)
                for m in range(KC):
                    po = m % 4
                    for k in range(KC):
                        S.op("pe", lambda m=m, k=k, po=po: nc.tensor.matmul(ps[po][:], wo[:, k, m * 128:(m + 1) * 128], ms[s][:, k, :], start=(k == 0), stop=(k == KC - 1)),
                             [wog[0 if m < 2 else 1], msb[s]], [psb[po]], acc=True)
                    S.op("dve", lambda m=m, po=po: nc.vector.tensor_tensor(out=xt[:, m, :], in0=ps[po][:], in1=xt[:, m, :], op=ALU.add),
                         [psb[po], xb[m]], [xb[m]])
                S.dma("sp", [(self.xview(self.xr, c), xt[:])], xb, [], xb[0])
            S.barrier()

    def phase_sb_qkv(self, src, l):
        nc, S = self.nc, self.S
        gi = 4 + l
        T = self.T
        with ExitStack() as st:
            wq = self.sb(st, "wq", [128, KC, 3 * D], BF16)
            wqb = Buf("wq")
            wqg = self.load_weight_groups(wq, self.w_qkv_sb[l // 2], KC, [[(0, 512)], [(512, D)], [(D, 2 * D)], [(2 * D, 3 * D)]])
            xs = [self.sb(st, "x%d" % i, [128, KC, CH], F32) for i in range(2)]
            xsb = [[Buf("x%d_%d" % (i, k)) for k in range(KC)] for i in range(2)]
            hT = self.sb(st, "hT", [128, KC, CH], BF16)
            hb = [Buf("h%d" % k) for k in range(KC)]
            sq = self.sb(st, "sq", [128, 2, CH], BF16)
            sqb = [Buf("sq%d" % k) for k in range(2)]
            rstd = self.sb(st, "rstd", [128, CH], F32)
            rstdb = Buf("rstd")
            qo = [self.sb(st, "qo%d" % i, [128, KC, CH], BF16) for i in range(2)]
            qob = [Buf("qo%d" % i) for i in range(2)]
            vo = [self.sb(st, "vo%d" % i, [128, 4, D], BF16) for i in range(2)]
            vob = [Buf("vo%d" % i) for i in range(2)]
            ps, psb = self.ps, self.psb
            it = 0

            def ld(c):
                S.dma("sp", [(xs[c % 2][:], self.xview(src, c))], [], xsb[c % 2], xsb[c % 2][0])

            ld(0)
            for c in range(self.NCH):
                s = c % 2
                xt, xb = xs[s], xsb[s]
                if c + 1 < self.NCH:
                    ld(c + 1)
                self.emit_norm(xt, xb, hT, hb, gi, sq, sqb, rstd, rstdb, ps[6], psb[6])
                for qk in range(2):
                    stg, stgb = qo[qk], qob[qk]
                    for m in range(KC):
                        po = it % 4
                        it += 1
                        f0 = qk * D + m * 128
                        for k in range(KC):
                            S.op("pe", lambda k=k, f0=f0, po=po: nc.tensor.matmul(ps[po][:], wq[:, k, f0:f0 + 128], hT[:, k, :], start=(k == 0), stop=(k == KC - 1)),
                                 [wqg[0 if f0 < 512 else (1 if f0 < D else 2)], hb[k]], [psb[po]], acc=True)
                        S.op("act", lambda m=m, po=po, stg=stg, qk=qk: nc.scalar.activation(out=stg[:, m, :], in_=ps[po][:], func=AF.Copy, scale=(0.125 if qk == 0 else 1.0)),
                             [psb[po]], [stgb])
                    dst = self.sb_q if qk == 0 else self.sb_k
                    S.dma("sp", [(self.xview(dst, c), stg[:])], [stgb], [], stgb)
                vs, vsb = vo[c % 2], vob[c % 2]
                for sub in range(4):
                    for half in range(2):
                        po = it % 4
                        it += 1
                        f0 = 2 * D + half * 512
                        for k in range(KC):
                            S.op("pe", lambda k=k, f0=f0, po=po, sub=sub: nc.tensor.matmul(ps[po][:], hT[:, k, sub * 128:(sub + 1) * 128], wq[:, k, f0:f0 + 512], start=(k == 0), stop=(k == KC - 1)),
                                 [wqg[3], hb[k]], [psb[po]], acc=True)
                        S.op("dve", lambda po=po, sub=sub, half=half, vs=vs: nc.vector.tensor_copy(out=vs[:, sub, half * 512:(half + 1) * 512], in_=ps[po][:]),
                             [psb[po]], [vsb])
                S.dma("sp", [(self.sb_v[c * CH:(c + 1) * CH, :].rearrange("(s p) f -> p s f", p=128), vs[:])], [vsb], [], vsb)
            S.barrier()

    def phase_sb_att(self, l):
        nc, S = self.nc, self.S
        T = self.T
        NB = T // 128
        with ExitStack() as st:
            kT = [self.sb(st, "kT%d" % i, [128, T], BF16) for i in range(2)]
            qT = [self.sb(st, "qT%d" % i, [128, T], BF16) for i in range(2)]
            vv = [self.sb(st, "vv%d" % i, [128, NB, 128], BF16) for i in range(2)]
            inb = [Buf("in%d" % i) for i in range(2)]
            oT = [self.sb(st, "oT%d" % i, [128, T], BF16) for i in range(2)]
            oTb = [Buf("oT%d" % i) for i in range(2)]
            e32 = [self.sb(st, "e32_%d" % i, [128, 2 * CH], F32) for i in range(2)]
            e32b = [Buf("e32_%d" % i) for i in range(2)]
            Lb = [self.sb(st, "Lb%d" % i, [128, 2 * CH], BF16) for i in range(4)]
            Lbb = [Buf("Lb%d" % i) for i in range(4)]
            Sb = [self.sb(st, "Sb%d" % i, [128, 2 * CH], BF16) for i in range(3)]
            Sbb = [Buf("Sb%d" % i) for i in range(3)]
            Ab = [self.sb(st, "Ab%d" % i, [128, 2 * CH], BF16) for i in range(3)]
            Abb = [Buf("Ab%d" % i) for i in range(3)]
            ps, psb = self.ps, self.psb
            tri = self.tri_bf
            it = 0

            def load_pair(p):
                s = p % 2
                rows = slice(p * 128, (p + 1) * 128)
                S.dma("sp", [(kT[s][:], self.sb_k[rows, :]), (qT[s][:], self.sb_q[rows, :]),
                             (vv[s][:], self.sb_v[:, rows].rearrange("(j p) f -> p j f", p=128))], [], [inb[s]], inb[s])

            tiles = []
            for p in range(8):
                for c in range(self.NCH):
                    jd = 4 * c + 3
                    for j in range(jd, -1, -1):
                        tiles.append(dict(p=p, s=p % 2, c=c, j=j, first=(j == jd), last=(j == 0), diag=(j >= 4 * c),
                                          pair_last=(c == self.NCH - 1 and j == 0)))
            psw = self.psw
            pswb = [Buf("psw%d" % i) for i in range(3)]
            def v3(tile_ap, c0):
                return tile_ap.rearrange("p (h c) -> p h c", h=2)[:, :, c0:CH]

            def hs(hh, c0):
                return slice(hh * CH + c0, (hh + 1) * CH)

            def stage_a(t, tl):
                s, c, j = tl["s"], tl["c"], tl["j"]
                Q0 = c * CH
                c0 = 128 * (j - 4 * c) if tl["diag"] else 0
                w = CH - c0
                wi, ei, li = t % 3, t % 2, t % 4
                ks = slice(j * 128, (j + 1) * 128)
                qs = slice(Q0 + c0, Q0 + CH)
                for hh in range(2):
                    pr = slice(64 * hh, 64 * hh + 64)
                    S.op("pe", lambda: nc.tensor.matmul(psw[wi][:, hs(hh, c0)], kT[s][pr, ks], qT[s][pr, qs], start=True, stop=True),
                         [inb[s]], [pswb[wi]], acc=(hh == 1))
                S.op("act", lambda: nc.scalar.activation(out=v3(e32[ei][:], c0), in_=v3(psw[wi][:], c0), func=AF.Exp), [pswb[wi]], [e32b[ei]])
                S.op("act", lambda: nc.scalar.activation(out=v3(Lb[li][:], c0), in_=v3(e32[ei][:], c0), func=AF.Ln, bias=1.0), [e32b[ei]], [Lbb[li]])
                if tl["diag"]:
                    S.op("pool", lambda: nc.gpsimd.affine_select(out=v3(Lb[li][:], c0), in_=v3(Lb[li][:], c0), pattern=[[0, 2], [1, w]], compare_op=ALU.is_gt,
                                                                 fill=0.0, base=0, channel_multiplier=-1), [Lbb[li]], [Lbb[li]])
                if not tl["last"]:
                    so, sn = t % 3, (t + 1) % 3
                    if tl["first"]:
                        S.op("dve", lambda: nc.vector.tensor_copy(out=v3(Sb[sn][:], c0), in_=v3(Lb[li][:], c0)), [Lbb[li]], [Sbb[sn]])
                    elif tl["diag"]:
                        S.op("dve", lambda: nc.vector.tensor_copy(out=v3(Sb[sn][:], c0)[:, :, 0:128], in_=v3(Lb[li][:], c0)[:, :, 0:128]), [Lbb[li]], [Sbb[sn]])
                        S.op("dve", lambda: nc.vector.tensor_tensor(out=v3(Sb[sn][:], c0 + 128), in0=v3(Sb[so][:], c0 + 128), in1=v3(Lb[li][:], c0 + 128), op=ALU.add),
                             [Lbb[li], Sbb[so], Sbb[sn]], [Sbb[sn]])
                    else:
                        S.op("dve", lambda: nc.vector.tensor_tensor(out=Sb[sn][:], in0=Sb[so][:], in1=Lb[li][:], op=ALU.add), [Lbb[li], Sbb[so]], [Sbb[sn]])

            def stage_b(t, tl):
                c, j = tl["c"], tl["j"]
                c0 = 128 * (j - 4 * c) if tl["diag"] else 0
                w = CH - c0
                wi, li, ai, so = t % 3, t % 4, t % 3, t % 3
                first = tl["first"]
                for hh in range(2):
                    S.op("pe", lambda: nc.tensor.matmul(psw[wi][:, hs(hh, c0)], tri[:, 0:128], Lb[li][:, hs(hh, c0)], start=False, stop=first, skip_group_check=True),
                         [Lbb[li], self.gconst], [pswb[wi]], acc=True)
                    if not first:
                        c1 = c0 + 128 if tl["diag"] else 0
                        S.op("pe", lambda: nc.tensor.matmul(psw[wi][:, hs(hh, c1)], tri[:, 128:256], Sb[so][:, hs(hh, c1)], start=False, stop=True, skip_group_check=True),
                             [Sbb[so], self.gconst], [pswb[wi]], acc=True)
                S.op("act", lambda: nc.scalar.activation(out=v3(Ab[ai][:], c0), in_=v3(psw[wi][:], c0), func=AF.Exp), [pswb[wi]], [Abb[ai]])
                if tl["diag"]:
                    S.op("pool", lambda: nc.gpsimd.affine_select(out=v3(Ab[ai][:], c0), in_=v3(Ab[ai][:], c0), pattern=[[0, 2], [1, w]], compare_op=ALU.is_gt,
                                                                 fill=0.0, base=0, channel_multiplier=-1), [Abb[ai]], [Abb[ai]])

            def stage_c(t, tl):
                s, c, j, p = tl["s"], tl["c"], tl["j"], tl["p"]
                Q0 = c * CH
                c0 = 128 * (j - 4 * c) if tl["diag"] else 0
                cq = slice(c0, CH)
                ai = t % 3
                for hh in range(2):
                    pr = slice(64 * hh, 64 * hh + 64)
                    S.op("pe", lambda: nc.tensor.matmul(ps[6][pr, cq], vv[s][:, j, pr], Ab[ai][:, hs(hh, c0)], start=tl["first"], stop=tl["last"], skip_group_check=True),
                         [inb[s], Abb[ai]], [psb[6]], acc=True)
                if tl["last"]:
                    S.op("dve", lambda: nc.vector.tensor_copy(out=oT[s][:, Q0:Q0 + CH], in_=ps[6][:, :]), [psb[6]], [oTb[s]])
                if tl["pair_last"]:
                    S.dma("sp", [(self.mixT[p * 128:(p + 1) * 128, :], oT[s][:])], [oTb[s]], [], oTb[s])
                    if p + 2 < 8:
                        load_pair(p + 2)

            load_pair(0)
            load_pair(1)
            n = len(tiles)
            for t in range(n + 2):
                if t < n:
                    stage_a(t, tiles[t])
                if 0 <= t - 1 < n:
                    stage_b(t - 1, tiles[t - 1])
                if 0 <= t - 2 < n:
                    stage_c(t - 2, tiles[t - 2])
            S.barrier()

    def phase_nsa_proj(self, src, l):
        nc, S = self.nc, self.S
        T = self.T
        e = l // 2
        gi = 4 + l
        NCB, NCP, CW, NBLK = self.NCB, self.NCP, self.CW, self.NBLK
        ps, psb = self.ps, self.psb
        with ExitStack() as st:
            wab = self.sb(st, "wab", [128, KC, AB_IN], BF16)
            wabb = Buf("wab")
            wabg = self.load_weight_groups(wab, self.w_in_ab[e], KC, [[(512, 1536)], [(0, 512)], [(1536, 2304)], [(2304, AB_IN)]])

            def wab_buf(f0):
                return wabg[0] if 512 <= f0 < 1536 else (wabg[1] if f0 < 512 else (wabg[2] if f0 < 2304 else wabg[3]))
            cw = self.sb(st, "cw", [128, 12], F32)
            cwb = Buf("cw")
            S.dma("sp", [(cw[:], self.convw[:, e * 12:(e + 1) * 12])], [], [cwb], cwb)
            xs = [self.sb(st, "x%d" % i, [128, KC, CH], F32) for i in range(2)]
            xsb = [[Buf("x%d_%d" % (i, k)) for k in range(KC)] for i in range(2)]
            hT = self.sb(st, "hT", [128, KC, CH], BF16)
            hb = [Buf("h%d" % k) for k in range(KC)]
            sq = self.sb(st, "sq", [128, 2, CH], BF16)
            sqb = [Buf("sq%d" % k) for k in range(2)]
            rstd = self.sb(st, "rstd", [128, CH], F32)
            rstdb = Buf("rstd")
            cT = [self.sb(st, "cT%d" % i, [128, T], BF16) for i in range(2)]
            cTb = [Buf("cT%d" % i) for i in range(2)]
            u = self.sb(st, "u", [128, 4, CH + 2], F32)
            ub = [Buf("u%d" % i) for i in range(4)]
            cS = [self.sb(st, "cS%d" % i, [128, CH], F32) for i in range(2)]
            cSb = [Buf("cS%d" % i) for i in range(2)]
            t1 = [self.sb(st, "t1_%d" % i, [128, CH], F32) for i in range(2)]
            t1b = [Buf("t1_%d" % i) for i in range(2)]
            ycv = [self.sb(st, "ycv%d" % i, [128, 4, CH], BF16) for i in range(2)]
            ycvb = [Buf("ycv%d" % i) for i in range(2)]
            qst = [self.sb(st, "qst%d" % i, [64, 8, CH], BF16) for i in range(2)]
            qstb = [Buf("qst%d" % i) for i in range(2)]
            kst = [self.sb(st, "kst%d" % i, [64, 4, CH], BF16) for i in range(2)]
            kstb = [Buf("kst%d" % i) for i in range(2)]
            vsl = [self.sb(st, "vsl%d" % i, [128, 2, 4, 66], BF16) for i in range(2)]
            vwn = [self.sb(st, "vwn%d" % i, [128, 2, 4, 66], BF16) for i in range(2)]
            gst = [self.sb(st, "gst%d" % i, [128, 4, 24], F32) for i in range(2)]
            tkb = [Buf("tk%d" % i) for i in range(2)]
            for i in range(2):
                S.op("pool", lambda i=i: nc.gpsimd.memset(vsl[i][:], 1.0), [], [tkb[i]])
                S.op("pool", lambda i=i: nc.gpsimd.memset(vwn[i][:], 1.0), [], [tkb[i]])
            S.op("pool", lambda: nc.gpsimd.memset(u[:], 0.0), [], ub)
            it = 0

            def proj(f0, M, rows=None):
                nonlocal it
                po = it % 6
                it += 1
                for k in range(KC):
                    S.op("pe", lambda k=k, po=po: nc.tensor.matmul(ps[po][0:M, :], wab[:, k, f0:f0 + M], hT[:, k, :], start=(k == 0), stop=(k == KC - 1)),
                         [wab_buf(f0), hb[k]], [psb[po]], acc=True)
                return po

            def ld(c):
                S.dma("sp", [(xs[c % 2][:], self.xview(src, c))], [], xsb[c % 2], xsb[c % 2][0])

            ld(0)
            for c in range(self.NCH):
                s = c % 2
                cs = slice(c * CH, (c + 1) * CH)
                xt, xb = xs[s], xsb[s]
                if c + 1 < self.NCH:
                    ld(c + 1)
                self.emit_norm(xt, xb, hT, hb, gi, sq, sqb, rstd, rstdb, ps[6], psb[6])
                for i in range(4 if NSTOP >= 2 else 0):
                    pc = proj(512 + 128 * i, 128)
                    S.op("act", lambda pc=pc, i=i: nc.scalar.activation(out=cS[i % 2][:], in_=ps[pc][:], func=AF.Copy), [psb[pc]], [cSb[i % 2]])
                    ph = proj(1024 + 128 * i, 128)
                    S.op("dve", lambda ph=ph, i=i: nc.vector.tensor_tensor(out=u[:, i, 2:CH + 2], in0=cS[i % 2][:], in1=ps[ph][:], op=ALU.mult),
                         [cSb[i % 2], psb[ph]], [ub[i]])
                    pb_ = proj(128 * i, 128)
                    tt, ttb = t1[i % 2], t1b[i % 2]
                    S.op("dve", lambda i=i, tt=tt: nc.vector.tensor_scalar(out=tt[:], in0=u[:, i, 0:CH], scalar1=cw[:, i * 3:i * 3 + 1], scalar2=None, op0=ALU.mult),
                         [ub[i], cwb], [ttb])
                    S.op("dve", lambda i=i, tt=tt: nc.vector.scalar_tensor_tensor(out=tt[:], in0=u[:, i, 1:CH + 1], scalar=cw[:, i * 3 + 1:i * 3 + 2], in1=tt[:], op0=ALU.mult, op1=ALU.add),
                         [ub[i], cwb, ttb], [ttb])
                    S.op("dve", lambda i=i, tt=tt: nc.vector.scalar_tensor_tensor(out=tt[:], in0=u[:, i, 2:CH + 2], scalar=cw[:, i * 3 + 2:i * 3 + 3], in1=tt[:], op0=ALU.mult, op1=ALU.add),
                         [ub[i], cwb, ttb], [ttb])
                    S.op("dve", lambda i=i, tt=tt, pb_=pb_: nc.vector.tensor_tensor(out=ycv[s][:, i, :], in0=tt[:], in1=ps[pb_][:], op=ALU.mult),
                         [ttb, psb[pb_]], [ycvb[s]])
                    S.op("dve", lambda i=i: nc.vector.tensor_copy(out=u[:, i, 0:2], in_=u[:, i, CH:CH + 2]), [ub[i]], [ub[i]])
                if NSTOP >= 2:
                    S.dma("sp", [(self.mixT[0:512, cs].rearrange("(i p) t -> p i t", p=128), ycv[s][:])], [ycvb[s]], [], ycvb[s])
                if NSTOP < 3:
                    continue
                for hq in range(8):
                    pq = proj(1536 + 64 * hq, 64)
                    S.op("act", lambda pq=pq, hq=hq: nc.scalar.activation(out=qst[s][:, hq, :], in_=ps[pq][0:64, :], func=AF.Copy, scale=0.125), [psb[pq]], [qstb[s]])
                S.dma("sp", [(self.n_q.rearrange("h d t -> d h t")[:, :, cs], qst[s][:])], [qstb[s]], [], qstb[s])
                if NSTOP < 4:
                    continue
                for kv in range(2):
                    pk = proj(2048 + 128 * kv, 128)
                    S.op("dve", lambda pk=pk, kv=kv: nc.vector.tensor_copy(out=cT[kv][:, cs], in_=ps[pk][:]), [psb[pk]], [cTb[kv]])
                for idx, f0 in enumerate((2304, 2368, 2560, 2624)):
                    pk = proj(f0, 64)
                    S.op("act", lambda pk=pk, idx=idx: nc.scalar.activation(out=kst[s][:, idx, :], in_=ps[pk][0:64, :], func=AF.Copy), [psb[pk]], [kstb[s]])
                S.dma("sp", [(self.n_ksel.rearrange("g d t -> d g t")[:, :, cs], kst[s][:, 0:2, :]),
                             (self.n_kwin.rearrange("g d t -> d g t")[:, :, cs], kst[s][:, 2:4, :])], [kstb[s]], [], kstb[s])
                if NSTOP < 5:
                    continue
                for sub in range(4):
                    po = it % 6
                    it += 1
                    for k in range(KC):
                        S.op("pe", lambda k=k, po=po, sub=sub: nc.tensor.matmul(ps[po][:, 0:408], hT[:, k, sub * 128:(sub + 1) * 128], wab[:, k, 2432:2840], start=(k == 0), stop=(k == KC - 1)),
                             [wabg[3], hb[k]], [psb[po]], acc=True)
                    S.op("dve", lambda po=po, sub=sub: nc.vector.tensor_copy(out=vsl[s][:, :, sub, 0:64], in_=ps[po][:, 0:128].rearrange("p (g d) -> p g d", g=2)), [psb[po]], [tkb[s]])
                    S.op("dve", lambda po=po, sub=sub: nc.vector.tensor_copy(out=vwn[s][:, :, sub, 0:64], in_=ps[po][:, 256:384].rearrange("p (g d) -> p g d", g=2)), [psb[po]], [tkb[s]])
                    S.op("act", lambda po=po, sub=sub: nc.scalar.activation(out=gst[s][:, sub, :], in_=ps[po][:, 384:408], func=AF.Sigmoid), [psb[po]], [tkb[s]])
                S.dma("sp", [(self.n_vsel[:, :, 4 * c:4 * c + 4, :].rearrange("g p s e -> p g s e"), vsl[s][:]),
                             (self.n_vwin[:, :, 4 * c:4 * c + 4, :].rearrange("g p s e -> p g s e"), vwn[s][:]),
                             (self.n_gates[:, 4 * c:4 * c + 4, :], gst[s][:])], [tkb[s]], [], tkb[s])

            if NSTOP < 6:
                S.barrier()
                return
            w1 = [self.sb(st, "w1_%d" % i, [128, 32, 128], BF16) for i in range(2)]
            w2 = [self.sb(st, "w2_%d" % i, [128, 64], BF16) for i in range(2)]
            peT = [self.sb(st, "peT%d" % i, [64, 32], BF16) for i in range(2)]
            cwtb = Buf("cmpw")
            pairs = []
            for kv in range(2):
                v = self.cmp_w1[kv][e].rearrange("(l d) h -> d l h", d=64)
                pairs += [(w1[kv][0:64], v), (w1[kv][64:128], v), (w2[kv][:], self.cmp_w2[kv][e]), (peT[kv][:], self.cmp_peT[kv][e])]
            S.dma("pool", pairs, [], [cwtb], cwtb)
            cvec = self.sb(st, "cvec", [128, 2], F32)
            cvecb = Buf("cvec")
            xh = self.sb(st, "xh", [128, NCP], F32)
            x2 = self.sb(st, "x2", [128, NCP], F32)
            sgm = self.sb(st, "sgm", [128, NCP], F32)
            hid = self.sb(st, "hid", [128, NCP], BF16)
            xhb, x2b, sgmb, hidb = Buf("xh"), Buf("x2"), Buf("sgm"), Buf("hid")
            kcs = self.sb(st, "kcs", [64, NCP], BF16)
            kcsb = Buf("kcs")
            vcs = self.sb(st, "vcs", [CW, NBLK, 130], BF16)
            vcsb = Buf("vcs")
            S.op("pool", lambda: nc.gpsimd.memset(hid[:], 0.0), [], [hidb])
            for kv in range(2):
                pv = it % 6
                it += 1
                for l_ in range(32):
                    S.op("pe", lambda l_=l_, pv=pv, kv=kv: nc.tensor.matmul(ps[pv][:, 0:1], w1[kv][0:64, l_, :], peT[kv][0:64, l_:l_ + 1], start=(l_ == 0), stop=(l_ == 31)),
                         [cwtb], [psb[pv]], acc=True)
                S.op("dve", lambda pv=pv, kv=kv: nc.vector.tensor_copy(out=cvec[:, kv:kv + 1], in_=ps[pv][:, 0:1]), [psb[pv]], [cvecb])
            for kv in range(2 if NSTOP >= 7 else 0):
                for g in range(2):
                    gr = slice(64 * g, 64 * g + 64)
                    ph = it % 6
                    it += 1
                    srcv = cT[kv][gr, :].rearrange("p (c s) -> p c s", s=16)
                    for l_ in range(32):
                        S.op("pe", lambda l_=l_, ph=ph, kv=kv, gr=gr, srcv=srcv: nc.tensor.matmul(ps[ph][:, 0:NCB], w1[kv][gr, l_, :], srcv[:, l_ // 16:l_ // 16 + NCB, l_ % 16],
                                                                                                   start=(l_ == 0), stop=(l_ == 31)),
                             [cwtb, cTb[kv]], [psb[ph]], acc=True)
                    S.op("act", lambda ph=ph, kv=kv: nc.scalar.activation(out=xh[:, 0:NCB], in_=ps[ph][:, 0:NCB], func=AF.Identity, bias=cvec[:, kv:kv + 1]), [psb[ph], cvecb], [xhb])
                    S.op("dve", lambda: nc.vector.tensor_tensor(out=x2[:, 0:NCB], in0=xh[:, 0:NCB], in1=xh[:, 0:NCB], op=ALU.mult), [xhb], [x2b])
                    S.op("dve", lambda: nc.vector.tensor_scalar(out=x2[:, 0:NCB], in0=x2[:, 0:NCB], scalar1=0.044715, scalar2=1.0, op0=ALU.mult, op1=ALU.add), [x2b], [x2b])
                    S.op("dve", lambda: nc.vector.tensor_tensor(out=x2[:, 0:NCB], in0=x2[:, 0:NCB], in1=xh[:, 0:NCB], op=ALU.mult), [x2b, xhb], [x2b])
                    S.op("act", lambda: nc.scalar.activation(out=sgm[:, 0:NCB], in_=x2[:, 0:NCB], func=AF.Sigmoid, scale=1.5957691216057308), [x2b], [sgmb])
                    S.op("dve", lambda: nc.vector.tensor_tensor(out=hid[:, 0:NCB], in0=xh[:, 0:NCB], in1=sgm[:, 0:NCB], op=ALU.mult), [xhb, sgmb], [hidb])
                    if NSTOP < 8:
                        continue
                    if kv == 1 and NSTOP < 9:
                        continue
                    if kv == 0:
                        pk = it % 6
                        it += 1
                        S.op("pe", lambda pk=pk: nc.tensor.matmul(ps[pk][0:64, 0:NCP], w2[0][:], hid[:], start=True, stop=True), [cwtb, hidb], [psb[pk]])
                        S.op("dve", lambda pk=pk: nc.vector.tensor_copy(out=kcs[:], in_=ps[pk][0:64, 0:NCP]), [psb[pk]], [kcsb])
                        S.dma("sp", [(self.n_kc[g], kcs[:])], [kcsb], [], kcsb)
                    else:
                        S.dma("pool", [(vcs[:, b, 0:64], self.c_overlap[b * 128:b * 128 + CW, :]) for b in range(NBLK)], [], [vcsb], vcsb)
                        S.op("pool", lambda: nc.gpsimd.memset(vcs[:, :, 128:130], 1.0), [], [vcsb])
                        for b in range(NBLK):
                            pk = it % 6
                            it += 1
                            S.op("pe", lambda pk=pk, b=b: nc.tensor.matmul(ps[pk][0:CW, 0:64], hid[:, b * 128:b * 128 + CW], w2[1][:], start=True, stop=True), [cwtb, hidb], [psb[pk]])
                            S.op("dve", lambda pk=pk, b=b: nc.vector.tensor_copy(out=vcs[:, b, 64:128], in_=ps[pk][0:CW, 0:64]), [psb[pk]], [vcsb])
                        S.dma("sp", [(self.n_vc[g], vcs[:])], [vcsb], [], vcsb)
            S.barrier()

    def phase_nsa_att(self, l):
        nc, S = self.nc, self.S
        T = self.T
        NB = T // 128
        NCB, NCP, CW, NBLK = self.NCB, self.NCP, self.CW, self.NBLK
        ps, psb = self.ps, self.psb
        pst = self.pst
        with ExitStack() as st:
            KE = self.sb(st, "KE", [128, T], BF16)
            keb_k, keb_e = Buf("ke_k"), Buf("ke_e")
            kwn = self.sb(st, "kwn", [64, T], BF16)
            kcs = self.sb(st, "kcs", [64, NCP], BF16)
            vcs = self.sb(st, "vcs", [CW, NBLK, 130], BF16)
            vsl = self.sb(st, "vsl", [128, NB, 66], BF16)
            vwn = self.sb(st, "vwn", [128, NB, 66], BF16)
            QM = self.sb(st, "QM", [128, 4, T], BF16)
            gin = Buf("gin")
            qm_m = [Buf("qm_m%d" % c) for c in range(self.NCH)]
            gts = [self.sb(st, "gts%d" % i, [128, 4, 24], F32) for i in range(2)]
            slb = [self.sb(st, "slb%d" % i, [128, 4, 64], F32) for i in range(2)]
            gtb = [Buf("gt%d" % i) for i in range(2)]
            P = [self.sb(st, "P%d" % i, [128, CH], BF16) for i in range(5)]
            Pb = [Buf("P%d" % i) for i in range(5)]
            yacc = self.sb(st, "yacc", [128, 4, 256], F32)
            yaccb = [Buf("yacc%d" % i) for i in range(4)]
            ybf = self.sb(st, "ybf", [128, 4, 256], BF16)
            ybfb = Buf("ybf")
            imp = self.sb(st, "imp", [128, 4, 64], F32)
            impb = [Buf("imp%d" % i) for i in range(4)]
            sm = self.sb(st, "sm", [128, 8], F32)
            smb = Buf("sm")
            sc = self.sb(st, "sc", [128, 64], F32)
            wk = self.sb(st, "wk", [128, 64], F32)
            wk2 = self.sb(st, "wk2", [128, 64], F32)
            m8 = self.sb(st, "m8", [128, 16], F32)
            tkb = Buf("topk")
            mk = self.sb(st, "mk", [128, 128], BF16)
            mkb = Buf("mk")
            yT = [self.sb(st, "yT%d" % i, [128, 2, CH], BF16) for i in range(2)]
            yTb = [Buf("yT%d" % i) for i in range(2)]
            pstb = [self.pstb] * 8
            S.dma("pool", [(KE[64:128, :], self.c_etab[:, :])], [], [keb_e], keb_e)
            S.op("pool", lambda: nc.gpsimd.memset(mk[:], 0.0), [], [mkb])
            it = 0
            pit = 0
            tit = 0

            def evac(acc_ap, z_ap, gate_ap, out_ap, first, accb, outb, gb):
                S.op("dve", lambda: nc.vector.tensor_scalar_max(out=sm[:, 0:1], in0=z_ap, scalar1=1e-30), [accb], [smb])
                S.op("dve", lambda: nc.vector.reciprocal(out=sm[:, 1:2], in_=sm[:, 0:1]), [smb], [smb])
                S.op("dve", lambda: nc.vector.tensor_tensor(out=sm[:, 2:3], in0=sm[:, 1:2], in1=gate_ap, op=ALU.mult), [smb, gb], [smb])
                if first:
                    S.op("dve", lambda: nc.vector.tensor_scalar(out=out_ap, in0=acc_ap, scalar1=sm[:, 2:3], scalar2=None, op0=ALU.mult), [accb, smb], [outb])
                else:
                    S.op("dve", lambda: nc.vector.scalar_tensor_tensor(out=out_ap, in0=acc_ap, scalar=sm[:, 2:3], in1=out_ap, op0=ALU.mult, op1=ALU.add), [accb, smb, outb], [outb])

            items = []

            def add_tile(s1, s2, pre=None, post=None, flush=False):
                items.append(dict(s1=s1, s2=s2, pre=pre or [], post=post or [], flush=flush))

            def load_g(g):
                S.dma("sp", [(KE[0:64, :], self.n_ksel[g]), (kwn[:], self.n_kwin[g]), (kcs[:], self.n_kc[g]), (vcs[:], self.n_vc[g]),
                             (vsl[:], self.n_vsel[g]), (vwn[:], self.n_vwin[g]),
                             (QM[0:64, :, :], self.n_q[4 * g:4 * g + 4].rearrange("h d t -> d h t"))], [], [gin, keb_k], gin)

            def load_gates(g, c):
                gs = (g * self.NCH + c) % 2
                S.dma("sp", [(gts[gs][:], self.n_gates[:, 4 * c:4 * c + 4, :]), (slb[gs][:], self.c_selbias[:, 4 * c:4 * c + 4, :])], [], [gtb[gs]], gtb[gs])

            def cmp_evac(g, c, r, banks):
                gs = (g * self.NCH + c) % 2
                hd = 4 * g + r
                for sub in range(4):
                    bk = banks[sub // 2]
                    c0 = (sub % 2) * 129
                    evac(ps[bk][:, c0 + 64:c0 + 128], ps[bk][:, c0 + 128:c0 + 129], gts[gs][:, sub, 3 * hd:3 * hd + 1], yacc[:, sub, r * 64:(r + 1) * 64],
                         True, psb[bk], yaccb[sub], gtb[gs])
                    if r == 0:
                        S.op("dve", lambda: nc.vector.tensor_scalar(out=imp[:, sub, :], in0=ps[bk][:, c0:c0 + 64], scalar1=sm[:, 1:2], scalar2=None, op0=ALU.mult),
                             [psb[bk], smb], [impb[sub]])
                    else:
                        S.op("dve", lambda: nc.vector.scalar_tensor_tensor(out=imp[:, sub, :], in0=ps[bk][:, c0:c0 + 64], scalar=sm[:, 1:2], in1=imp[:, sub, :],
                                                                             op0=ALU.mult, op1=ALU.add), [psb[bk], smb, impb[sub]], [impb[sub]])

            def topk(g, c):
                Q0 = c * CH
                gs = (g * self.NCH + c) % 2
                for sub in range(4):
                    S.op("dve", lambda: nc.vector.tensor_tensor(out=sc[:], in0=imp[:, sub, :], in1=slb[gs][:, sub, :], op=ALU.add), [impb[sub], gtb[gs]], [tkb])
                    S.op("dve", lambda: nc.vector.max(out=m8[:, 0:8], in_=sc[:]), [tkb], [tkb])
                    S.op("dve", lambda: nc.vector.match_replace(out=wk[:], in_to_replace=m8[:, 0:8], in_values=sc[:], imm_value=-3e38), [tkb], [tkb])
                    S.op("dve", lambda: nc.vector.max(out=m8[:, 8:16], in_=wk[:]), [tkb], [tkb])
                    S.op("dve", lambda: nc.vector.match_replace(out=wk2[:], in_to_replace=m8[:, 8:16], in_values=wk[:], imm_value=-3e38), [tkb], [tkb])
                    S.op("dve", lambda: nc.vector.tensor_tensor(out=wk[:], in0=sc[:], in1=wk2[:], op=ALU.subtract), [tkb], [tkb])
                    S.op("dve", lambda: nc.vector.tensor_scalar_min(out=mk[:, 64:128], in0=wk[:], scalar1=1.0), [tkb], [mkb])
                    S.op("pe", lambda: nc.tensor.transpose(out=pst[:, 0:128], in_=mk[:], identity=self.ident_bf[:]), [mkb, self.gconst], [self.pstb])
                    for r in range(4):
                        q0 = Q0 + sub * 128
                        if r % 2 == 0:
                            S.op("dve", lambda: nc.vector.tensor_scalar_add(out=QM[64:128, r, q0:q0 + 128], in0=pst[64:128, 0:128], scalar1=-1.0),
                                 [self.pstb], [qm_m[c]])
                        else:
                            S.op("act", lambda: nc.scalar.activation(out=QM[64:128, r, q0:q0 + 128], in_=pst[64:128, 0:128], func=AF.Identity, bias=-1.0),
                                 [self.pstb], [qm_m[c]])

            def branch_evac(g, c, r, br, bk):
                gs = (g * self.NCH + c) % 2
                hd = 4 * g + r
                gidx = 3 * hd + (2 if br == "win" else 1)
                for sub in range(4):
                    evac(ps[bk][:, sub * 65:sub * 65 + 64], ps[bk][:, sub * 65 + 64:sub * 65 + 65], gts[gs][:, sub, gidx:gidx + 1], yacc[:, sub, r * 64:(r + 1) * 64],
                         False, psb[bk], yaccb[sub], gtb[gs])

            def epilogue(g, c):
                cs = slice(c * CH, (c + 1) * CH)
                S.op("dve", lambda: nc.vector.tensor_copy(out=ybf[:], in_=yacc[:]), yaccb, [ybfb])
                ys = (g * self.NCH + c) % 2
                for sub in range(4):
                    for half in range(2):
                        S.op("pe", lambda: nc.tensor.transpose(out=pst[:, 128:256], in_=ybf[:, sub, half * 128:(half + 1) * 128], identity=self.ident_bf[:]),
                             [ybfb, self.gconst], [self.pstb])
                        S.op("act", lambda: nc.scalar.activation(out=yT[ys][:, half, sub * 128:(sub + 1) * 128], in_=pst[:, 128:256], func=AF.Copy),
                             [self.pstb], [yTb[ys]])
                r0 = 512 + 256 * g
                S.dma("sp", [(self.mixT[r0:r0 + 256, cs].rearrange("(h p) t -> p h t", p=128), yT[ys][:])], [yTb[ys]], [], yTb[ys])

            tcount = [0]

            def mk_tile(kind, g, c, r, j, bk, subs_fl, mask, pre=None, post=None, flush=False):
                t = tcount[0]
                tcount[0] += 1
                pz, pi = t % 2, t % 5
                Q0 = c * CH
                cs = slice(Q0, Q0 + CH)
                rows = CW if kind == "cmp" else 128

                def s1():
                    if kind == "cmp":
                        S.op("pe", lambda: nc.tensor.matmul(ps[pz][0:CW, :], kcs[0:64, j * 128:j * 128 + CW], QM[0:64, r, cs], start=True, stop=True), [gin], [psb[pz]])
                    elif kind == "win":
                        S.op("pe", lambda: nc.tensor.matmul(ps[pz][:], kwn[0:64, j * 128:(j + 1) * 128], QM[0:64, r, cs], start=True, stop=True), [gin], [psb[pz]])
                    else:
                        S.op("pe", lambda: nc.tensor.matmul(ps[pz][:], KE[:, j * 128:(j + 1) * 128], QM[:, r, cs], start=True, stop=True),
                             [gin, keb_k, keb_e, qm_m[c]], [psb[pz]])
                    S.op("act", lambda: nc.scalar.activation(out=P[pi][0:rows, :], in_=ps[pz][0:rows, :], func=AF.Exp), [psb[pz]], [Pb[pi]])
                    if mask is not None:
                        S.op("pool", lambda: nc.gpsimd.affine_select(out=P[pi][0:rows, :], in_=P[pi][0:rows, :], compare_op=ALU.is_ge, fill=0.0, **mask), [Pb[pi]], [Pb[pi]])

                def s2():
                    for (sub, st_, sp_) in subs_fl:
                        if kind == "cmp":
                            b_ = bk[sub // 2]
                            c0 = (sub % 2) * 129
                            S.op("pe", lambda: nc.tensor.matmul(ps[b_][:, c0:c0 + 129], P[pi][0:CW, sub * 128:(sub + 1) * 128], vcs[0:CW, j, 0:129],
                                                                start=st_, stop=sp_, skip_group_check=True), [Pb[pi], gin], [psb[b_]], acc=True)
                        else:
                            vt = vwn if kind == "win" else vsl
                            S.op("pe", lambda: nc.tensor.matmul(ps[bk][:, sub * 65:sub * 65 + 65], P[pi][:, sub * 128:(sub + 1) * 128], vt[:, j, 0:65],
                                                                start=st_, stop=sp_, skip_group_check=True), [Pb[pi], gin], [psb[bk]], acc=True)

                add_tile(s1, s2, pre, post, flush)

            from functools import partial
            for g in range(2):
                for c in range(self.NCH):
                    Q0 = c * CH
                    first_item_pre = [partial(load_gates, g, c)]
                    flush = False
                    if c == 0:
                        first_item_pre = [partial(load_g, g)] + first_item_pre
                        flush = True
                    vblk = [b for b in range(NBLK) if 16 * (128 * b) + 31 <= Q0 + CH - 1]
                    for r in range(4):
                        banks = (2, 3) if r % 2 == 0 else (4, 5)
                        for bi, b in enumerate(vblk):
                            mask = None
                            if not (16 * (128 * b + CW - 1) + 31 <= Q0):
                                mask = dict(pattern=[[1, CH]], base=Q0 - 2048 * b - 31, channel_multiplier=-16)
                            subs_fl = [(sub, (bi == 0 and sub % 2 == 0), (bi == len(vblk) - 1)) for sub in range(4)]
                            post = []
                            if bi == len(vblk) - 1:
                                post.append(partial(cmp_evac, g, c, r, banks))
                                if r == 3:
                                    post.append(partial(topk, g, c))
                            mk_tile("cmp", g, c, r, b, banks, subs_fl, mask, pre=first_item_pre, post=post, flush=flush)
                            first_item_pre = None
                            flush = False
                    for br in ("win", "sel"):
                        for r in range(4):
                            if br == "win":
                                blocks = list(range(max(0, 4 * c - 4), 4 * c + 4))
                                bk = 6 if r % 2 == 0 else 5
                            else:
                                blocks = list(range(0, 4 * c + 4))
                                bk = 2 + (r % 3)
                            for j in blocks:
                                rel = 128 * j - Q0
                                if rel >= 0:
                                    subs = list(range(rel // 128, 4))
                                    mask = dict(pattern=[[1, CH]], base=Q0 - 128 * j, channel_multiplier=-1)
                                elif br == "win":
                                    subs = list(range(0, (rel + 512) // 128 + 1))
                                    mask = dict(pattern=[[-1, CH]], base=128 * j - Q0 + 511, channel_multiplier=1)
                                else:
                                    subs = [0, 1, 2, 3]
                                    mask = None
                                subs_fl = [(sub, (j == blocks[0] and sub == subs[0]), (j == 4 * c + sub)) for sub in subs]
                                post = []
                                if j == blocks[-1]:
                                    post.append(partial(branch_evac, g, c, r, br, bk))
                                    if br == "sel" and r == 3:
                                        post.append(partial(epilogue, g, c))
                                mk_tile(br, g, c, r, j, bk, subs_fl, mask, post=post)
            SKEW = 3
            pending = []

            def retire():
                itm = pending.pop(0)
                itm["s2"]()
                for f in itm["post"]:
                    f()

            for itm in items:
                if itm["flush"]:
                    while pending:
                        retire()
                for f in itm["pre"]:
                    f()
                itm["s1"]()
                pending.append(itm)
                while len(pending) > SKEW:
                    retire()
            while pending:
                retire()
            S.barrier()


def host_constants(T):
    pos = np.arange(T)
    s = np.arange(64)
    cur = pos[:, None] // 64
    forced = (s[None, :] == 0) | (s[None, :] == cur) | (s[None, :] == cur - 1)
    valid = s[None, :] * 64 <= pos[:, None]
    selbias = np.where(valid, np.where(forced, 1e4, 0.0), NEG).astype(np.float32)
    selbias = np.ascontiguousarray(selbias.reshape(T // 128, 128, 64).transpose(1, 0, 2))
    etab = np.where((np.arange(T)[None, :] // 64) == s[:, None], BIG, 0.0).astype(np.float32)
    c0 = np.arange(256) * 16
    s0 = np.arange(64) * 64
    lo = np.maximum(c0[:, None], s0[None, :])
    hi = np.minimum(c0[:, None] + 32, s0[None, :] + 64)
    overlap = (np.maximum(hi - lo, 0).astype(np.float32) / 32)
    overlap[255:] = 0
    k = np.arange(128)
    tri = np.concatenate([np.where(k[:, None] >= k[None, :], -1.0, 0.0), -np.ones((128, 128))], axis=1).astype(np.float32)
    ident = np.eye(128, dtype=np.float32)
    return dict(c_selbias=selbias, c_etab=etab, c_overlap=overlap.astype(np.float32), c_tri=tri, c_ident=ident)


def pack_gains(norm_ffn1, norm_mix, norm_ffn2, norm_final, depth=4):
    g = np.ones((13, D), np.float32)
    for l in range(depth):
        g[l] = norm_ffn1[l]
        g[4 + l] = norm_mix[l]
        g[8 + l] = norm_ffn2[l]
    g[12] = norm_final
    return np.ascontiguousarray(g.reshape(13, KC, 128).transpose(2, 0, 1).reshape(128, 13 * KC))


def make_shared_inputs(inputs, T, depth=4):
    f = lambda a: np.ascontiguousarray(np.asarray(a, dtype=np.float32))
    ne = (depth + 1) // 2
    sh = {}
    sh["gains"] = pack_gains(f(inputs["norm_ffn1"]), f(inputs["norm_mix"]), f(inputs["norm_ffn2"]), f(inputs["norm_final"]), depth)
    for k in ("w_ffn1_in", "w_ffn2_in", "w_ffn1_out", "w_ffn2_out", "w_in_ab", "w_out_ab", "w_qkv_sb", "w_out_sb",
              "cmp_w1_k", "cmp_w1_v", "cmp_w2_k", "cmp_w2_v"):
        sh[k] = f(inputs[k])
    cw = f(inputs["conv_w"])
    sh["convw"] = np.ascontiguousarray(cw.reshape(ne, 3, 4, 128).transpose(3, 0, 2, 1).reshape(128, ne * 12))
    sh["cmp_peT_k"] = np.ascontiguousarray(f(inputs["cmp_pe_k"]).transpose(0, 2, 1))
    sh["cmp_peT_v"] = np.ascontiguousarray(f(inputs["cmp_pe_v"]).transpose(0, 2, 1))
    sh.update(host_constants(T))
    return sh


_PROG_CACHE = {}


def kernel(**inputs):
    x = np.asarray(inputs["x"], dtype=np.float32)
    B, T, _ = x.shape
    key = (T,)
    if key not in _PROG_CACHE:
        _PROG_CACHE[key] = Prog(T).build()
    nc = _PROG_CACHE[key]
    sh = make_shared_inputs(inputs, T)
    in_maps = []
    for b in range(B):
        m = dict(sh)
        m["xT"] = np.ascontiguousarray(x[b].T)
        in_maps.append(m)
    res = run_bass_kernel_spmd(nc, in_maps, core_ids=list(range(B)))
    out = np.stack([np.ascontiguousarray(r["outT"].T) for r in res.results], axis=0)
    return out.astype(np.float32)
```

```python
import numpy as np
from contextlib import ExitStack
import concourse.bass as bass
import concourse.mybir as mybir
from concourse.bass_utils import run_bass_kernel_spmd

F32 = mybir.dt.float32
BF16 = mybir.dt.bfloat16
AF = mybir.ActivationFunctionType
ALU = mybir.AluOpType

D = 1024
KC = 8
DFF = 2816
NJ = 22
EPS = 1e-6
NEG = -1e30
BIG = 16384.0
AB_IN = 2840
CH = 512
import os
NSTOP = int(os.environ.get('NSTOP', '99'))
NATT = int(os.environ.get('NATT', '99'))


class Buf:
    __slots__ = ("name", "w", "r", "sem", "sem_sw")

    def __init__(self, name):
        self.name = name
        self.w = None
        self.r = {}
        self.sem = None
        self.sem_sw = None


class DSem:
    __slots__ = ("h", "cnt", "name")

    def __init__(self, h, name):
        self.h = h
        self.cnt = 0
        self.name = name


class Sched:
    def __init__(self, nc, st, n_dma_sems=88):
        self.nc = nc
        self.eng = {"pe": nc.tensor, "act": nc.scalar, "dve": nc.vector, "pool": nc.gpsimd, "sp": nc.sync}
        self.sem = {}
        self.cnt = {}
        for e in ("pe", "act", "dve", "pool"):
            self.sem[e] = st.enter_context(nc.semaphore("s_" + e))
            self.cnt[e] = 0
        self.known = {e: {} for e in self.eng}
        alld = [DSem(st.enter_context(nc.semaphore("d%d" % i)), "d%d" % i) for i in range(n_dma_sems)]
        self.free_dsems = alld[:n_dma_sems - 24]
        self.free_dsems_sw = alld[n_dma_sems - 24:]
        self.used_dsems = []
        self.used_dsems_sw = []
        self.bufs_with_sem = []

    def _waits(self, e, evs):
        need = {}
        for (key, h, val) in evs:
            if self.known[e].get(key, 0) < val and need.get(key, (None, 0))[1] < val:
                need[key] = (h, val)
        for key, (h, val) in need.items():
            self.eng[e].wait_ge(h, val)
            self.known[e][key] = val

    def _collect(self, e, reads, writes, acc):
        ev = []
        for b in reads:
            if b.w is not None:
                ev.append(b.w)
        for b in writes:
            if b.w is not None and not (acc and b.w[0] == e):
                ev.append(b.w)
            for k, v in b.r.items():
                ev.append(v)
        return ev

    def op(self, e, fn, reads=(), writes=(), acc=False):
        self._waits(e, self._collect(e, reads, writes, acc))
        inst = fn()
        self.cnt[e] += 1
        inst.then_inc(self.sem[e], 1)
        me = (e, self.sem[e], self.cnt[e])
        for b in reads:
            b.r[e] = me
        for b in writes:
            b.w = me
            b.r = {}
        return inst

    def dma(self, q, pairs, reads, writes, owner):
        self._waits(q, self._collect(q, reads, writes, False))
        if q == "pool":
            if owner.sem_sw is None:
                owner.sem_sw = self.free_dsems_sw.pop()
                self.used_dsems_sw.append(owner.sem_sw)
                self.bufs_with_sem.append(owner)
            ds = owner.sem_sw
        else:
            if owner.sem is None:
                owner.sem = self.free_dsems.pop()
                self.used_dsems.append(owner.sem)
                self.bufs_with_sem.append(owner)
            ds = owner.sem
        for (o, i) in pairs:
            self.eng[q].dma_start(out=o, in_=i).then_inc(ds.h, 16)
            ds.cnt += 16
        me = (ds.name, ds.h, ds.cnt)
        for b in reads:
            b.r[ds.name] = me
        for b in writes:
            b.w = me
            b.r = {}

    def barrier(self):
        evs = [(e, self.sem[e], self.cnt[e]) for e in self.sem if self.cnt[e] > 0]
        evs += [(d.name, d.h, d.cnt) for d in self.used_dsems + self.used_dsems_sw if d.cnt > 0]
        for e in self.eng:
            self._waits(e, evs)
        for b in self.bufs_with_sem:
            b.sem = None
            b.sem_sw = None
        self.bufs_with_sem = []
        self.free_dsems.extend(self.used_dsems)
        self.free_dsems_sw.extend(self.used_dsems_sw)
        self.used_dsems = []
        self.used_dsems_sw = []


class Prog:
    def __init__(self, T, depth=4, plan=None):
        self.T = T
        self.NCH = T // CH
        self.depth = depth
        self.plan = plan
        nc = self.nc = bass.Bass("TRN2", target_bir_lowering=False)
        self.st = ExitStack()
        self.inp = {}
        self.declare_io()

    def din(self, name, shape, dt=F32):
        t = self.nc.dram_tensor(name, list(shape), dt, kind="ExternalInput").ap()
        self.inp[name] = t
        return t

    def dscr(self, name, shape, dt):
        return self.nc.dram_tensor(name, list(shape), dt).ap()

    def declare_io(self):
        T = self.T
        nd = self.depth
        ne = (nd + 1) // 2
        no = nd // 2
        self.xT = self.din("xT", [D, T])
        self.gains = self.din("gains", [128, 13 * KC])
        self.w_ffn_in = [self.din("w_ffn1_in", [nd, D, 2 * DFF]), self.din("w_ffn2_in", [nd, D, 2 * DFF])]
        self.w_ffn_out = [self.din("w_ffn1_out", [nd, DFF, D]), self.din("w_ffn2_out", [nd, DFF, D])]
        self.w_in_ab = self.din("w_in_ab", [ne, D, AB_IN])
        self.w_out_ab = self.din("w_out_ab", [ne, D, D])
        self.w_qkv_sb = self.din("w_qkv_sb", [max(no, 1), D, 3 * D])
        self.w_out_sb = self.din("w_out_sb", [max(no, 1), D, D])
        self.convw = self.din("convw", [128, ne * 12])
        self.cmp_w1 = [self.din("cmp_w1_k", [ne, 2048, 128]), self.din("cmp_w1_v", [ne, 2048, 128])]
        self.cmp_w2 = [self.din("cmp_w2_k", [ne, 128, 64]), self.din("cmp_w2_v", [ne, 128, 64])]
        self.cmp_peT = [self.din("cmp_peT_k", [ne, 64, 32]), self.din("cmp_peT_v", [ne, 64, 32])]
        self.c_selbias = self.din("c_selbias", [128, T // 128, 64])
        self.c_etab = self.din("c_etab", [64, T])
        self.c_overlap = self.din("c_overlap", [256, 64])
        self.c_tri = self.din("c_tri", [128, 256])
        self.c_ident = self.din("c_ident", [128, 128])
        self.outT = self.nc.dram_tensor("outT", [D, T], F32, kind="ExternalOutput").ap()
        self.xr = self.dscr("xr", [D, T], F32)
        self.mixT = self.dscr("mixT", [D, T], BF16)
        self.sb_q = self.dscr("sb_q", [D, T], BF16)
        self.sb_k = self.dscr("sb_k", [D, T], BF16)
        self.sb_v = self.dscr("sb_v", [T, D], BF16)
        self.n_q = self.dscr("n_q", [8, 64, T], BF16)
        self.n_ksel = self.dscr("n_ksel", [2, 64, T], BF16)
        self.n_kwin = self.dscr("n_kwin", [2, 64, T], BF16)
        NB = T // 128
        self.NCP = T // 16
        self.NCB = T // 16 - 1
        self.CW = min(128, self.NCP)
        self.NBLK = (self.NCP + 127) // 128
        self.n_vsel = self.dscr("n_vsel", [2, 128, NB, 66], BF16)
        self.n_vwin = self.dscr("n_vwin", [2, 128, NB, 66], BF16)
        self.n_gates = self.dscr("n_gates", [128, NB, 24], F32)
        self.n_kc = self.dscr("n_kc", [2, 64, self.NCP], BF16)
        self.n_vc = self.dscr("n_vc", [2, self.CW, self.NBLK, 130], BF16)

    def sb(self, st, name, shape, dt):
        self._uid = getattr(self, "_uid", 0) + 1
        return st.enter_context(self.nc.sbuf_tensor("%s_%d" % (name, self._uid), list(shape), dt))

    def build(self):
        nc = self.nc
        st = self.st
        S = self.S = Sched(nc, st)
        self.psw = [st.enter_context(nc.psum_tensor("psw%d" % i, [128, 1024], F32)) for i in range(3)]
        ps6 = st.enter_context(nc.psum_tensor("ps6", [128, 512], F32))
        self.ps = [self.psw[i // 2][:, (i % 2) * 512:(i % 2 + 1) * 512] for i in range(6)] + [ps6[:]]
        self.psb = [Buf("ps%d" % i) for i in range(7)]
        self.pst = st.enter_context(nc.psum_tensor("pst", [128, 1024], BF16))
        self.pstb = Buf("pst")
        self.gains_sb = self.sb(st, "gains_sb", [128, 13 * KC], F32)
        self.ones_bf = self.sb(st, "ones_bf", [128, 128], BF16)
        self.tri_bf = self.sb(st, "tri_bf", [128, 256], BF16)
        self.ident_bf = self.sb(st, "ident_bf", [128, 128], BF16)
        self.gconst = Buf("gconst")
        S.dma("sp", [(self.gains_sb[:], self.gains[:, :])], [], [self.gconst], self.gconst)
        S.dma("pool", [(self.tri_bf[:], self.c_tri[:, :]), (self.ident_bf[:], self.c_ident[:, :])], [], [self.gconst], self.gconst)
        S.op("dve", lambda: nc.vector.memset(self.ones_bf[:], 1.0), [], [self.gconst])
        S.barrier()

        plan = self.plan
        if plan is None:
            plan = []
            for l in range(self.depth):
                plan.append(("ffn", l, 0))
                plan.append(("mix", l))
                plan.append(("ffn", l, 1) if l < self.depth - 1 else ("ffn", l, 1, True))
        src = self.xT
        for ph in plan:
            if ph[0] == "ffn":
                l, which = ph[1], ph[2]
                gi = (l if which == 0 else 8 + l)
                self.phase_ffn(src, self.w_ffn_in[which][l], self.w_ffn_out[which][l], gi, final=(len(ph) > 3 and ph[3]))
                src = self.xr
            elif ph[0] == "mix":
                l = ph[1]
                if l % 2 == 1:
                    self.phase_sb_qkv(src, l)
                    self.phase_sb_att(l)
                    self.phase_mix_out(src, self.w_out_sb[l // 2])
                else:
                    self.phase_nsa_proj(src, l)
                    self.phase_nsa_att(l)
                    self.phase_mix_out(src, self.w_out_ab[l // 2])
                src = self.xr
            elif ph[0] == "sb_qkv":
                self.phase_sb_qkv(src, ph[1])
            elif ph[0] == "sb_att":
                self.phase_sb_att(ph[1])
            elif ph[0] == "mix_out":
                self.phase_mix_out(src, self.w_out_sb[0])
                src = self.xr
            elif ph[0] == "nsa_proj":
                self.phase_nsa_proj(src, ph[1])
            elif ph[0] == "nsa_att":
                self.phase_nsa_att(ph[1])
            elif ph[0] == "final":
                self.phase_final(src)
        S.barrier()
        st.close()
        return nc

    def xview(self, ap, c, w=CH):
        return ap.rearrange("(k p) t -> p k t", p=128)[:, :, c * w:(c + 1) * w]

    def load_weight(self, dst_tile, dst_buf, src_ap, nk, split=1):
        F = src_ap.shape[1]
        v = src_ap.rearrange("(k p) f -> p k f", p=128)
        pairs = []
        fs = F // split
        for k in range(nk):
            for s in range(split):
                f0 = s * fs
                f1 = F if s == split - 1 else (s + 1) * fs
                pairs.append((dst_tile[:, k, f0:f1], v[:, k, f0:f1]))
        self.S.dma("pool", pairs, [], [dst_buf], dst_buf)

    def load_weight_groups(self, dst_tile, src_ap, nk, groups):
        v = src_ap.rearrange("(k p) f -> p k f", p=128)
        bufs = []
        for gi_, ranges in enumerate(groups):
            b = Buf("wg%d" % gi_)
            pairs = [(dst_tile[:, k, c0:c1], v[:, k, c0:c1]) for k in range(nk) for (c0, c1) in ranges]
            self.S.dma("pool", pairs, [], [b], b)
            bufs.append(b)
        return bufs

    def emit_norm(self, xt, xb, hT, hb, gi, sq, sqb, rstd, rstdb, ps_ss, ps_ssb, w=CH, sq_eng="pool"):
        nc, S = self.nc, self.S
        nsq = len(sqb)
        for k in range(KC):
            S.op(sq_eng, lambda k=k: (nc.gpsimd if sq_eng == "pool" else nc.vector).tensor_tensor(out=sq[:, k % nsq, :w], in0=xt[:, k, :w], in1=xt[:, k, :w], op=ALU.mult),
                 [xb[k]], [sqb[k % nsq]])
            S.op("pe", lambda k=k: nc.tensor.matmul(ps_ss[:, :w], self.ones_bf[:], sq[:, k % nsq, :w], start=(k == 0), stop=(k == KC - 1)),
                 [sqb[k % nsq], self.gconst], [ps_ssb], acc=True)
        S.op("act", lambda: nc.scalar.activation(out=rstd[:, :w], in_=ps_ss[:, :w], func=AF.Ln, scale=1.0 / D, bias=EPS), [ps_ssb], [rstdb])
        S.op("act", lambda: nc.scalar.activation(out=rstd[:, :w], in_=rstd[:, :w], func=AF.Exp, scale=-0.5), [rstdb], [rstdb])
        for k in range(KC):
            S.op("dve", lambda k=k: nc.vector.scalar_tensor_tensor(out=hT[:, k, :w], in0=xt[:, k, :w], scalar=self.gains_sb[:, gi * KC + k:gi * KC + k + 1],
                                                                     in1=rstd[:, :w], op0=ALU.mult, op1=ALU.mult),
                 [xb[k], rstdb, self.gconst], [hb[k]])

    def phase_ffn(self, src, w_in, w_out, gi, final=False):
        nc, S = self.nc, self.S
        with ExitStack() as st:
            win = self.sb(st, "win", [128, KC, 2 * DFF], BF16)
            wout = self.sb(st, "wout", [128, NJ, D], BF16)
            winb, woutb = Buf("win"), Buf("wout")
            xs = [self.sb(st, "x%d" % i, [128, KC, CH], F32) for i in range(2)]
            xsb = [[Buf("x%d_%d" % (i, k)) for k in range(KC)] for i in range(2)]
            hT = self.sb(st, "hT", [128, KC, CH], BF16)
            hb = [Buf("h%d" % k) for k in range(KC)]
            sq = self.sb(st, "sq", [128, 2, CH], BF16)
            sqb = [Buf("sq%d" % k) for k in range(2)]
            act = self.sb(st, "act", [128, NJ, CH], BF16)
            actb = [Buf("act%d" % j) for j in range(NJ)]
            sg = [self.sb(st, "sg%d" % i, [128, CH], F32) for i in range(2)]
            sgb = [Buf("sg%d" % i) for i in range(2)]
            rstd = self.sb(st, "rstd", [128, CH], F32)
            rstdb = Buf("rstd")
            jr = [(0, 6), (6, 12), (12, 17), (17, 22)]
            wing = self.load_weight_groups(win, w_in, KC, [[(a * 128, b * 128), (DFF + a * 128, DFF + b * 128)] for (a, b) in jr])
            winj = [wing[[i for i, (a, b) in enumerate(jr) if a <= j < b][0]] for j in range(NJ)]
            woutg = self.load_weight_groups(wout, w_out, NJ, [[(0, 512)], [(512, 1024)]])
            ps, psb = self.ps, self.psb

            def load_x(c):
                s = c % 2
                S.dma("sp", [(xs[s][:], self.xview(src, c))], [], xsb[s], xsb[s][0])

            load_x(0)
            self.emit_norm(xs[0], xsb[0], hT, hb, gi, sq, sqb, rstd, rstdb, ps[6], psb[6], sq_eng="dve")
            for c in range(self.NCH):
                s = c % 2
                xt, xb = xs[s], xsb[s]
                if c + 1 < self.NCH:
                    load_x(c + 1)
                for j in range(NJ):
                    pg, pu = j % 2, 2 + j % 2
                    for k in range(KC):
                        S.op("pe", lambda k=k, j=j, pg=pg: nc.tensor.matmul(ps[pg][:], win[:, k, j * 128:(j + 1) * 128], hT[:, k, :], start=(k == 0), stop=(k == KC - 1)),
                             [winj[j], hb[k]], [psb[pg]], acc=True)
                    for k in range(KC):
                        S.op("pe", lambda k=k, j=j, pu=pu: nc.tensor.matmul(ps[pu][:], win[:, k, DFF + j * 128:DFF + (j + 1) * 128], hT[:, k, :], start=(k == 0), stop=(k == KC - 1)),
                             [winj[j], hb[k]], [psb[pu]], acc=True)
                    S.op("act", lambda j=j, pg=pg: nc.scalar.activation(out=sg[j % 2][:], in_=ps[pg][:], func=AF.Silu),
                         [psb[pg]], [sgb[j % 2]])
                    S.op("dve", lambda j=j, pu=pu: nc.vector.tensor_tensor(out=act[:, j, :], in0=sg[j % 2][:], in1=ps[pu][:], op=ALU.mult),
                         [sgb[j % 2], psb[pu]], [actb[j]])
                if c + 1 < self.NCH:
                    self.emit_norm(xs[(c + 1) % 2], xsb[(c + 1) % 2], hT, hb, gi, sq, sqb, rstd, rstdb, ps[6], psb[6])
                for m in range(KC):
                    po = 4 + m % 2
                    for j in range(NJ):
                        S.op("pe", lambda m=m, j=j, po=po: nc.tensor.matmul(ps[po][:], wout[:, j, m * 128:(m + 1) * 128], act[:, j, :], start=(j == 0), stop=(j == NJ - 1)),
                             [woutg[m // 4], actb[j]], [psb[po]], acc=True)
                    S.op("dve", lambda m=m, po=po: nc.vector.scalar_tensor_tensor(out=xt[:, m, :], in0=ps[po][:], scalar=0.5, in1=xt[:, m, :], op0=ALU.mult, op1=ALU.add),
                         [psb[po], xb[m]], [xb[m]])
                if final:
                    g12 = 12
                    for k in range(KC):
                        S.op("pool", lambda k=k: nc.gpsimd.tensor_tensor(out=sq[:, k % 2, :], in0=xt[:, k, :], in1=xt[:, k, :], op=ALU.mult), [xb[k]], [sqb[k % 2]])
                        S.op("pe", lambda k=k: nc.tensor.matmul(ps[6][:], self.ones_bf[:], sq[:, k % 2, :], start=(k == 0), stop=(k == KC - 1)),
                             [sqb[k % 2], self.gconst], [psb[6]], acc=True)
                    S.op("act", lambda: nc.scalar.activation(out=rstd[:], in_=ps[6][:], func=AF.Ln, scale=1.0 / D, bias=EPS), [psb[6]], [rstdb])
                    S.op("act", lambda: nc.scalar.activation(out=rstd[:], in_=rstd[:], func=AF.Exp, scale=-0.5), [rstdb], [rstdb])
                    for k in range(KC):
                        S.op("dve", lambda k=k: nc.vector.scalar_tensor_tensor(out=xt[:, k, :], in0=xt[:, k, :], scalar=self.gains_sb[:, g12 * KC + k:g12 * KC + k + 1],
                                                                                 in1=rstd[:], op0=ALU.mult, op1=ALU.mult), [xb[k], rstdb, self.gconst], [xb[k]])
                    S.dma("sp", [(self.xview(self.outT, c), xt[:])], xb, [], xb[0])
                else:
                    S.dma("sp", [(self.xview(self.xr, c), xt[:])], xb, [], xb[0])
            S.barrier()

    def phase_final(self, src):
        nc, S = self.nc, self.S
        gi = 12
        with ExitStack() as st:
            xs = [self.sb(st, "x%d" % i, [128, KC, CH], F32) for i in range(2)]
            xsb = [[Buf("x%d_%d" % (i, k)) for k in range(KC)] for i in range(2)]
            sq = self.sb(st, "sq", [128, 2, CH], BF16)
            sqb = [Buf("sq%d" % k) for k in range(2)]
            rstd = self.sb(st, "rstd", [128, CH], F32)
            rstdb = Buf("rstd")
            ps, psb = self.ps, self.psb
            for c in range(self.NCH):
                s = c % 2
                xt, xb = xs[s], xsb[s]
                S.dma("sp", [(xt[:], self.xview(src, c))], [], xb, xb[0])
                for k in range(KC):
                    S.op("pool", lambda k=k: nc.gpsimd.tensor_tensor(out=sq[:, k % 2, :], in0=xt[:, k, :], in1=xt[:, k, :], op=ALU.mult), [xb[k]], [sqb[k % 2]])
                    S.op("pe", lambda k=k: nc.tensor.matmul(ps[6][:], self.ones_bf[:], sq[:, k % 2, :], start=(k == 0), stop=(k == KC - 1)),
                         [sqb[k % 2], self.gconst], [psb[6]], acc=True)
                S.op("act", lambda: nc.scalar.activation(out=rstd[:], in_=ps[6][:], func=AF.Ln, scale=1.0 / D, bias=EPS), [psb[6]], [rstdb])
                S.op("act", lambda: nc.scalar.activation(out=rstd[:], in_=rstd[:], func=AF.Exp, scale=-0.5), [rstdb], [rstdb])
                for k in range(KC):
                    S.op("dve", lambda k=k: nc.vector.scalar_tensor_tensor(out=xt[:, k, :], in0=xt[:, k, :], scalar=self.gains_sb[:, gi * KC + k:gi * KC + k + 1],
                                                                             in1=rstd[:], op0=ALU.mult, op1=ALU.mult), [xb[k], rstdb, self.gconst], [xb[k]])
                S.dma("sp", [(self.xview(self.outT, c), xt[:])], xb, [], xb[0])
            S.barrier()

    def phase_mix_out(self, src, w_out):
        nc, S = self.nc, self.S
        with ExitStack() as st:
            wo = self.sb(st, "wo", [128, KC, D], BF16)
            wob = Buf("wo")
            xs = [self.sb(st, "x%d" % i, [128, KC, CH], F32) for i in range(2)]
            xsb = [[Buf("x%d_%d" % (i, k)) for k in range(KC)] for i in range(2)]
            ms = [self.sb(st, "m%d" % i, [128, KC, CH], BF16) for i in range(2)]
            msb = [Buf("m%d" % i) for i in range(2)]
            wog = self.load_weight_groups(wo, w_out, KC, [[(0, 256)], [(256, 1024)]])
            ps, psb = self.ps, self.psb
            def ld(c):
                s_ = c % 2
                S.dma("sp", [(xs[s_][:], self.xview(src, c))], [], xsb[s_], xsb[s_][0])
                S.dma("sp", [(ms[s_][:], self.xview(self.mixT, c))], [], [msb[s_]], msb[s_])

            ld(0)
            for c in range(self.NCH):
                s = c % 2
                xt, xb = xs[s], xsb[s]
                if c + 1 < self.NCH:
                    ld(c + 1)
                for m in range(KC):
                    po = m % 4
                    for k in range(KC):
                        S.op("pe", lambda m=m, k=k, po=po: nc.tensor.matmul(ps[po][:], wo[:, k, m * 128:(m + 1) * 128], ms[s][:, k, :], start=(k == 0), stop=(k == KC - 1)),
                             [wog[0 if m < 2 else 1], msb[s]], [psb[po]], acc=True)
                    S.op("dve", lambda m=m, po=po: nc.vector.tensor_tensor(out=xt[:, m, :], in0=ps[po][:], in1=xt[:, m, :], op=ALU.add),
                         [psb[po], xb[m]], [xb[m]])
                S.dma("sp", [(self.xview(self.xr, c), xt[:])], xb, [], xb[0])
            S.barrier()

    def phase_sb_qkv(self, src, l):
        nc, S = self.nc, self.S
        gi = 4 + l
        T = self.T
        with ExitStack() as st:
            wq = self.sb(st, "wq", [128, KC, 3 * D], BF16)
            wqb = Buf("wq")
            wqg = self.load_weight_groups(wq, self.w_qkv_sb[l // 2], KC, [[(0, 512)], [(512, D)], [(D, 2 * D)], [(2 * D, 3 * D)]])
            xs = [self.sb(st, "x%d" % i, [128, KC, CH], F32) for i in range(2)]
            xsb = [[Buf("x%d_%d" % (i, k)) for k in range(KC)] for i in range(2)]
            hT = self.sb(st, "hT", [128, KC, CH], BF16)
            hb = [Buf("h%d" % k) for k in range(KC)]
            sq = self.sb(st, "sq", [128, 2, CH], BF16)
            sqb = [Buf("sq%d" % k) for k in range(2)]
            rstd = self.sb(st, "rstd", [128, CH], F32)
            rstdb = Buf("rstd")
            qo = [self.sb(st, "qo%d" % i, [128, KC, CH], BF16) for i in range(2)]
            qob = [Buf("qo%d" % i) for i in range(2)]
            vo = [self.sb(st, "vo%d" % i, [128, 4, D], BF16) for i in range(2)]
            vob = [Buf("vo%d" % i) for i in range(2)]
            ps, psb = self.ps, self.psb
            it = 0

            def ld(c):
                S.dma("sp", [(xs[c % 2][:], self.xview(src, c))], [], xsb[c % 2], xsb[c % 2][0])

            ld(0)
            for c in range(self.NCH):
                s = c % 2
                xt, xb = xs[s], xsb[s]
                if c + 1 < self.NCH:
                    ld(c + 1)
                self.emit_norm(xt, xb, hT, hb, gi, sq, sqb, rstd, rstdb, ps[6], psb[6], sq_eng=("dve" if c == 0 else "pool"))
                for qk in range(2):
                    stg, stgb = qo[qk], qob[qk]
                    for m in range(KC):
                        po = it % 4
                        it += 1
                        f0 = qk * D + m * 128
                        for k in range(KC):
                            S.op("pe", lambda k=k, f0=f0, po=po: nc.tensor.matmul(ps[po][:], wq[:, k, f0:f0 + 128], hT[:, k, :], start=(k == 0), stop=(k == KC - 1)),
                                 [wqg[0 if f0 < 512 else (1 if f0 < D else 2)], hb[k]], [psb[po]], acc=True)
                        S.op("act", lambda m=m, po=po, stg=stg, qk=qk: nc.scalar.activation(out=stg[:, m, :], in_=ps[po][:], func=AF.Copy, scale=(0.125 if qk == 0 else 1.0)),
                             [psb[po]], [stgb])
                    dst = self.sb_q if qk == 0 else self.sb_k
                    S.dma("sp", [(self.xview(dst, c), stg[:])], [stgb], [], stgb)
                vs, vsb = vo[c % 2], vob[c % 2]
                for sub in range(4):
                    for half in range(2):
                        po = it % 4
                        it += 1
                        f0 = 2 * D + half * 512
                        for k in range(KC):
                            S.op("pe", lambda k=k, f0=f0, po=po, sub=sub: nc.tensor.matmul(ps[po][:], hT[:, k, sub * 128:(sub + 1) * 128], wq[:, k, f0:f0 + 512], start=(k == 0), stop=(k == KC - 1)),
                                 [wqg[3], hb[k]], [psb[po]], acc=True)
                        S.op("dve", lambda po=po, sub=sub, half=half, vs=vs: nc.vector.tensor_copy(out=vs[:, sub, half * 512:(half + 1) * 512], in_=ps[po][:]),
                             [psb[po]], [vsb])
                S.dma("sp", [(self.sb_v[c * CH:(c + 1) * CH, :].rearrange("(s p) f -> p s f", p=128), vs[:])], [vsb], [], vsb)
            S.barrier()

    def phase_sb_att(self, l):
        nc, S = self.nc, self.S
        T = self.T
        NB = T // 128
        with ExitStack() as st:
            kT = [self.sb(st, "kT%d" % i, [128, T], BF16) for i in range(2)]
            qT = [self.sb(st, "qT%d" % i, [128, T], BF16) for i in range(2)]
            vv = [self.sb(st, "vv%d" % i, [128, NB, 128], BF16) for i in range(2)]
            inb = [Buf("in%d" % i) for i in range(2)]
            oT = [self.sb(st, "oT%d" % i, [128, T], BF16) for i in range(2)]
            oTb = [Buf("oT%d" % i) for i in range(2)]
            e32 = [self.sb(st, "e32_%d" % i, [128, 2 * CH], F32) for i in range(2)]
            e32b = [Buf("e32_%d" % i) for i in range(2)]
            Lb = [self.sb(st, "Lb%d" % i, [128, 2 * CH], BF16) for i in range(4)]
            Lbb = [Buf("Lb%d" % i) for i in range(4)]
            Sb = [self.sb(st, "Sb%d" % i, [128, 2 * CH], BF16) for i in range(3)]
            Sbb = [Buf("Sb%d" % i) for i in range(3)]
            Ab = [self.sb(st, "Ab%d" % i, [128, 2 * CH], BF16) for i in range(3)]
            Abb = [Buf("Ab%d" % i) for i in range(3)]
            ps, psb = self.ps, self.psb
            tri = self.tri_bf
            it = 0

            def load_pair(p):
                s = p % 2
                rows = slice(p * 128, (p + 1) * 128)
                S.dma("sp", [(kT[s][:], self.sb_k[rows, :]), (qT[s][:], self.sb_q[rows, :]),
                             (vv[s][:], self.sb_v[:, rows].rearrange("(j p) f -> p j f", p=128))], [], [inb[s]], inb[s])

            tiles = []
            for p in range(8):
                for c in range(self.NCH):
                    jd = 4 * c + 3
                    for j in range(jd, -1, -1):
                        tiles.append(dict(p=p, s=p % 2, c=c, j=j, first=(j == jd), last=(j == 0), diag=(j >= 4 * c),
                                          pair_last=(c == self.NCH - 1 and j == 0)))
            psw = self.psw
            pswb = [Buf("psw%d" % i) for i in range(3)]
            def v3(tile_ap, c0):
                return tile_ap.rearrange("p (h c) -> p h c", h=2)[:, :, c0:CH]

            def hs(hh, c0):
                return slice(hh * CH + c0, (hh + 1) * CH)

            def stage_a(t, tl):
                s, c, j = tl["s"], tl["c"], tl["j"]
                Q0 = c * CH
                c0 = 128 * (j - 4 * c) if tl["diag"] else 0
                w = CH - c0
                wi, ei, li = t % 3, t % 2, t % 4
                ks = slice(j * 128, (j + 1) * 128)
                qs = slice(Q0 + c0, Q0 + CH)
                for hh in range(2):
                    pr = slice(64 * hh, 64 * hh + 64)
                    S.op("pe", lambda: nc.tensor.matmul(psw[wi][:, hs(hh, c0)], kT[s][pr, ks], qT[s][pr, qs], start=True, stop=True),
                         [inb[s]], [pswb[wi]], acc=(hh == 1))
                S.op("act", lambda: nc.scalar.activation(out=v3(e32[ei][:], c0), in_=v3(psw[wi][:], c0), func=AF.Exp), [pswb[wi]], [e32b[ei]])
                S.op("act", lambda: nc.scalar.activation(out=v3(Lb[li][:], c0), in_=v3(e32[ei][:], c0), func=AF.Ln, bias=1.0), [e32b[ei]], [Lbb[li]])
                if tl["diag"]:
                    S.op("pool", lambda: nc.gpsimd.affine_select(out=v3(Lb[li][:], c0), in_=v3(Lb[li][:], c0), pattern=[[0, 2], [1, w]], compare_op=ALU.is_gt,
                                                                 fill=0.0, base=0, channel_multiplier=-1), [Lbb[li]], [Lbb[li]])
                if not tl["last"]:
                    so, sn = t % 3, (t + 1) % 3
                    if tl["first"]:
                        S.op("dve", lambda: nc.vector.tensor_copy(out=v3(Sb[sn][:], c0), in_=v3(Lb[li][:], c0)), [Lbb[li]], [Sbb[sn]])
                    elif tl["diag"]:
                        S.op("dve", lambda: nc.vector.tensor_copy(out=v3(Sb[sn][:], c0)[:, :, 0:128], in_=v3(Lb[li][:], c0)[:, :, 0:128]), [Lbb[li]], [Sbb[sn]])
                        S.op("dve", lambda: nc.vector.tensor_tensor(out=v3(Sb[sn][:], c0 + 128), in0=v3(Sb[so][:], c0 + 128), in1=v3(Lb[li][:], c0 + 128), op=ALU.add),
                             [Lbb[li], Sbb[so], Sbb[sn]], [Sbb[sn]])
                    else:
                        S.op("dve", lambda: nc.vector.tensor_tensor(out=Sb[sn][:], in0=Sb[so][:], in1=Lb[li][:], op=ALU.add), [Lbb[li], Sbb[so]], [Sbb[sn]])

            def stage_b(t, tl):
                c, j = tl["c"], tl["j"]
                c0 = 128 * (j - 4 * c) if tl["diag"] else 0
                w = CH - c0
                wi, li, ai, so = t % 3, t % 4, t % 3, t % 3
                first = tl["first"]
                for hh in range(2):
                    S.op("pe", lambda: nc.tensor.matmul(psw[wi][:, hs(hh, c0)], tri[:, 0:128], Lb[li][:, hs(hh, c0)], start=False, stop=first, skip_group_check=True),
                         [Lbb[li], self.gconst], [pswb[wi]], acc=True)
                    if not first:
                        c1 = c0 + 128 if tl["diag"] else 0
                        S.op("pe", lambda: nc.tensor.matmul(psw[wi][:, hs(hh, c1)], tri[:, 128:256], Sb[so][:, hs(hh, c1)], start=False, stop=True, skip_group_check=True),
                             [Sbb[so], self.gconst], [pswb[wi]], acc=True)
                S.op("act", lambda: nc.scalar.activation(out=v3(Ab[ai][:], c0), in_=v3(psw[wi][:], c0), func=AF.Exp), [pswb[wi]], [Abb[ai]])
                if tl["diag"]:
                    S.op("pool", lambda: nc.gpsimd.affine_select(out=v3(Ab[ai][:], c0), in_=v3(Ab[ai][:], c0), pattern=[[0, 2], [1, w]], compare_op=ALU.is_gt,
                                                                 fill=0.0, base=0, channel_multiplier=-1), [Abb[ai]], [Abb[ai]])

            def stage_c(t, tl):
                s, c, j, p = tl["s"], tl["c"], tl["j"], tl["p"]
                Q0 = c * CH
                c0 = 128 * (j - 4 * c) if tl["diag"] else 0
                cq = slice(c0, CH)
                ai = t % 3
                for hh in range(2):
                    pr = slice(64 * hh, 64 * hh + 64)
                    S.op("pe", lambda: nc.tensor.matmul(ps[6][pr, cq], vv[s][:, j, pr], Ab[ai][:, hs(hh, c0)], start=tl["first"], stop=tl["last"], skip_group_check=True),
                         [inb[s], Abb[ai]], [psb[6]], acc=True)
                if tl["last"]:
                    S.op("dve", lambda: nc.vector.tensor_copy(out=oT[s][:, Q0:Q0 + CH], in_=ps[6][:, :]), [psb[6]], [oTb[s]])
                if tl["pair_last"]:
                    S.dma("sp", [(self.mixT[p * 128:(p + 1) * 128, :], oT[s][:])], [oTb[s]], [], oTb[s])
                    if p + 2 < 8:
                        load_pair(p + 2)

            load_pair(0)
            load_pair(1)
            n = len(tiles)
            for t in range(n + 2):
                if t < n:
                    stage_a(t, tiles[t])
                if 0 <= t - 1 < n:
                    stage_b(t - 1, tiles[t - 1])
                if 0 <= t - 2 < n:
                    stage_c(t - 2, tiles[t - 2])
            S.barrier()

    def phase_nsa_proj(self, src, l):
        nc, S = self.nc, self.S
        T = self.T
        e = l // 2
        gi = 4 + l
        NCB, NCP, CW, NBLK = self.NCB, self.NCP, self.CW, self.NBLK
        ps, psb = self.ps, self.psb
        with ExitStack() as st:
            wab = self.sb(st, "wab", [128, KC, AB_IN], BF16)
            wabb = Buf("wab")
            wabg = self.load_weight_groups(wab, self.w_in_ab[e], KC, [[(512, 1536)], [(0, 512)], [(1536, 2304)], [(2304, AB_IN)]])

            def wab_buf(f0):
                return wabg[0] if 512 <= f0 < 1536 else (wabg[1] if f0 < 512 else (wabg[2] if f0 < 2304 else wabg[3]))
            cw = self.sb(st, "cw", [128, 12], F32)
            cwb = Buf("cw")
            S.dma("sp", [(cw[:], self.convw[:, e * 12:(e + 1) * 12])], [], [cwb], cwb)
            xs = [self.sb(st, "x%d" % i, [128, KC, CH], F32) for i in range(2)]
            xsb = [[Buf("x%d_%d" % (i, k)) for k in range(KC)] for i in range(2)]
            hT = self.sb(st, "hT", [128, KC, CH], BF16)
            hb = [Buf("h%d" % k) for k in range(KC)]
            sq = self.sb(st, "sq", [128, 2, CH], BF16)
            sqb = [Buf("sq%d" % k) for k in range(2)]
            rstd = self.sb(st, "rstd", [128, CH], F32)
            rstdb = Buf("rstd")
            cT = [self.sb(st, "cT%d" % i, [128, T], BF16) for i in range(2)]
            cTb = [Buf("cT%d" % i) for i in range(2)]
            u = self.sb(st, "u", [128, 4, CH + 2], F32)
            ub = [Buf("u%d" % i) for i in range(4)]
            cS = [self.sb(st, "cS%d" % i, [128, CH], F32) for i in range(2)]
            cSb = [Buf("cS%d" % i) for i in range(2)]
            t1 = [self.sb(st, "t1_%d" % i, [128, CH], F32) for i in range(2)]
            t1b = [Buf("t1_%d" % i) for i in range(2)]
            ycv = [self.sb(st, "ycv%d" % i, [128, 4, CH], BF16) for i in range(2)]
            ycvb = [Buf("ycv%d" % i) for i in range(2)]
            qst = [self.sb(st, "qst%d" % i, [64, 8, CH], BF16) for i in range(2)]
            qstb = [Buf("qst%d" % i) for i in range(2)]
            kst = [self.sb(st, "kst%d" % i, [64, 4, CH], BF16) for i in range(2)]
            kstb = [Buf("kst%d" % i) for i in range(2)]
            vsl = [self.sb(st, "vsl%d" % i, [128, 2, 4, 66], BF16) for i in range(2)]
            vwn = [self.sb(st, "vwn%d" % i, [128, 2, 4, 66], BF16) for i in range(2)]
            gst = [self.sb(st, "gst%d" % i, [128, 4, 24], F32) for i in range(2)]
            tkb = [Buf("tk%d" % i) for i in range(2)]
            for i in range(2):
                S.op("pool", lambda i=i: nc.gpsimd.memset(vsl[i][:], 1.0), [], [tkb[i]])
                S.op("pool", lambda i=i: nc.gpsimd.memset(vwn[i][:], 1.0), [], [tkb[i]])
            S.op("pool", lambda: nc.gpsimd.memset(u[:], 0.0), [], ub)
            it = 0

            def proj(f0, M, rows=None):
                nonlocal it
                po = it % 6
                it += 1
                for k in range(KC):
                    S.op("pe", lambda k=k, po=po: nc.tensor.matmul(ps[po][0:M, :], wab[:, k, f0:f0 + M], hT[:, k, :], start=(k == 0), stop=(k == KC - 1)),
                         [wab_buf(f0), hb[k]], [psb[po]], acc=True)
                return po

            def ld(c):
                S.dma("sp", [(xs[c % 2][:], self.xview(src, c))], [], xsb[c % 2], xsb[c % 2][0])

            ld(0)
            for c in range(self.NCH):
                s = c % 2
                cs = slice(c * CH, (c + 1) * CH)
                xt, xb = xs[s], xsb[s]
                if c + 1 < self.NCH:
                    ld(c + 1)
                self.emit_norm(xt, xb, hT, hb, gi, sq, sqb, rstd, rstdb, ps[6], psb[6], sq_eng=("dve" if c == 0 else "pool"))
                for i in range(4 if NSTOP >= 2 else 0):
                    pc = proj(512 + 128 * i, 128)
                    S.op("act", lambda pc=pc, i=i: nc.scalar.activation(out=cS[i % 2][:], in_=ps[pc][:], func=AF.Copy), [psb[pc]], [cSb[i % 2]])
                    ph = proj(1024 + 128 * i, 128)
                    S.op("dve", lambda ph=ph, i=i: nc.vector.tensor_tensor(out=u[:, i, 2:CH + 2], in0=cS[i % 2][:], in1=ps[ph][:], op=ALU.mult),
                         [cSb[i % 2], psb[ph]], [ub[i]])
                    pb_ = proj(128 * i, 128)
                    tt, ttb = t1[i % 2], t1b[i % 2]
                    S.op("dve", lambda i=i, tt=tt: nc.vector.tensor_scalar(out=tt[:], in0=u[:, i, 0:CH], scalar1=cw[:, i * 3:i * 3 + 1], scalar2=None, op0=ALU.mult),
                         [ub[i], cwb], [ttb])
                    S.op("dve", lambda i=i, tt=tt: nc.vector.scalar_tensor_tensor(out=tt[:], in0=u[:, i, 1:CH + 1], scalar=cw[:, i * 3 + 1:i * 3 + 2], in1=tt[:], op0=ALU.mult, op1=ALU.add),
                         [ub[i], cwb, ttb], [ttb])
                    S.op("dve", lambda i=i, tt=tt: nc.vector.scalar_tensor_tensor(out=tt[:], in0=u[:, i, 2:CH + 2], scalar=cw[:, i * 3 + 2:i * 3 + 3], in1=tt[:], op0=ALU.mult, op1=ALU.add),
                         [ub[i], cwb, ttb], [ttb])
                    S.op("dve", lambda i=i, tt=tt, pb_=pb_: nc.vector.tensor_tensor(out=ycv[s][:, i, :], in0=tt[:], in1=ps[pb_][:], op=ALU.mult),
                         [ttb, psb[pb_]], [ycvb[s]])
                    S.op("dve", lambda i=i: nc.vector.tensor_copy(out=u[:, i, 0:2], in_=u[:, i, CH:CH + 2]), [ub[i]], [ub[i]])
                if NSTOP >= 2:
                    S.dma("sp", [(self.mixT[0:512, cs].rearrange("(i p) t -> p i t", p=128), ycv[s][:])], [ycvb[s]], [], ycvb[s])
                if NSTOP < 3:
                    continue
                for hq in range(8):
                    pq = proj(1536 + 64 * hq, 64)
                    S.op("act", lambda pq=pq, hq=hq: nc.scalar.activation(out=qst[s][:, hq, :], in_=ps[pq][0:64, :], func=AF.Copy, scale=0.125), [psb[pq]], [qstb[s]])
                S.dma("sp", [(self.n_q.rearrange("h d t -> d h t")[:, :, cs], qst[s][:])], [qstb[s]], [], qstb[s])
                if NSTOP < 4:
                    continue
                for kv in range(2):
                    pk = proj(2048 + 128 * kv, 128)
                    S.op("dve", lambda pk=pk, kv=kv: nc.vector.tensor_copy(out=cT[kv][:, cs], in_=ps[pk][:]), [psb[pk]], [cTb[kv]])
                for idx, f0 in enumerate((2304, 2368, 2560, 2624)):
                    pk = proj(f0, 64)
                    S.op("act", lambda pk=pk, idx=idx: nc.scalar.activation(out=kst[s][:, idx, :], in_=ps[pk][0:64, :], func=AF.Copy), [psb[pk]], [kstb[s]])
                S.dma("sp", [(self.n_ksel.rearrange("g d t -> d g t")[:, :, cs], kst[s][:, 0:2, :]),
                             (self.n_kwin.rearrange("g d t -> d g t")[:, :, cs], kst[s][:, 2:4, :])], [kstb[s]], [], kstb[s])
                if NSTOP < 5:
                    continue
                for sub in range(4):
                    po = it % 6
                    it += 1
                    for k in range(KC):
                        S.op("pe", lambda k=k, po=po, sub=sub: nc.tensor.matmul(ps[po][:, 0:408], hT[:, k, sub * 128:(sub + 1) * 128], wab[:, k, 2432:2840], start=(k == 0), stop=(k == KC - 1)),
                             [wabg[3], hb[k]], [psb[po]], acc=True)
                    S.op("dve", lambda po=po, sub=sub: nc.vector.tensor_copy(out=vsl[s][:, :, sub, 0:64], in_=ps[po][:, 0:128].rearrange("p (g d) -> p g d", g=2)), [psb[po]], [tkb[s]])
                    S.op("dve", lambda po=po, sub=sub: nc.vector.tensor_copy(out=vwn[s][:, :, sub, 0:64], in_=ps[po][:, 256:384].rearrange("p (g d) -> p g d", g=2)), [psb[po]], [tkb[s]])
                    S.op("act", lambda po=po, sub=sub: nc.scalar.activation(out=gst[s][:, sub, :], in_=ps[po][:, 384:408], func=AF.Sigmoid), [psb[po]], [tkb[s]])
                S.dma("sp", [(self.n_vsel[:, :, 4 * c:4 * c + 4, :].rearrange("g p s e -> p g s e"), vsl[s][:]),
                             (self.n_vwin[:, :, 4 * c:4 * c + 4, :].rearrange("g p s e -> p g s e"), vwn[s][:]),
                             (self.n_gates[:, 4 * c:4 * c + 4, :], gst[s][:])], [tkb[s]], [], tkb[s])

            if NSTOP < 6:
                S.barrier()
                return
            w1 = [self.sb(st, "w1_%d" % i, [128, 32, 128], BF16) for i in range(2)]
            w2 = [self.sb(st, "w2_%d" % i, [128, 64], BF16) for i in range(2)]
            peT = [self.sb(st, "peT%d" % i, [64, 32], BF16) for i in range(2)]
            cwtb = Buf("cmpw")
            pairs = []
            for kv in range(2):
                v = self.cmp_w1[kv][e].rearrange("(l d) h -> d l h", d=64)
                pairs += [(w1[kv][0:64], v), (w1[kv][64:128], v), (w2[kv][:], self.cmp_w2[kv][e]), (peT[kv][:], self.cmp_peT[kv][e])]
            S.dma("pool", pairs, [], [cwtb], cwtb)
            cvec = self.sb(st, "cvec", [128, 2], F32)
            cvecb = Buf("cvec")
            xh = self.sb(st, "xh", [128, NCP], F32)
            x2 = self.sb(st, "x2", [128, NCP], F32)
            sgm = self.sb(st, "sgm", [128, NCP], F32)
            hid = self.sb(st, "hid", [128, NCP], BF16)
            xhb, x2b, sgmb, hidb = Buf("xh"), Buf("x2"), Buf("sgm"), Buf("hid")
            kcs = self.sb(st, "kcs", [64, NCP], BF16)
            kcsb = Buf("kcs")
            vcs = self.sb(st, "vcs", [CW, NBLK, 130], BF16)
            vcsb = Buf("vcs")
            S.op("pool", lambda: nc.gpsimd.memset(hid[:], 0.0), [], [hidb])
            for kv in range(2):
                pv = it % 6
                it += 1
                for l_ in range(32):
                    S.op("pe", lambda l_=l_, pv=pv, kv=kv: nc.tensor.matmul(ps[pv][:, 0:1], w1[kv][0:64, l_, :], peT[kv][0:64, l_:l_ + 1], start=(l_ == 0), stop=(l_ == 31)),
                         [cwtb], [psb[pv]], acc=True)
                S.op("dve", lambda pv=pv, kv=kv: nc.vector.tensor_copy(out=cvec[:, kv:kv + 1], in_=ps[pv][:, 0:1]), [psb[pv]], [cvecb])
            for kv in range(2 if NSTOP >= 7 else 0):
                for g in range(2):
                    gr = slice(64 * g, 64 * g + 64)
                    ph = it % 6
                    it += 1
                    srcv = cT[kv][gr, :].rearrange("p (c s) -> p c s", s=16)
                    for l_ in range(32):
                        S.op("pe", lambda l_=l_, ph=ph, kv=kv, gr=gr, srcv=srcv: nc.tensor.matmul(ps[ph][:, 0:NCB], w1[kv][gr, l_, :], srcv[:, l_ // 16:l_ // 16 + NCB, l_ % 16],
                                                                                                   start=(l_ == 0), stop=(l_ == 31)),
                             [cwtb, cTb[kv]], [psb[ph]], acc=True)
                    S.op("act", lambda ph=ph, kv=kv: nc.scalar.activation(out=xh[:, 0:NCB], in_=ps[ph][:, 0:NCB], func=AF.Identity, bias=cvec[:, kv:kv + 1]), [psb[ph], cvecb], [xhb])
                    S.op("dve", lambda: nc.vector.tensor_tensor(out=x2[:, 0:NCB], in0=xh[:, 0:NCB], in1=xh[:, 0:NCB], op=ALU.mult), [xhb], [x2b])
                    S.op("dve", lambda: nc.vector.tensor_scalar(out=x2[:, 0:NCB], in0=x2[:, 0:NCB], scalar1=0.044715, scalar2=1.0, op0=ALU.mult, op1=ALU.add), [x2b], [x2b])
                    S.op("dve", lambda: nc.vector.tensor_tensor(out=x2[:, 0:NCB], in0=x2[:, 0:NCB], in1=xh[:, 0:NCB], op=ALU.mult), [x2b, xhb], [x2b])
                    S.op("act", lambda: nc.scalar.activation(out=sgm[:, 0:NCB], in_=x2[:, 0:NCB], func=AF.Sigmoid, scale=1.5957691216057308), [x2b], [sgmb])
                    S.op("dve", lambda: nc.vector.tensor_tensor(out=hid[:, 0:NCB], in0=xh[:, 0:NCB], in1=sgm[:, 0:NCB], op=ALU.mult), [xhb, sgmb], [hidb])
                    if NSTOP < 8:
                        continue
                    if kv == 1 and NSTOP < 9:
                        continue
                    if kv == 0:
                        pk = it % 6
                        it += 1
                        S.op("pe", lambda pk=pk: nc.tensor.matmul(ps[pk][0:64, 0:NCP], w2[0][:], hid[:], start=True, stop=True), [cwtb, hidb], [psb[pk]])
                        S.op("dve", lambda pk=pk: nc.vector.tensor_copy(out=kcs[:], in_=ps[pk][0:64, 0:NCP]), [psb[pk]], [kcsb])
                        S.dma("sp", [(self.n_kc[g], kcs[:])], [kcsb], [], kcsb)
                    else:
                        S.dma("pool", [(vcs[:, b, 0:64], self.c_overlap[b * 128:b * 128 + CW, :]) for b in range(NBLK)], [], [vcsb], vcsb)
                        S.op("pool", lambda: nc.gpsimd.memset(vcs[:, :, 128:130], 1.0), [], [vcsb])
                        for b in range(NBLK):
                            pk = it % 6
                            it += 1
                            S.op("pe", lambda pk=pk, b=b: nc.tensor.matmul(ps[pk][0:CW, 0:64], hid[:, b * 128:b * 128 + CW], w2[1][:], start=True, stop=True), [cwtb, hidb], [psb[pk]])
                            S.op("dve", lambda pk=pk, b=b: nc.vector.tensor_copy(out=vcs[:, b, 64:128], in_=ps[pk][0:CW, 0:64]), [psb[pk]], [vcsb])
                        S.dma("sp", [(self.n_vc[g], vcs[:])], [vcsb], [], vcsb)
            S.barrier()

    def phase_nsa_att(self, l):
        nc, S = self.nc, self.S
        T = self.T
        NB = T // 128
        NCB, NCP, CW, NBLK = self.NCB, self.NCP, self.CW, self.NBLK
        ps, psb = self.ps, self.psb
        pst = self.pst
        with ExitStack() as st:
            KE = self.sb(st, "KE", [128, T], BF16)
            keb_k, keb_e = Buf("ke_k"), Buf("ke_e")
            kwn = self.sb(st, "kwn", [64, T], BF16)
            kcs = self.sb(st, "kcs", [64, NCP], BF16)
            vcs = self.sb(st, "vcs", [CW, NBLK, 130], BF16)
            vsl = self.sb(st, "vsl", [128, NB, 66], BF16)
            vwn = self.sb(st, "vwn", [128, NB, 66], BF16)
            QM = self.sb(st, "QM", [128, 4, T], BF16)
            gin = Buf("gin")
            qm_m = [Buf("qm_m%d" % c) for c in range(self.NCH)]
            gts = [self.sb(st, "gts%d" % i, [128, 4, 24], F32) for i in range(2)]
            slb = [self.sb(st, "slb%d" % i, [128, 4, 64], F32) for i in range(2)]
            gtb = [Buf("gt%d" % i) for i in range(2)]
            P = [self.sb(st, "P%d" % i, [128, CH], BF16) for i in range(5)]
            Pb = [Buf("P%d" % i) for i in range(5)]
            yacc = self.sb(st, "yacc", [128, 4, 256], F32)
            yaccb = [Buf("yacc%d" % i) for i in range(4)]
            ybf = self.sb(st, "ybf", [128, 4, 256], BF16)
            ybfb = Buf("ybf")
            imp = self.sb(st, "imp", [128, 4, 64], F32)
            impb = [Buf("imp%d" % i) for i in range(4)]
            sm = self.sb(st, "sm", [128, 8], F32)
            smb = Buf("sm")
            sc = self.sb(st, "sc", [128, 64], F32)
            wk = self.sb(st, "wk", [128, 64], F32)
            wk2 = self.sb(st, "wk2", [128, 64], F32)
            m8 = self.sb(st, "m8", [128, 16], F32)
            tkb = Buf("topk")
            mk = self.sb(st, "mk", [128, 128], BF16)
            mkb = Buf("mk")
            yT = [self.sb(st, "yT%d" % i, [128, 2, CH], BF16) for i in range(2)]
            yTb = [Buf("yT%d" % i) for i in range(2)]
            pstb = [self.pstb] * 8
            S.dma("pool", [(KE[64:128, :], self.c_etab[:, :])], [], [keb_e], keb_e)
            S.op("pool", lambda: nc.gpsimd.memset(mk[:], 0.0), [], [mkb])
            it = 0
            pit = 0
            tit = 0

            def evac(acc_ap, z_ap, gate_ap, out_ap, first, accb, outb, gb):
                S.op("dve", lambda: nc.vector.tensor_scalar_max(out=sm[:, 0:1], in0=z_ap, scalar1=1e-30), [accb], [smb])
                S.op("dve", lambda: nc.vector.reciprocal(out=sm[:, 1:2], in_=sm[:, 0:1]), [smb], [smb])
                S.op("dve", lambda: nc.vector.tensor_tensor(out=sm[:, 2:3], in0=sm[:, 1:2], in1=gate_ap, op=ALU.mult), [smb, gb], [smb])
                if first:
                    S.op("dve", lambda: nc.vector.tensor_scalar(out=out_ap, in0=acc_ap, scalar1=sm[:, 2:3], scalar2=None, op0=ALU.mult), [accb, smb], [outb])
                else:
                    S.op("dve", lambda: nc.vector.scalar_tensor_tensor(out=out_ap, in0=acc_ap, scalar=sm[:, 2:3], in1=out_ap, op0=ALU.mult, op1=ALU.add), [accb, smb, outb], [outb])

            items = []

            def add_tile(s1, s2, pre=None, post=None, flush=False):
                items.append(dict(s1=s1, s2=s2, pre=pre or [], post=post or [], flush=flush))

            def load_g(g):
                S.dma("sp", [(KE[0:64, :], self.n_ksel[g]), (kwn[:], self.n_kwin[g]), (kcs[:], self.n_kc[g]), (vcs[:], self.n_vc[g]),
                             (vsl[:], self.n_vsel[g]), (vwn[:], self.n_vwin[g]),
                             (QM[0:64, :, :], self.n_q[4 * g:4 * g + 4].rearrange("h d t -> d h t"))], [], [gin, keb_k], gin)

            def load_gates(g, c):
                gs = (g * self.NCH + c) % 2
                S.dma("sp", [(gts[gs][:], self.n_gates[:, 4 * c:4 * c + 4, :]), (slb[gs][:], self.c_selbias[:, 4 * c:4 * c + 4, :])], [], [gtb[gs]], gtb[gs])

            def cmp_evac(g, c, r, banks):
                gs = (g * self.NCH + c) % 2
                hd = 4 * g + r
                for sub in range(4):
                    bk = banks[sub // 2]
                    c0 = (sub % 2) * 129
                    evac(ps[bk][:, c0 + 64:c0 + 128], ps[bk][:, c0 + 128:c0 + 129], gts[gs][:, sub, 3 * hd:3 * hd + 1], yacc[:, sub, r * 64:(r + 1) * 64],
                         True, psb[bk], yaccb[sub], gtb[gs])
                    if r == 0:
                        S.op("dve", lambda: nc.vector.tensor_scalar(out=imp[:, sub, :], in0=ps[bk][:, c0:c0 + 64], scalar1=sm[:, 1:2], scalar2=None, op0=ALU.mult),
                             [psb[bk], smb], [impb[sub]])
                    else:
                        S.op("dve", lambda: nc.vector.scalar_tensor_tensor(out=imp[:, sub, :], in0=ps[bk][:, c0:c0 + 64], scalar=sm[:, 1:2], in1=imp[:, sub, :],
                                                                             op0=ALU.mult, op1=ALU.add), [psb[bk], smb, impb[sub]], [impb[sub]])

            def topk(g, c):
                Q0 = c * CH
                gs = (g * self.NCH + c) % 2
                for sub in range(4):
                    S.op("dve", lambda: nc.vector.tensor_tensor(out=sc[:], in0=imp[:, sub, :], in1=slb[gs][:, sub, :], op=ALU.add), [impb[sub], gtb[gs]], [tkb])
                    S.op("dve", lambda: nc.vector.max(out=m8[:, 0:8], in_=sc[:]), [tkb], [tkb])
                    S.op("dve", lambda: nc.vector.match_replace(out=wk[:], in_to_replace=m8[:, 0:8], in_values=sc[:], imm_value=-3e38), [tkb], [tkb])
                    S.op("dve", lambda: nc.vector.max(out=m8[:, 8:16], in_=wk[:]), [tkb], [tkb])
                    S.op("dve", lambda: nc.vector.match_replace(out=wk2[:], in_to_replace=m8[:, 8:16], in_values=wk[:], imm_value=-3e38), [tkb], [tkb])
                    S.op("dve", lambda: nc.vector.tensor_tensor(out=wk[:], in0=sc[:], in1=wk2[:], op=ALU.subtract), [tkb], [tkb])
                    S.op("dve", lambda: nc.vector.tensor_scalar_min(out=mk[:, 64:128], in0=wk[:], scalar1=1.0), [tkb], [mkb])
                    S.op("pe", lambda: nc.tensor.transpose(out=pst[:, 0:128], in_=mk[:], identity=self.ident_bf[:]), [mkb, self.gconst], [self.pstb])
                    for r in range(4):
                        q0 = Q0 + sub * 128
                        if r % 2 == 0:
                            S.op("dve", lambda: nc.vector.tensor_scalar_add(out=QM[64:128, r, q0:q0 + 128], in0=pst[64:128, 0:128], scalar1=-1.0),
                                 [self.pstb], [qm_m[c]])
                        else:
                            S.op("act", lambda: nc.scalar.activation(out=QM[64:128, r, q0:q0 + 128], in_=pst[64:128, 0:128], func=AF.Identity, bias=-1.0),
                                 [self.pstb], [qm_m[c]])

            def branch_evac(g, c, r, br, bk):
                gs = (g * self.NCH + c) % 2
                hd = 4 * g + r
                gidx = 3 * hd + (2 if br == "win" else 1)
                for sub in range(4):
                    evac(ps[bk][:, sub * 65:sub * 65 + 64], ps[bk][:, sub * 65 + 64:sub * 65 + 65], gts[gs][:, sub, gidx:gidx + 1], yacc[:, sub, r * 64:(r + 1) * 64],
                         False, psb[bk], yaccb[sub], gtb[gs])

            def epilogue(g, c):
                cs = slice(c * CH, (c + 1) * CH)
                S.op("dve", lambda: nc.vector.tensor_copy(out=ybf[:], in_=yacc[:]), yaccb, [ybfb])
                ys = (g * self.NCH + c) % 2
                for sub in range(4):
                    for half in range(2):
                        S.op("pe", lambda: nc.tensor.transpose(out=pst[:, 128:256], in_=ybf[:, sub, half * 128:(half + 1) * 128], identity=self.ident_bf[:]),
                             [ybfb, self.gconst], [self.pstb])
                        S.op("act", lambda: nc.scalar.activation(out=yT[ys][:, half, sub * 128:(sub + 1) * 128], in_=pst[:, 128:256], func=AF.Copy),
                             [self.pstb], [yTb[ys]])
                r0 = 512 + 256 * g
                S.dma("sp", [(self.mixT[r0:r0 + 256, cs].rearrange("(h p) t -> p h t", p=128), yT[ys][:])], [yTb[ys]], [], yTb[ys])

            tcount = [0]

            def mk_tile(kind, g, c, r, j, bk, subs_fl, mask, pre=None, post=None, flush=False):
                t = tcount[0]
                tcount[0] += 1
                pz, pi = t % 2, t % 5
                Q0 = c * CH
                cs = slice(Q0, Q0 + CH)
                rows = CW if kind == "cmp" else 128

                def s1():
                    if kind == "cmp":
                        S.op("pe", lambda: nc.tensor.matmul(ps[pz][0:CW, :], kcs[0:64, j * 128:j * 128 + CW], QM[0:64, r, cs], start=True, stop=True), [gin], [psb[pz]])
                    elif kind == "win":
                        S.op("pe", lambda: nc.tensor.matmul(ps[pz][:], kwn[0:64, j * 128:(j + 1) * 128], QM[0:64, r, cs], start=True, stop=True), [gin], [psb[pz]])
                    else:
                        S.op("pe", lambda: nc.tensor.matmul(ps[pz][:], KE[:, j * 128:(j + 1) * 128], QM[:, r, cs], start=True, stop=True),
                             [gin, keb_k, keb_e, qm_m[c]], [psb[pz]])
                    S.op("act", lambda: nc.scalar.activation(out=P[pi][0:rows, :], in_=ps[pz][0:rows, :], func=AF.Exp), [psb[pz]], [Pb[pi]])
                    if mask is not None:
                        S.op("pool", lambda: nc.gpsimd.affine_select(out=P[pi][0:rows, :], in_=P[pi][0:rows, :], compare_op=ALU.is_ge, fill=0.0, **mask), [Pb[pi]], [Pb[pi]])

                def s2():
                    for (sub, st_, sp_) in subs_fl:
                        if kind == "cmp":
                            b_ = bk[sub // 2]
                            c0 = (sub % 2) * 129
                            S.op("pe", lambda: nc.tensor.matmul(ps[b_][:, c0:c0 + 129], P[pi][0:CW, sub * 128:(sub + 1) * 128], vcs[0:CW, j, 0:129],
                                                                start=st_, stop=sp_, skip_group_check=True), [Pb[pi], gin], [psb[b_]], acc=True)
                        else:
                            vt = vwn if kind == "win" else vsl
                            S.op("pe", lambda: nc.tensor.matmul(ps[bk][:, sub * 65:sub * 65 + 65], P[pi][:, sub * 128:(sub + 1) * 128], vt[:, j, 0:65],
                                                                start=st_, stop=sp_, skip_group_check=True), [Pb[pi], gin], [psb[bk]], acc=True)

                add_tile(s1, s2, pre, post, flush)

            from functools import partial
            for g in range(2):
                for c in range(self.NCH):
                    Q0 = c * CH
                    first_item_pre = [partial(load_gates, g, c)]
                    flush = False
                    if c == 0:
                        first_item_pre = [partial(load_g, g)] + first_item_pre
                        flush = True
                    vblk = [b for b in range(NBLK) if 16 * (128 * b) + 31 <= Q0 + CH - 1]
                    for r in range(4):
                        banks = (2, 3) if r % 2 == 0 else (4, 5)
                        for bi, b in enumerate(vblk):
                            mask = None
                            if not (16 * (128 * b + CW - 1) + 31 <= Q0):
                                mask = dict(pattern=[[1, CH]], base=Q0 - 2048 * b - 31, channel_multiplier=-16)
                            subs_fl = [(sub, (bi == 0 and sub % 2 == 0), (bi == len(vblk) - 1)) for sub in range(4)]
                            post = []
                            if bi == len(vblk) - 1:
                                post.append(partial(cmp_evac, g, c, r, banks))
                                if r == 3:
                                    post.append(partial(topk, g, c))
                            mk_tile("cmp", g, c, r, b, banks, subs_fl, mask, pre=first_item_pre, post=post, flush=flush)
                            first_item_pre = None
                            flush = False
                    for br in ("win", "sel"):
                        for r in range(4):
                            if br == "win":
                                blocks = list(range(max(0, 4 * c - 4), 4 * c + 4))
                                bk = 6 if r % 2 == 0 else 5
                            else:
                                blocks = list(range(0, 4 * c + 4))
                                bk = 2 + (r % 3)
                            for j in blocks:
                                rel = 128 * j - Q0
                                if rel >= 0:
                                    subs = list(range(rel // 128, 4))
                                    mask = dict(pattern=[[1, CH]], base=Q0 - 128 * j, channel_multiplier=-1)
                                elif br == "win":
                                    subs = list(range(0, (rel + 512) // 128 + 1))
                                    mask = dict(pattern=[[-1, CH]], base=128 * j - Q0 + 511, channel_multiplier=1)
                                else:
                                    subs = [0, 1, 2, 3]
                                    mask = None
                                subs_fl = [(sub, (j == blocks[0] and sub == subs[0]), (j == 4 * c + sub)) for sub in subs]
                                post = []
                                if j == blocks[-1]:
                                    post.append(partial(branch_evac, g, c, r, br, bk))
                                    if br == "sel" and r == 3:
                                        post.append(partial(epilogue, g, c))
                                mk_tile(br, g, c, r, j, bk, subs_fl, mask, post=post)
            SKEW = 3
            pending = []

            def retire():
                itm = pending.pop(0)
                itm["s2"]()
                for f in itm["post"]:
                    f()

            for itm in items:
                if itm["flush"]:
                    while pending:
                        retire()
                for f in itm["pre"]:
                    f()
                itm["s1"]()
                pending.append(itm)
                while len(pending) > SKEW:
                    retire()
            while pending:
                retire()
            S.barrier()


def host_constants(T):
    pos = np.arange(T)
    s = np.arange(64)
    cur = pos[:, None] // 64
    forced = (s[None, :] == 0) | (s[None, :] == cur) | (s[None, :] == cur - 1)
    valid = s[None, :] * 64 <= pos[:, None]
    selbias = np.where(valid, np.where(forced, 1e4, 0.0), NEG).astype(np.float32)
    selbias = np.ascontiguousarray(selbias.reshape(T // 128, 128, 64).transpose(1, 0, 2))
    etab = np.where((np.arange(T)[None, :] // 64) == s[:, None], BIG, 0.0).astype(np.float32)
    c0 = np.arange(256) * 16
    s0 = np.arange(64) * 64
    lo = np.maximum(c0[:, None], s0[None, :])
    hi = np.minimum(c0[:, None] + 32, s0[None, :] + 64)
    overlap = (np.maximum(hi - lo, 0).astype(np.float32) / 32)
    overlap[255:] = 0
    k = np.arange(128)
    tri = np.concatenate([np.where(k[:, None] >= k[None, :], -1.0, 0.0), -np.ones((128, 128))], axis=1).astype(np.float32)
    ident = np.eye(128, dtype=np.float32)
    return dict(c_selbias=selbias, c_etab=etab, c_overlap=overlap.astype(np.float32), c_tri=tri, c_ident=ident)


def pack_gains(norm_ffn1, norm_mix, norm_ffn2, norm_final, depth=4):
    g = np.ones((13, D), np.float32)
    for l in range(depth):
        g[l] = norm_ffn1[l]
        g[4 + l] = norm_mix[l]
        g[8 + l] = norm_ffn2[l]
    g[12] = norm_final
    return np.ascontiguousarray(g.reshape(13, KC, 128).transpose(2, 0, 1).reshape(128, 13 * KC))


def make_shared_inputs(inputs, T, depth=4):
    f = lambda a: np.ascontiguousarray(np.asarray(a, dtype=np.float32))
    ne = (depth + 1) // 2
    sh = {}
    sh["gains"] = pack_gains(f(inputs["norm_ffn1"]), f(inputs["norm_mix"]), f(inputs["norm_ffn2"]), f(inputs["norm_final"]), depth)
    for k in ("w_ffn1_in", "w_ffn2_in", "w_ffn1_out", "w_ffn2_out", "w_in_ab", "w_out_ab", "w_qkv_sb", "w_out_sb",
              "cmp_w1_k", "cmp_w1_v", "cmp_w2_k", "cmp_w2_v"):
        sh[k] = f(inputs[k])
    cw = f(inputs["conv_w"])
    sh["convw"] = np.ascontiguousarray(cw.reshape(ne, 3, 4, 128).transpose(3, 0, 2, 1).reshape(128, ne * 12))
    sh["cmp_peT_k"] = np.ascontiguousarray(f(inputs["cmp_pe_k"]).transpose(0, 2, 1))
    sh["cmp_peT_v"] = np.ascontiguousarray(f(inputs["cmp_pe_v"]).transpose(0, 2, 1))
    sh.update(host_constants(T))
    return sh


_PROG_CACHE = {}


def kernel(**inputs):
    x = np.asarray(inputs["x"], dtype=np.float32)
    B, T, _ = x.shape
    key = (T,)
    if key not in _PROG_CACHE:
        _PROG_CACHE[key] = Prog(T).build()
    nc = _PROG_CACHE[key]
    sh = make_shared_inputs(inputs, T)
    in_maps = []
    for b in range(B):
        m = dict(sh)
        m["xT"] = np.ascontiguousarray(x[b].T)
        in_maps.append(m)
    res = run_bass_kernel_spmd(nc, in_maps, core_ids=list(range(B)))
    out = np.stack([np.ascontiguousarray(r["outT"].T) for r in res.results], axis=0)
    return out.astype(np.float32)
```
